# Optimizing a Trainium2 kernel written in Bass

```python
import jax, jax.numpy as jnp
from jax import lax
import numpy as np

D_MODEL = 1024
BATCH = 4
SEQ = 4096
DEPTH = 1

HEAD_DIM = 64
A_Q_HEADS = 8
A_KV_HEADS = 2
B_GROUPS = ((128, 1), (512, 4), (2048, 16))
B_HEADS_PER_GROUP = 4
B_HEADS = B_HEADS_PER_GROUP * len(B_GROUPS)
A_Q_W = A_Q_HEADS * HEAD_DIM
A_KV_W = A_KV_HEADS * HEAD_DIM
B_W = B_HEADS * HEAD_DIM
B_OUT_W = B_HEADS_PER_GROUP * HEAD_DIM
QKV_COLS = A_Q_W + 2 * A_KV_W + 3 * B_W
QKV_SPLITS = [A_Q_W, A_Q_W + A_KV_W, A_Q_W + 2 * A_KV_W,
              A_Q_W + 2 * A_KV_W + B_W, A_Q_W + 2 * A_KV_W + 2 * B_W]
D_FF = -(-(8 * D_MODEL) // (3 * 256)) * 256
GRID_W = 64
Q_BLOCK = 128
AXIAL_THETA = 10000.0
PARTIAL_THETA = 500000.0
PARTIAL_ROT_DIM = HEAD_DIM // 4
EPS = 1e-6
NEG_INF = -1e30

kernel_name = "hybrid_gqa_axial_dilated_swa_adaln_block"


def _rms(x, g):
    xf = x.astype(jnp.float32)
    y = xf * lax.rsqrt(jnp.mean(xf * xf, axis=-1, keepdims=True) + EPS)
    return (y * g.astype(jnp.float32)).astype(x.dtype)


def _rope_angles(pos, dim, theta):
    inv = theta ** (-jnp.arange(0, dim, 2, dtype=jnp.float32) / dim)
    ang = pos.astype(jnp.float32)[:, None] * inv[None, :]
    return jnp.cos(ang), jnp.sin(ang)


def _rotate(x, cos, sin):
    half = x.shape[-1] // 2
    x1 = x[..., :half].astype(jnp.float32)
    x2 = x[..., half:].astype(jnp.float32)
    c = cos[None, :, None, :]
    s = sin[None, :, None, :]
    return jnp.concatenate([x1 * c - x2 * s, x2 * c + x1 * s], axis=-1).astype(x.dtype)


def _axial_rope(x, row, col):
    half = HEAD_DIM // 2
    cr, sr = _rope_angles(row, half, AXIAL_THETA)
    cc, sc = _rope_angles(col, half, AXIAL_THETA)
    return jnp.concatenate([_rotate(x[..., :half], cr, sr),
                            _rotate(x[..., half:], cc, sc)], axis=-1)


def _partial_rope(x, pos):
    cr, sr = _rope_angles(pos, PARTIAL_ROT_DIM, PARTIAL_THETA)
    return jnp.concatenate([_rotate(x[..., :PARTIAL_ROT_DIM], cr, sr),
                            x[..., PARTIAL_ROT_DIM:]], axis=-1)


def _global_gqa(q, k, v):
    b, s, _, d = q.shape
    g = A_Q_HEADS // A_KV_HEADS
    nq = s // Q_BLOCK
    qb = q.reshape(b, nq, Q_BLOCK, A_KV_HEADS, g, d).transpose(1, 0, 2, 3, 4, 5)
    scale = d ** -0.5

    def attend(qblk):
        sc = jnp.einsum('bqhgd,bkhd->bhgqk', qblk, k).astype(jnp.float32) * scale
        p = jax.nn.softmax(sc, axis=-1)
        return jnp.einsum('bhgqk,bkhd->bqhgd', p.astype(v.dtype), v)

    o = lax.map(attend, qb)
    return o.transpose(1, 0, 2, 3, 4, 5).reshape(b, s, A_Q_HEADS * d)


def _banded(q, k, v, radius):
    L, d = q.shape[-2], q.shape[-1]
    blk = radius
    nb = -(-L // blk)
    lp = nb * blk
    lead = q.shape[:-2]
    pad = [(0, 0)] * len(lead)
    qp = jnp.pad(q, pad + [(0, lp - L), (0, 0)]).reshape(*lead, nb, blk, d)

    def windows(t):
        tp = jnp.pad(t, pad + [(blk, lp - L + blk), (0, 0)]).reshape(*lead, nb + 2, blk, d)
        return jnp.concatenate([tp[..., :-2, :, :], tp[..., 1:-1, :, :], tp[..., 2:, :, :]], axis=-2)

    kw, vw = windows(k), windows(v)
    s = jnp.einsum('...nqd,...nkd->...nqk', qp, kw).astype(jnp.float32) * (d ** -0.5)
    qpos = jnp.arange(nb)[:, None] * blk + jnp.arange(blk)[None, :]
    kpos = (jnp.arange(nb)[:, None] - 1) * blk + jnp.arange(3 * blk)[None, :]
    kp = kpos[:, None, :]
    valid = (jnp.abs(qpos[:, :, None] - kp) <= radius) & (kp >= 0) & (kp < L)
    s = jnp.where(valid, s, NEG_INF)
    lse = jax.nn.logsumexp(s, axis=-1, keepdims=True)
    p = jnp.exp(s - lse)
    o = jnp.einsum('...nqk,...nkd->...nqd', p.astype(v.dtype), vw)
    o = o.reshape(*lead, lp, d)[..., :L, :]
    return o, lse.reshape(*lead, lp)[..., :L]


def _dilated_swa(q, k, v):
    b, s, _, d = q.shape
    h = B_HEADS_PER_GROUP
    outs, lses = [], []
    for gi, (window, dil) in enumerate(B_GROUPS):
        lo, hi = gi * h, (gi + 1) * h
        L = s // dil
        qs = q[:, :, lo:hi].reshape(b, L, dil, h, d).transpose(0, 2, 3, 1, 4)
        ks = k[:, :, lo:hi].reshape(b, L, dil, h, d).transpose(0, 2, 3, 1, 4)
        vs = v[:, :, lo:hi].reshape(b, L, dil, h, d).transpose(0, 2, 3, 1, 4)
        o, lse = _banded(qs, ks, vs, window // 2 // dil)
        outs.append(o.transpose(0, 3, 1, 2, 4).reshape(b, s, h, d))
        lses.append(lse.transpose(0, 3, 1, 2).reshape(b, s, h))
    o_all = jnp.stack(outs, axis=0)
    lse_all = jnp.stack(lses, axis=0)
    w = jax.nn.softmax(lse_all, axis=0)
    out = jnp.sum(w[..., None].astype(o_all.dtype) * o_all, axis=0)
    return out.reshape(b, s, h * d)


def setup_inputs(seed: int = 0) -> dict:
    key = jax.random.key(seed)
    ks = jax.random.split(key, 17)

    def nrm(k, shape, fan_in, s=1.0):
        return jax.random.normal(k, shape, jnp.float32) * (s * fan_in ** -0.5)

    def gain(k, shape):
        return 1.0 + 0.1 * jax.random.normal(k, shape, jnp.float32)

    return {
        "x": jax.random.normal(ks[0], (BATCH, SEQ, D_MODEL), jnp.float32),
        "c": jax.random.normal(ks[1], (BATCH, D_MODEL), jnp.float32),
        "w_ada": nrm(ks[2], (DEPTH, D_MODEL, 6 * D_MODEL), D_MODEL, 0.5),
        "b_ada": 0.02 * jax.random.normal(ks[3], (DEPTH, 6 * D_MODEL), jnp.float32),
        "norm1_g": gain(ks[4], (DEPTH, D_MODEL)),
        "w_qkv": nrm(ks[5], (DEPTH, D_MODEL, QKV_COLS), D_MODEL),
        "q_norm_a": gain(ks[6], (DEPTH, HEAD_DIM)),
        "k_norm_a": gain(ks[7], (DEPTH, HEAD_DIM)),
        "w_proj_a": nrm(ks[8], (DEPTH, A_Q_W, D_MODEL), A_Q_W),
        "w_proj_b": nrm(ks[9], (DEPTH, B_OUT_W, D_MODEL), B_OUT_W),
        "w_gate": nrm(ks[10], (DEPTH, D_MODEL, 2 * D_MODEL), D_MODEL),
        "b_gate": 0.1 * jax.random.normal(ks[11], (DEPTH, 2 * D_MODEL), jnp.float32),
        "w_o": nrm(ks[12], (DEPTH, D_MODEL, D_MODEL), D_MODEL),
        "norm2_g": gain(ks[13], (DEPTH, D_MODEL)),
        "w_ffn_in": nrm(ks[14], (DEPTH, D_MODEL, 2 * D_FF), D_MODEL),
        "w_ffn_out": nrm(ks[15], (DEPTH, D_FF, D_MODEL), D_FF),
        "final_norm_g": gain(ks[16], (D_MODEL,)),
    }


def reference(x, c, w_ada, b_ada, norm1_g, w_qkv, q_norm_a, k_norm_a, w_proj_a, w_proj_b,
              w_gate, b_gate, w_o, norm2_g, w_ffn_in, w_ffn_out, final_norm_g):
    b, s, _ = x.shape
    rows = s // GRID_W
    row = jnp.repeat(jnp.arange(rows, dtype=jnp.int32), GRID_W)
    col = jnp.tile(jnp.arange(GRID_W, dtype=jnp.int32), rows)
    pos = jnp.arange(s, dtype=jnp.int32)
    cond = jax.nn.silu(c)
    for l in range(DEPTH):
        mod = cond @ w_ada[l] + b_ada[l]
        sh1, sc1, g1, sh2, sc2, g2 = jnp.split(mod, 6, axis=-1)
        u = _rms(x, norm1_g[l]) * (1.0 + sc1[:, None, :]) + sh1[:, None, :]
        qkv = u @ w_qkv[l]
        qa, ka, va, qb, kb, vb = jnp.split(qkv, QKV_SPLITS, axis=-1)
        qa = qa.reshape(b, s, A_Q_HEADS, HEAD_DIM)
        ka = ka.reshape(b, s, A_KV_HEADS, HEAD_DIM)
        va = va.reshape(b, s, A_KV_HEADS, HEAD_DIM)
        qa = _axial_rope(_rms(qa, q_norm_a[l]), row, col)
        ka = _axial_rope(_rms(ka, k_norm_a[l]), row, col)
        ya = _global_gqa(qa, ka, va) @ w_proj_a[l]
        qb = _partial_rope(qb.reshape(b, s, B_HEADS, HEAD_DIM), pos)
        kb = _partial_rope(kb.reshape(b, s, B_HEADS, HEAD_DIM), pos)
        vb = vb.reshape(b, s, B_HEADS, HEAD_DIM)
        yb = _dilated_swa(qb, kb, vb) @ w_proj_b[l]
        gates = jax.nn.sigmoid(u @ w_gate[l] + b_gate[l])
        ga, gb = jnp.split(gates, 2, axis=-1)
        mix = (ga * ya + gb * yb) @ w_o[l]
        x = x + g1[:, None, :] * mix
        u2 = _rms(x, norm2_g[l]) * (1.0 + sc2[:, None, :]) + sh2[:, None, :]
        hg, hu = jnp.split(u2 @ w_ffn_in[l], 2, axis=-1)
        x = x + g2[:, None, :] * ((jax.nn.silu(hg) * hu) @ w_ffn_out[l])
    return _rms(x, final_norm_g)
```

```python
import numpy as np
from contextlib import ExitStack
import concourse.bass as bass
import concourse.mybir as mybir
from concourse.bass_utils import run_bass_kernel_spmd

F32 = mybir.dt.float32
BF16 = mybir.dt.bfloat16
AF = mybir.ActivationFunctionType
ALU = mybir.AluOpType

SEM_LIMIT = 30000
S = 4096
OWN = 2048
D = 1024
DFF = 2816
EPS = 1e-6
FFN_GROUPS = [(0, 6), (6, 6), (12, 5), (17, 5)]


class T:
    __slots__ = ("name", "lw", "rd", "dsem", "dcnt", "shared")

    def __init__(self, name, shared=False):
        self.name = name
        self.lw = None
        self.rd = []
        self.dsem = None
        self.dcnt = 0
        self.shared = shared


class FW:
    ENG = ("pe", "act", "dve", "pool", "sp")

    def __init__(self, nc, stack):
        self.nc = nc
        self.stack = stack
        self.ops = {e: [] for e in self.ENG}
        self.sem = {}
        self.cnt = {}
        self.known = {e: {} for e in self.ENG}
        self.snap = {}
        self.nsem = 0
        self.pending_out = []
        for e in self.ENG:
            self._newsem(e)
        self.same_engine_sync = {"pe": False, "act": True, "dve": True, "pool": True, "sp": False}

    def _alloc_sem(self, name):
        self.nsem += 1
        return self.stack.enter_context(self.nc.semaphore("%s_%d" % (name, self.nsem)))

    def _newsem(self, e):
        self.sem[e] = self._alloc_sem("s_" + e)
        self.cnt[e] = 0

    def _learn(self, e, s, v):
        kn = self.known[e]
        kn[s] = v
        sn = self.snap.get((id(s), v))
        if sn:
            for s2, v2 in sn.items():
                if kn.get(s2, 0) < v2:
                    kn[s2] = v2

    def _waits(self, e, reads, writes):
        need = {}

        def add(ev):
            if ev is None:
                return
            s, v = ev
            if need.get(s, 0) < v:
                need[s] = v
        for t in reads:
            add(t.lw)
            if not t.shared:
                for ev in t.rd:
                    if ev[0] is not self.sem[e]:
                        add(ev)
        for t in writes:
            add(t.lw)
            for ev in t.rd:
                add(ev)
        out = []
        kn = self.known[e]
        for s, v in need.items():
            if s is self.sem[e] and not self.same_engine_sync[e]:
                continue
            if kn.get(s, 0) >= v:
                continue
            out.append((s, v))
            self._learn(e, s, v)
        return out

    def op(self, e, fn, reads=(), writes=()):
        waits = self._waits(e, reads, writes)
        if self.cnt[e] >= SEM_LIMIT:
            self._newsem(e)
        self.cnt[e] += 1
        s = self.sem[e]
        ev = (s, self.cnt[e])
        sn = dict(self.known[e])
        sn[s] = self.cnt[e]
        self.snap[(id(s), self.cnt[e])] = sn
        for t in reads:
            t.rd.append(ev)
        for t in writes:
            t.lw = ev
            t.rd = []
        self.ops[e].append((waits, fn, (s, 1)))
        return ev

    def dma(self, q, fn, dst, src=None):
        reads = [src] if src is not None else []
        waits = self._waits(q, reads, [dst])
        if dst.dsem is None:
            dst.dsem = self._alloc_sem("d")
        dst.dcnt += 16
        ev = (dst.dsem, dst.dcnt)
        self.snap[(id(dst.dsem), dst.dcnt)] = dict(self.known[q])
        for t in reads:
            t.rd.append(ev)
            self.pending_out.append(ev)
        dst.lw = ev
        dst.rd = []
        self.ops[q].append((waits, fn, (dst.dsem, 16)))
        return ev

    def wait_all(self, e, tiles):
        waits = self._waits(e, tiles, [])
        self.ops[e].append((waits, None, None))

    def barrier(self):
        evs = [(f, self.sem[f], self.cnt[f]) for f in self.ENG if self.cnt[f] > 0]
        for e in self.ENG:
            waits = []
            for f, s, v in evs:
                if f == e and not self.same_engine_sync[e]:
                    continue
                if self.known[e].get(s, 0) < v:
                    waits.append((s, v))
                    self._learn(e, s, v)
            for s, v in self.pending_out:
                if self.known[e].get(s, 0) < v:
                    waits.append((s, v))
                    self._learn(e, s, v)
            self.ops[e].append((waits, None, None))
        self.pending_out = []

    def emit(self):
        nc = self.nc
        with nc.Block() as block:
            def mk(e):
                def body(eng):
                    for waits, fn, inc in self.ops[e]:
                        for s, v in waits:
                            eng.wait_ge(s, v)
                        if fn is None:
                            continue
                        ins = fn(eng)
                        if inc is not None:
                            ins.then_inc(inc[0], inc[1])
                return body
            block.tensor(mk("pe"))
            block.scalar(mk("act"))
            block.vector(mk("dve"))
            block.gpsimd(mk("pool"))
            block.sync(mk("sp"))


NV = 48 + 8 + 8 + 8 + 16 + 2
V_BADA, V_N1G, V_N2G, V_FNG, V_BG, V_GQ, V_GK = 0, 48, 56, 64, 72, 88, 89
NCM = 4 * 128 + 384
C_ONES, C_BO, C_RA, C_RB, C_MASK = 0, 128, 256, 384, 512


def build(debug=None):
    nc = bass.Bass("TRN2", target_bir_lowering=False)

    def din(name, shape, dt=F32):
        return nc.dram_tensor(name, list(shape), dt, kind="ExternalInput").ap()

    xT = din("xT", [D, S])
    cvec = din("cvec", [128, 8])
    vecs = din("vecs", [128, NV])
    cmat = din("cmat", [128, NCM])
    tabA = din("tabA", [2, 128, S])
    tabB = din("tabB", [2, 128, 3072])
    w_ada = din("w_ada", [D, 6 * D])
    w_qkv = din("w_qkv", [D, 3072])
    w_pa = din("w_pa", [512, D])
    w_pb = din("w_pb", [256, D])
    w_gate = din("w_gate", [D, 2 * D])
    w_o = din("w_o", [D, D])
    w_fi = din("w_fi", [D, 2 * DFF])
    w_fo = din("w_fo", [DFF, D])
    outT = nc.dram_tensor("outT", [D, OWN], F32, kind="ExternalOutput").ap()
    dbg = {}

    def dbg_out(name, shape):
        dbg[name] = nc.dram_tensor(name, list(shape), F32, kind="ExternalOutput").ap()
        return dbg[name]

    xTv = xT.rearrange("(k p) t -> p k t", p=128)
    outTv = outT.rearrange("(k p) t -> p k t", p=128)
    w_qkv_v = w_qkv.rearrange("(k p) n -> p k n", p=128)

    with ExitStack() as top:
        fw = FW(nc, top)

        def sb(ctx, name, shape, dt):
            return ctx.enter_context(nc.sbuf_tensor(name, list(shape), dt))

        def ps(ctx, name, shape, dt=F32):
            return ctx.enter_context(nc.psum_tensor(name, list(shape), dt))

        def mm(out_ap, pairs, reads, twrite):
            def fn(e):
                n = len(pairs)
                ins = None
                for i, (l, r) in enumerate(pairs):
                    ins = e.matmul(out_ap, lhsT=l, rhs=r, start=(i == 0), stop=(i == n - 1))
                return ins
            return fw.op("pe", fn, reads, [twrite])

        def mm_multi(groups, reads, writes):
            def fn(e):
                ins = None
                for out_ap, pairs in groups:
                    n = len(pairs)
                    for i, (l, r) in enumerate(pairs):
                        ins = e.matmul(out_ap, lhsT=l, rhs=r, start=(i == 0), stop=(i == n - 1))
                return ins
            return fw.op("pe", fn, reads, writes)

        def act(out, in_, func, reads, writes, **kw):
            return fw.op("act", lambda e: e.activation(out=out, in_=in_, func=func, **kw), reads, writes)

        def dve_tt(out, in0, in1, op, reads, writes, eng="dve"):
            return fw.op(eng, lambda e: e.tensor_tensor(out=out, in0=in0, in1=in1, op=op), reads, writes)

        def dve_stt(out, in0, scalar, in1, op0, op1, reads, writes):
            return fw.op("dve", lambda e: e.scalar_tensor_tensor(out=out, in0=in0, scalar=scalar, in1=in1,
                                                                op0=op0, op1=op1), reads, writes)

        def dve_recip(out, in_, reads, writes):
            return fw.op("dve", lambda e: e.reciprocal(out=out, in_=in_), reads, writes)

        def sl(start, count, step):
            return slice(start, start + step * (count - 1) + 1, step)

        def finish(tds, q="pool"):
            fw.wait_all(q, tds)
            fw.emit()
            return nc, dbg

        vec_sb = sb(top, "vec_sb", [128, NV], F32)
        cm_f = sb(top, "cm_f", [128, NCM], F32)
        cm = sb(top, "cm", [128, NCM], BF16)
        mod = sb(top, "mod", [128, 48], F32)
        A12 = sb(top, "A12", [128, 16], F32)
        uT_own = sb(top, "uT_own", [128, 8, OWN], BF16)
        t_vec, t_cmf, t_cm, t_mod, t_A = T("vec", shared=True), T("cmf"), T("cm", shared=True), T("mod", shared=True), T("A12", shared=True)
        t_modhi, t_A2 = T("mod_hi", shared=True), T("A12_hi", shared=True)
        cb = sb(top, "cb", [128, 8], BF16)
        t_cb = T("cb", shared=True)
        t_uown = [T("uown%d" % i) for i in range(4)]

        ONES = cm[:, C_ONES:C_ONES + 128]
        BO = cm[:, C_BO:C_BO + 128]
        RA = cm[:, C_RA:C_RA + 128]
        RB = cm[:, C_RB:C_RB + 128]
        MASK = cm[:, C_MASK:C_MASK + 384]

        fw.dma("sp", lambda e: e.dma_start(out=vec_sb[:], in_=vecs), t_vec)
        fw.dma("sp", lambda e: e.dma_start(out=cm_f[:], in_=cmat), t_cmf)
        act(cm[:], cm_f[:], AF.Copy, [t_cmf], [t_cm])

        def vcol(c, n=1):
            return vec_sb[:, c:c + n]

        SH1 = lambda k: mod[:, k:k + 1]
        SH2 = lambda k: mod[:, 24 + k:25 + k]
        G1 = lambda k: mod[:, 16 + k:17 + k]
        G2 = lambda k: mod[:, 40 + k:41 + k]

        def norm_bufs(ctx, tag):
            sq = [sb(ctx, "nrm_sq%s%d" % (tag, i), [128, 512], BF16) for i in range(2)]
            ssp = ps(ctx, "nrm_ssp" + tag, [128, 512])
            rs = sb(ctx, "nrm_rs" + tag, [128, 512], F32)
            rstd = sb(ctx, "nrm_rstd" + tag, [128, 512], F32)
            tmp = [sb(ctx, "ntmp%s%d" % (tag, i), [128, 512], F32) for i in range(2)]
            return dict(sq=sq, t_sq=[T("sq0" + tag), T("sq1" + tag)], ssp=ssp, t_ssp=T("ssp" + tag), rs=rs, t_rs=T("rs" + tag),
                        rstd=rstd, t_rstd=T("rstd" + tag), tmp=tmp, t_tmp=[T("tmp0" + tag), T("tmp1" + tag)])

        def norm_sq(nb, src, t_src, ks):
            for k in ks:
                i = k % 2
                act(nb["sq"][i][:], src[:, k, :], AF.Square, t_src, [nb["t_sq"][i]])
                fw.op("pe", lambda e, k=k, i=i: e.matmul(nb["ssp"][:], lhsT=ONES, rhs=nb["sq"][i][:], start=(k == 0), stop=(k == 7)),
                      [nb["t_sq"][i], t_cm], [nb["t_ssp"]])

        def norm_rstd(nb):
            act(nb["rs"][:], nb["ssp"][:], AF.Sqrt, [nb["t_ssp"]], [nb["t_rs"]], scale=1.0 / D, bias=EPS)
            dve_recip(nb["rstd"][:], nb["rs"][:], [nb["t_rs"]], [nb["t_rstd"]])

        def norm_apply(nb, src, t_src, a_col, sh_col, dst, t_dst, cdeps):
            for k in range(8):
                i = k % 2
                rd = t_src + [nb["t_rstd"], t_vec] + cdeps
                if sh_col is None:
                    dve_stt(dst[:, k, :], src[:, k, :], a_col(k), nb["rstd"][:], ALU.mult, ALU.mult, rd, [t_dst(k)])
                else:
                    dve_stt(nb["tmp"][i][:], src[:, k, :], a_col(k), nb["rstd"][:], ALU.mult, ALU.mult, rd, [nb["t_tmp"][i]])
                    act(dst[:, k, :], nb["tmp"][i][:], AF.Identity, [nb["t_tmp"][i]] + cdeps, [t_dst(k)], bias=sh_col(k), scale=1.0)

        def norm_block(nb, src, t_src, a_col, sh_col, dst, t_dst, cdeps):
            norm_sq(nb, src, t_src, range(8))
            norm_rstd(nb)
            norm_apply(nb, src, t_src, a_col, sh_col, dst, t_dst, cdeps)

        with ExitStack() as ctxA:
            yaT = sb(ctxA, "yaT", [128, 4, OWN], BF16)
            ybT = sb(ctxA, "ybT", [128, 2, OWN], BF16)
            t_ya = [[T("ya%d_%d" % (h, q)) for q in range(4)] for h in range(8)]
            t_yb = [[T("yb%d_%d" % (h, q)) for q in range(4)] for h in range(4)]

            with ExitStack() as attn:
                uT_elo = sb(attn, "uT_elo", [128, 8, 1024], BF16)
                t_uelo = [T("uelo%d" % i) for i in range(2)]

                with ExitStack() as pha:
                    uT_ehi = sb(pha, "uT_ehi", [128, 8, 1024], BF16)
                    t_uehi = [T("uehi%d" % i) for i in range(2)]
                    wA = sb(pha, "wA", [128, 8, 768], BF16)
                    t_wA = T("wA")

                    def ublk(blk):
                        if blk < 4:
                            return uT_own[:, :, blk * 512:(blk + 1) * 512], t_uown[blk]
                        if blk < 6:
                            return uT_elo[:, :, (blk - 4) * 512:(blk - 3) * 512], t_uelo[blk - 4]
                        return uT_ehi[:, :, (blk - 6) * 512:(blk - 5) * 512], t_uehi[blk - 6]

                    with ExitStack() as p0:
                        cv = sb(p0, "cv", [128, 8], F32)
                        wad = sb(p0, "wad", [128, 8, 2 * D], BF16)
                        ps_mod = ps(p0, "ps_mod", [128, 512])[:, 0:16]
                        t_cv, t_psm = T("cv"), T("psm")
                        t_wad = [T("wad%d" % i) for i in range(2)]
                        fw.dma("sp", lambda e: e.dma_start(out=cv[:], in_=cvec), t_cv)
                        wadv = w_ada.rearrange("(k p) n -> p k n", p=128)
                        for g in (1, 0):
                            fw.dma("pool", lambda e, g=g: e.dma_start(out=wad[:, :, g * D:(g + 1) * D],
                                                                     in_=wadv[:, :, g * D:(g + 1) * D]), t_wad[g])
                        t_wAq = [T("wAq%d" % i) for i in range(8)]
                        for hd in range(8):
                            slot = (hd % 4) * 2 + hd // 4
                            fw.dma("pool", lambda e, hd=hd, slot=slot: e.dma_start(
                                out=wA[:, :, slot * 64:(slot + 1) * 64], in_=w_qkv_v[:, :, hd * 64:(hd + 1) * 64]), t_wAq[hd])
                        fw.dma("pool", lambda e: e.dma_start(out=wA[:, :, 512:768], in_=w_qkv_v[:, :, 512:768]), t_wA)
                        act(cb[:], cv[:], AF.Silu, [t_cv], [t_cb])
                        for g in (1, 0):
                            groups = []
                            for j in range(g * 8, g * 8 + 8):
                                groups.append((ps_mod[:, j:j + 1],
                                               [(wad[:, k, j * 128:(j + 1) * 128], cb[:, k:k + 1]) for k in range(8)]))
                            mm_multi(groups, [t_cb, t_wad[g]], [t_psm])
                        dve_tt(mod[:, 0:16], ps_mod[:], vcol(V_BADA, 16), ALU.add, [t_psm, t_vec], [t_mod])
                        dve_stt(A12[:, 0:8], mod[:, 8:16], 1.0, vcol(V_N1G, 8), ALU.add, ALU.mult, [t_mod, t_vec], [t_A])
                        fw.barrier()
                    if debug == "mod":
                        o = dbg_out("dbg_mod", [128, 64])
                        td, td2 = T("dbgmod"), T("dbgA")
                        fw.dma("sp", lambda e: e.dma_start(out=o[:, 0:16], in_=mod[:, 0:16]), td, t_mod)
                        fw.dma("sp", lambda e: e.dma_start(out=o[:, 48:56], in_=A12[:, 0:8]), td2, t_A)
                        return finish([td, td2], "sp")

                    with ExitStack() as p1:
                        xb = [sb(p1, "xb%d" % i, [128, 8, 512], F32) for i in range(2)]
                        t_xb = [T("xb%d" % i) for i in range(2)]
                        nb = [norm_bufs(p1, "a"), norm_bufs(p1, "b")]
                        NHOIST = 6

                        def n1_load(blk):
                            i = blk % 2
                            fw.dma("sp", lambda e: e.dma_start(out=xb[i][:], in_=xTv[:, :, blk * 512:(blk + 1) * 512]), t_xb[i])
                        n1_load(0)
                        norm_sq(nb[0], xb[0][:], [t_xb[0]], range(8))
                        for blk in range(8):
                            i = blk % 2
                            j = (blk + 1) % 2
                            if blk + 1 < 8:
                                n1_load(blk + 1)
                            norm_rstd(nb[i])
                            if blk + 1 < 8:
                                norm_sq(nb[j], xb[j][:], [t_xb[j]], range(0, NHOIST))
                            dst, t_dst = ublk(blk)
                            norm_apply(nb[i], xb[i][:], [t_xb[i]], lambda k: A12[:, k:k + 1], SH1, dst, lambda k, t=t_dst: t, [t_A, t_mod])
                            if blk + 1 < 8:
                                norm_sq(nb[j], xb[j][:], [t_xb[j]], range(NHOIST, 8))
                        fw.barrier()
                    if debug == "u":
                        o = dbg_out("dbg_u", [128, 8, S])
                        tds = [T("dbgu%d" % i) for i in range(8)]
                        for blk in range(8):
                            src, t_s = ublk(blk)
                            fw.dma("pool", lambda e, blk=blk, src=src: e.dma_start(out=o[:, :, blk * 512:(blk + 1) * 512], in_=src), tds[blk], t_s)
                        return finish(tds)

                    with ExitStack() as p23:
                        QaT = sb(p23, "QaT", [128, 4, OWN], BF16)
                        KaT = sb(p23, "KaT", [128, S], BF16)
                        Va = sb(p23, "Va", [128, 32, 3, 64], BF16)
                        t_Qa = [[T("Qa%d_%d" % (i, q)) for q in range(4)] for i in range(4)]
                        t_Ka = [T("Ka%d" % q) for q in range(8)]
                        t_Va = [T("Va%d" % q) for q in range(8)]
                        t_Vones = T("Vones")
                        fw.op("pool", lambda e: e.memset(Va[:, :, 1, :], 1.0), [], [t_Vones])
                        with ExitStack() as p2:
                            tab = [sb(p2, "tabA%d" % i, [128, 2, 512], F32) for i in range(2)]
                            t_tab = [T("tabA%d" % i) for i in range(2)]

                            def two(name, shape, dt, psum=False):
                                if psum:
                                    return [ps(p2, "%s%d" % (name, i), shape) for i in range(2)], [T("%s%d" % (name, i)) for i in range(2)]
                                return [sb(p2, "%s%d" % (name, i), shape, dt) for i in range(2)], [T("%s%d" % (name, i)) for i in range(2)]
                            qps, t_qps = two("qps", [128, 512], F32, True)
                            ssq, t_ssq = two("ssq", [128, 512], F32, True)
                            rqp, t_rqp = two("rqp", [128, 512], F32, True)
                            vps, t_vps = two("vps", [128, 4, 128], F32, True)
                            sqb, t_sqb = two("sqb", [128, 512], BF16)
                            qg, t_qg = two("qg", [128, 512], BF16)
                            rsq, t_rsq = two("rsq", [128, 512], F32)
                            rq, t_rq = two("rq", [128, 512], F32)
                            t1, t_t1 = two("t1_", [128, 512], F32)
                            t2, t_t2 = two("t2_", [128, 512], F32)
                            cnt = 0
                            for blk in range(8):
                                ti = blk % 2
                                fw.dma("sp", lambda e, blk=blk, ti=ti: e.dma_start(
                                    out=tab[ti][:], in_=tabA[:, :, blk * 512:(blk + 1) * 512].rearrange("c p t -> p c t")), t_tab[ti])
                                u_ap, t_u = ublk(blk)
                                tiles = ([("q", i) for i in range(4)] if blk < 4 else []) + [("k", 0)]
                                for kind, i in tiles:
                                    b = cnt % 2
                                    cnt += 1
                                    if kind == "q":
                                        def lw(k, i=i):
                                            return wA[:, k, i * 128:(i + 1) * 128]
                                        gain = vcol(V_GQ)
                                        dst = QaT[:, i, blk * 512:(blk + 1) * 512]
                                        t_dst = t_Qa[i][blk]
                                    else:
                                        def lw(k):
                                            return wA[:, k, 512:640]
                                        gain = vcol(V_GK)
                                        dst = KaT[:, blk * 512:(blk + 1) * 512]
                                        t_dst = t_Ka[blk]
                                    mm(qps[b][:], [(lw(k), u_ap[:, k, :]) for k in range(8)], [t_wA, t_u] + t_wAq, t_qps[b])
                                    act(sqb[b][:], qps[b][:], AF.Square, [t_qps[b]], [t_sqb[b]])
                                    act(qg[b][:], qps[b][:], AF.Copy, [t_qps[b], t_vec], [t_qg[b]], scale=gain)
                                    mm(ssq[b][:], [(BO, sqb[b][:])], [t_sqb[b], t_cm], t_ssq[b])
                                    mm(rqp[b][:], [(RA, qg[b][:])], [t_qg[b], t_cm], t_rqp[b])
                                    act(rsq[b][:], ssq[b][:], AF.Sqrt, [t_ssq[b]], [t_rsq[b]], scale=1.0 / 64, bias=EPS)
                                    dve_recip(rq[b][:], rsq[b][:], [t_rsq[b]], [t_rq[b]])
                                    dve_stt(t1[b][:], qps[b][:], gain, tab[ti][:, 0, :], ALU.mult, ALU.mult,
                                            [t_qps[b], t_tab[ti], t_vec], [t_t1[b]])
                                    dve_tt(t2[b][:], rqp[b][:], tab[ti][:, 1, :], ALU.mult, [t_rqp[b], t_tab[ti]], [t_t2[b]])
                                    dve_tt(t1[b][:], t1[b][:], t2[b][:], ALU.add, [t_t1[b], t_t2[b]], [t_t1[b]], eng="pool")
                                    dve_tt(dst, t1[b][:], rq[b][:], ALU.mult, [t_t1[b], t_rq[b]], [t_dst], eng="pool")
                                vb = blk % 2
                                groups = []
                                for j in range(4):
                                    groups.append((vps[vb][:, j, :],
                                                   [(u_ap[:, k, j * 128:(j + 1) * 128], wA[:, k, 640:768]) for k in range(8)]))
                                mm_multi(groups, [t_wA, t_u], [t_vps[vb]])
                                for v in range(2):
                                    act(Va[:, blk * 4:blk * 4 + 4, 2 * v, :], vps[vb][:, :, v * 64:(v + 1) * 64], AF.Copy,
                                        [t_vps[vb], t_Vones], [t_Va[blk]])
                            fw.barrier()
                        if debug == "qkv_a":
                            o1 = dbg_out("dbg_Qa", [128, 4, OWN])
                            o2 = dbg_out("dbg_Ka", [128, S])
                            o3 = dbg_out("dbg_Va", [128, 32 * 3 * 64])
                            td = [T("d1"), T("d2"), T("d3")]
                            fw.dma("pool", lambda e: e.dma_start(out=o1, in_=QaT[:]), td[0], t_Qa[0][0])
                            fw.dma("pool", lambda e: e.dma_start(out=o2, in_=KaT[:]), td[1], t_Ka[0])
                            fw.dma("pool", lambda e: e.dma_start(out=o3, in_=Va[:].rearrange("p a b c -> p (a b c)")), td[2], t_Va[0])
                            return finish(td)

                        with ExitStack() as p3:
                            NS = 3
                            Sps = [ps(p3, "Sps%d" % i, [128, 2, 512]) for i in range(NS)]
                            t_S = [T("Sps%d" % i) for i in range(NS)]
                            Ops = ps(p3, "Ops0", [128, 2, 512])
                            t_O = T("Ops0")
                            Osb = [sb(p3, "Osb%d" % i, [128, 2, 512], F32) for i in range(2)]
                            t_Osb = [T("Osb%d" % i) for i in range(2)]
                            NPT = 3
                            PT = [sb(p3, "PT%d" % i, [128, 2, 512], BF16) for i in range(NPT)]
                            t_PT = [T("PT%d" % i) for i in range(NPT)]
                            rz = [sb(p3, "rz%d" % i, [128, 2, 512], F32) for i in range(2)]
                            t_rz = [T("rz%d" % i) for i in range(2)]
                            npair = 1 if debug == "ya1" else 4
                            steps = [(i, qb, kt) for i in range(npair) for qb in range(4) for kt in range(32)]
                            Vav = [Va[:, kt, :, :].rearrange("p a b -> p (a b)") for kt in range(32)]

                            def qk(sidx):
                                i, qb, kt = steps[sidx]
                                b = sidx % NS
                                qs = slice(qb * 512, (qb + 1) * 512)
                                ks = slice(kt * 128, (kt + 1) * 128)
                                groups = [(Sps[b][:, 0, :], [(KaT[0:64, ks], QaT[0:64, i, qs])]),
                                          (Sps[b][:, 1, :], [(KaT[64:128, ks], QaT[64:128, i, qs])])]
                                mm_multi(groups, [t_Ka[kt // 4], t_Qa[i][qb]], [t_S[b]])

                            def ex(sidx):
                                b = sidx % NS
                                p = sidx % NPT
                                act(PT[p][:], Sps[b][:], AF.Exp, [t_S[b]], [t_PT[p]], scale=0.125)

                            def pv(sidx):
                                i, qb, kt = steps[sidx]
                                p = sidx % NPT
                                o2 = (sidx // 32) % 2

                                def fn(e):
                                    ins = None
                                    for j in range(2):
                                        lhs = Vav[kt][:, 64 * j:64 * j + 128]
                                        ins = e.matmul(Ops[:, j, :], lhsT=lhs, rhs=PT[p][:, j, :],
                                                       start=(kt == 0), stop=(kt == 31))
                                    return ins
                                fw.op("pe", fn, [t_Va[kt // 4], t_PT[p]], [t_O])
                                if kt == 31:
                                    fw.op("dve", lambda e: e.tensor_copy(out=Osb[o2][:], in_=Ops[:]), [t_O], [t_Osb[o2]])
                                    for j in range(2):
                                        head = i + 4 * j
                                        orow = slice(64 * j, 64 * j + 64)
                                        zrow = slice(64 - 64 * j, 128 - 64 * j)
                                        dve_recip(rz[o2][orow, j, :], Osb[o2][zrow, j, :], [t_Osb[o2]], [t_rz[o2]])
                                        pb_ = 64 * (head % 2)
                                        dve_tt(yaT[pb_:pb_ + 64, head // 2, qb * 512:(qb + 1) * 512], Osb[o2][orow, j, :],
                                               rz[o2][orow, j, :], ALU.mult, [t_Osb[o2], t_rz[o2]], [t_ya[head][qb]])

                            for sidx in range(min(NS, len(steps))):
                                qk(sidx)
                            for sidx in range(len(steps)):
                                ex(sidx)
                                pv(sidx)
                                if sidx + NS < len(steps):
                                    qk(sidx + NS)
                            fw.barrier()
                    if debug in ("ya", "ya1"):
                        o1 = dbg_out("dbg_ya", [128, 4, OWN])
                        td = T("d1")
                        fw.dma("pool", lambda e: e.dma_start(out=o1, in_=yaT[:]), td, t_ya[0][0])
                        return finish([td])

                acc = sb(attn, "acc", [128, 4, OWN], F32)
                rzb = sb(attn, "rzb", [128, 512], F32)
                with ExitStack() as p4:
                    wB = sb(p4, "wB", [128, 8, 768], BF16)
                    t_wB = [T("wBq"), T("wBk"), T("wBv")]
                    QbT = sb(p4, "QbT", [128, 2, OWN], BF16)
                    KbT = sb(p4, "KbT", [128, 2, 4096], BF16)
                    Vb = sb(p4, "Vb", [128, 32, 6, 64], BF16)
                    t_Qb = [T("Qb%d" % i) for i in range(2)]
                    t_Kb = [T("Kb%d" % i) for i in range(2)]
                    t_Vb = T("Vb")
                    t_acc = [T("acc%d" % i) for i in range(2)]
                    tabb = [sb(p4, "tabB%d" % i, [128, 2, 512], F32) for i in range(2)]
                    t_tabb = [T("tabB%d" % i) for i in range(2)]
                    pps = [ps(p4, "pps%d" % i, [128, 512]) for i in range(2)]
                    t_pps = [T("pps%d" % i) for i in range(2)]
                    rps = [ps(p4, "rps%d" % i, [128, 512]) for i in range(2)]
                    t_rps = [T("rps%d" % i) for i in range(2)]
                    OB_ = [ps(p4, "OB%d" % i, [128, 2, 256]) for i in range(2)]
                    t_OB = [T("OB%d" % i) for i in range(2)]
                    Sbuf = [(pps, t_pps), (rps, t_rps)]
                    pbb = [sb(p4, "pbb%d" % i, [128, 512], BF16) for i in range(2)]
                    t_pbb = [T("pbb%d" % i) for i in range(2)]
                    b1 = [sb(p4, "b1_%d" % i, [128, 512], F32) for i in range(2)]
                    t_b1 = [T("b1_%d" % i) for i in range(2)]
                    b2 = [sb(p4, "b2_%d" % i, [128, 512], F32) for i in range(2)]
                    t_b2 = [T("b2_%d" % i) for i in range(2)]
                    EB = [sb(p4, "EB%d" % i, [128, 2, 384], BF16) for i in range(2)]
                    t_EB = [T("EB%d" % i) for i in range(2)]
                    PB = [sb(p4, "PB%d" % i, [128, 2, 384], BF16) for i in range(2)]
                    t_PB = [T("PB%d" % i) for i in range(2)]
                    t_rzb = T("rzb")
                    fw.op("pool", lambda e: e.memset(KbT[:].rearrange("p f (a b) -> p (f a) b", b=128), 0.0), [], t_Kb)
                    fw.op("pool", lambda e: e.memset(Vb[:], 0.0), [], [t_Vb])
                    fw.op("pool", lambda e: e.memset(Vb[:, :, 1:5:3, :], 1.0), [], [t_Vb])
                    def p4_exit():
                        fw.barrier()
                        o1 = dbg_out("dbg_ya", [128, 4, OWN])
                        td = T("d1")
                        fw.dma("pool", lambda e: e.dma_start(out=o1, in_=yaT[:]), td, t_ya[0][0])
                        return finish([td])
                    if debug == "p4_a":
                        return p4_exit()
                    cnt = 0
                    ucnt = 0
                    glist = [(0, 1), (1, 4), (2, 16)]
                    if debug == "yb0":
                        glist = glist[:1]
                    for g, dil in glist:
                        nq = 16 // dil
                        segw = (nq + 1) * 128
                        for part in range(3):
                            c0 = 768 + part * 768 + g * 256
                            fw.dma("pool", lambda e, part=part, c0=c0: e.dma_start(
                                out=wB[:, :, part * 256:(part + 1) * 256], in_=w_qkv_v[:, :, c0:c0 + 256]), t_wB[part])
                        Kview = [KbT[:, ft, 0:dil * segw].rearrange("p (s c) -> p c s", s=dil) for ft in range(2)]
                        Qview = [QbT[:, ft, :].rearrange("p (s c) -> p c s", s=dil) for ft in range(2)]
                        nhalo = 64 * dil
                        blocks = [("own", tb, 512) for tb in range(4)] + \
                                 [("halo", hb, min(512, nhalo)) for hb in range((nhalo + 511) // 512)]
                        if debug == "yb_pown":
                            blocks = [bk for bk in blocks if bk[0] == "own"]
                        def make_tile(kind, bi, blen, what, ft, u_ap, t_u, ti, b, dil, nq, Qview, Kview):
                            pi = 0 if what == "q" else 1
                            wcol = pi * 256 + ft * 128

                            def proj():
                                mm(pps[b][:, 0:blen], [(wB[:, k, wcol:wcol + 128], u_ap[:, k, :]) for k in range(8)],
                                   [t_wB[pi], t_u], t_pps[b])

                            def rest():
                                act(pbb[b][:, 0:blen], pps[b][:, 0:blen], AF.Copy, [t_pps[b]], [t_pbb[b]])
                                mm(rps[b][:, 0:blen], [(RB, pbb[b][:, 0:blen])], [t_pbb[b], t_cm], t_rps[b])
                                dve_tt(b1[b][:, 0:blen], pps[b][:, 0:blen], tabb[ti][:, 0, 0:blen], ALU.mult,
                                       [t_pps[b], t_tabb[ti]], [t_b1[b]])
                                dve_tt(b2[b][:, 0:blen], rps[b][:, 0:blen], tabb[ti][:, 1, 0:blen], ALU.mult,
                                       [t_rps[b], t_tabb[ti]], [t_b2[b]])
                                j0 = (bi * 512) // dil
                                nj = blen // dil
                                if what == "q":
                                    dst = Qview[ft][:, j0:j0 + nj, :]
                                    t_dst = t_Qb[ft]
                                else:
                                    cbase = j0 if kind == "own" else nq * 128 + j0
                                    dst = Kview[ft][:, cbase:cbase + nj, :]
                                    t_dst = t_Kb[ft]
                                in0 = b1[b][:, 0:blen].rearrange("p (j r) -> p j r", r=dil)
                                in1 = b2[b][:, 0:blen].rearrange("p (j r) -> p j r", r=dil)
                                fw.op("pool", lambda e: e.tensor_tensor(out=dst, in0=in0, in1=in1, op=ALU.add),
                                      [t_b1[b], t_b2[b]], [t_dst])
                            return proj, rest

                        def make_tab_dma(ti, tcol, blen):
                            def go():
                                fw.dma("sp", lambda e: e.dma_start(
                                    out=tabb[ti][:, :, 0:blen], in_=tabB[:, :, tcol:tcol + blen].rearrange("c p t -> p c t")), t_tabb[ti])
                            return go

                        psteps = []
                        for kind, bi, blen in blocks:
                            if kind == "own":
                                u_ap, t_u = uT_own[:, :, bi * 512:(bi + 1) * 512], t_uown[bi]
                                tcol = bi * 512
                            else:
                                u_ap, t_u = uT_elo[:, :, bi * 512:bi * 512 + blen], t_uelo[bi]
                                tcol = OWN + bi * 512
                            ti = cnt % 2
                            psteps.append(("dma", make_tab_dma(ti, tcol, blen)))
                            tiles = ([("q", 0), ("q", 1)] if kind == "own" else []) + [("k", 0), ("k", 1)]
                            for what, ft in tiles:
                                psteps.append(("tile",) + make_tile(kind, bi, blen, what, ft, u_ap, t_u, ti, cnt % 2, dil, nq, Qview, Kview))
                                cnt += 1
                        tile_pos = [i for i, s in enumerate(psteps) if s[0] == "tile"]
                        done_proj = set()
                        for pos, s in enumerate(psteps):
                            if s[0] == "dma":
                                s[1]()
                                continue
                            if pos not in done_proj:
                                s[1]()
                                done_proj.add(pos)
                            nxt = next((p for p in tile_pos if p > pos), None)
                            if nxt is not None and nxt not in done_proj:
                                psteps[nxt][1]()
                                done_proj.add(nxt)
                            s[2]()
                        if debug in ("yb_p", "yb_pown"):
                            fw.barrier()
                            o1 = dbg_out("dbg_Qb", [128, 2, OWN])
                            o2 = dbg_out("dbg_Kb", [128, 2, 4096])
                            td = [T("d1"), T("d2")]
                            fw.dma("pool", lambda e: e.dma_start(out=o1, in_=QbT[:]), td[0], t_Qb[0])
                            fw.dma("pool", lambda e: e.dma_start(out=o2, in_=KbT[:]), td[1], t_Kb[0])
                            return finish(td)
                        vt = 0
                        for seg in range(dil):
                            for m in range(nq + 1):
                                tile = seg * (nq + 1) + m
                                ob = vt % 2
                                vt += 1
                                if m < nq:
                                    st0 = seg + dil * 128 * m
                                    lhs = lambda k, st0=st0, dil=dil: uT_own[:, k, sl(st0, 128, dil)]
                                    t_u = t_uown
                                    mrows = 128
                                else:
                                    lhs = lambda k, seg=seg, dil=dil: uT_elo[:, k, sl(seg, 64, dil)]
                                    t_u = t_uelo
                                    mrows = 64
                                mm(OB_[ob][0:mrows, 0, :], [(lhs(k), wB[:, k, 512:768]) for k in range(8)], [t_wB[2]] + t_u, t_OB[ob])
                                for hp2 in range(2):
                                    act(Vb[0:mrows, tile, 3 * hp2:3 * hp2 + 3:2, :],
                                        OB_[ob][0:mrows, 0, hp2 * 128:(hp2 + 1) * 128].rearrange("p (h d) -> p h d", h=2),
                                        AF.Copy, [t_OB[ob]], [t_Vb])
                        if debug == "yb_v":
                            fw.barrier()
                            o1 = dbg_out("dbg_Vb", [128, 32 * 6 * 64])
                            td = [T("d1")]
                            fw.dma("pool", lambda e: e.dma_start(out=o1, in_=Vb[:].rearrange("p a b c -> p (a b c)")), td[0], t_Vb)
                            return finish(td)
                        def make_unit(g, dil, nq, segw, seg, n, hp, u):
                            St, t_St = Sbuf[u]
                            cs = [c for c in range(3) if n - 1 + c >= 0]
                            c_lo = cs[0] * 128
                            qcol = seg * nq * 128 + n * 128

                            def qk():
                                groups = []
                                for hh in range(2):
                                    pr = slice(64 * hh, 64 * hh + 64)
                                    for c in cs:
                                        m = n - 1 + c
                                        kc = seg * segw + m * 128
                                        groups.append((St[hh][:, c * 128:(c + 1) * 128],
                                                       [(KbT[pr, hp, kc:kc + 128], QbT[pr, hp, qcol:qcol + 128])]))
                                mm_multi(groups, [t_Kb[hp], t_Qb[hp]], t_St)

                            def mid():
                                for hh in range(2):
                                    act(EB[u][:, hh, c_lo:384], St[hh][:, c_lo:384], AF.Exp, [t_St[hh]], [t_EB[u]], scale=0.125)
                                for hh in range(2):
                                    dve_tt(PB[u][:, hh, c_lo:384], EB[u][:, hh, c_lo:384], MASK[:, c_lo:384], ALU.mult,
                                           [t_EB[u], t_cm], [t_PB[u]])

                            def pv():
                                def fn(e):
                                    ins = None
                                    for hh in range(2):
                                        head = 2 * hp + hh
                                        for ci, c in enumerate(cs):
                                            m = n - 1 + c
                                            tile = seg * (nq + 1) + m
                                            kr = 64 if m == nq else 128
                                            sblk = (0, 1, 3, 4)[head]
                                            lhs = Vb[0:kr, tile, sblk:sblk + 2, :].rearrange("p a b -> p (a b)")
                                            ins = e.matmul(OB_[u][:, hh, 0:128], lhsT=lhs,
                                                           rhs=PB[u][0:kr, hh, c * 128:(c + 1) * 128],
                                                           start=(ci == 0), stop=(ci == len(cs) - 1))
                                    return ins
                                fw.op("pe", fn, [t_Vb, t_PB[u]], [t_OB[u]])

                            def accum():
                                st0 = seg + dil * 128 * n
                                av = acc[:, 2 * hp:2 * hp + 2, sl(st0, 128, dil)]
                                if g == 0:
                                    act(av, OB_[u][:, :, 0:128], AF.Copy, [t_OB[u]], [t_acc[hp]])
                                else:
                                    dve_tt(av, av, OB_[u][:, :, 0:128], ALU.add, [t_OB[u], t_acc[hp]], [t_acc[hp]])
                            return qk, mid, pv, accum

                        units = []
                        for seg in range(dil):
                            for n in range(nq):
                                for hp in range(2):
                                    units.append(make_unit(g, dil, nq, segw, seg, n, hp, ucnt % 2))
                                    ucnt += 1
                        units[0][0]()
                        for ui in range(len(units)):
                            units[ui][1]()
                            if ui + 1 < len(units):
                                units[ui + 1][0]()
                            units[ui][2]()
                            if ui >= 1:
                                units[ui - 1][3]()
                        units[len(units) - 1][3]()
                    fw.barrier()

                with ExitStack() as p5:
                    wG = sb(p5, "wG", [128, 8, 2 * D], BF16)
                    wPA = sb(p5, "wPA", [128, 4, D], BF16)
                    wPB = sb(p5, "wPB", [128, 2, D], BF16)
                    t_wG = [T("wG0"), T("wG1")]
                    t_wPA, t_wPB = T("wPA"), T("wPB")
                    w_gate_v = w_gate.rearrange("(k p) n -> p k n", p=128)
                    for i in range(2):
                        fw.dma("pool", lambda e, i=i: e.dma_start(out=wG[:, :, i * D:(i + 1) * D], in_=w_gate_v[:, :, i * D:(i + 1) * D]), t_wG[i])
                    fw.dma("pool", lambda e: e.dma_start(out=wPA[:], in_=w_pa.rearrange("(k p) n -> p k n", p=128)), t_wPA)
                    fw.dma("pool", lambda e: e.dma_start(out=wPB[:], in_=w_pb.rearrange("(k p) n -> p k n", p=128)), t_wPB)
                    gsb = sb(p5, "gsb", [128, 16, 512], BF16)
                    t_gsb = [T("gsb%d" % i) for i in range(16)]
                    m1 = [sb(p5, "m1_%d" % i, [128, 512], F32) for i in range(2)]
                    t_m1 = [T("m1_%d" % i) for i in range(2)]
                    m2 = [sb(p5, "m2_%d" % i, [128, 512], F32) for i in range(2)]
                    t_m2 = [T("m2_%d" % i) for i in range(2)]
                    gps = [ps(p5, "gps%d" % i, [128, 512]) for i in range(2)]
                    t_gps = [T("gps%d" % i) for i in range(2)]
                    pap = [ps(p5, "pap%d" % i, [128, 512]) for i in range(2)]
                    t_pap = [T("pap%d" % i) for i in range(2)]
                    pbp = [ps(p5, "pbp%d" % i, [128, 512]) for i in range(2)]
                    t_pbp = [T("pbp%d" % i) for i in range(2)]
                    wad2 = sb(p5, "wad2", [128, 8, D], BF16)
                    t_wad2 = T("wad2")
                    ps_mod2 = ps(p5, "ps_mod2", [128, 512])[:, 0:32]
                    t_psm2 = T("psm2")
                    wadv2 = w_ada.rearrange("(k p) n -> p k n", p=128)

                    def late_load(g):
                        fw.dma("pool", lambda e: e.dma_start(out=wad2[:], in_=wadv2[:, :, g * D:(g + 1) * D]), t_wad2)

                    def late_mm(g):
                        groups = []
                        for j in range(8):
                            col = (g - 2) * 8 + j
                            groups.append((ps_mod2[:, col:col + 1],
                                           [(wad2[:, k, j * 128:(j + 1) * 128], cb[:, k:k + 1]) for k in range(8)]))
                        mm_multi(groups, [t_cb, t_wad2], [t_psm2])
                    for tb in range(4):
                        for h in range(4):
                            cs_ = slice(tb * 512, (tb + 1) * 512)
                            orow = slice(64 * (h % 2), 64 * (h % 2) + 64)
                            zrow = slice(64 - 64 * (h % 2), 128 - 64 * (h % 2))
                            dve_recip(rzb[orow, :], acc[zrow, h, cs_], [t_acc[h // 2]], [t_rzb])
                            dve_tt(ybT[orow, h // 2, cs_], acc[orow, h, cs_], rzb[orow, :], ALU.mult,
                                   [t_acc[h // 2], t_rzb], [t_yb[h][tb]])
                    if debug in ("yb", "yb0"):
                        fw.barrier()
                        o1 = dbg_out("dbg_yb", [128, 2, OWN])
                        td = T("d1")
                        fw.dma("pool", lambda e: e.dma_start(out=o1, in_=ybT[:]), td, t_yb[0][0])
                        return finish([td])
                    late_load(2)
                    for tb in range(4):
                        cs_ = slice(tb * 512, (tb + 1) * 512)
                        for gt in range(16):
                            b = gt % 2
                            mm(gps[b][:], [(wG[:, k, gt * 128:(gt + 1) * 128], uT_own[:, k, cs_]) for k in range(8)],
                               [t_wG[gt // 8], t_uown[tb]], t_gps[b])
                            act(gsb[:, gt, :], gps[b][:], AF.Sigmoid, [t_gps[b], t_vec], [t_gsb[gt]],
                                bias=vcol(V_BG + gt), scale=1.0)
                        for ot in range(8):
                            b = ot % 2
                            osl = slice(ot * 128, (ot + 1) * 128)
                            mm(pap[b][:], [(wPA[:, kc, osl], yaT[:, kc, cs_]) for kc in range(4)],
                               [t_wPA] + [t_ya[h][tb] for h in range(8)], t_pap[b])
                            mm(pbp[b][:], [(wPB[:, kc, osl], ybT[:, kc, cs_]) for kc in range(2)],
                               [t_wPB] + [t_yb[h][tb] for h in range(4)], t_pbp[b])
                            dve_tt(m1[b][:], pap[b][:], gsb[:, ot, :], ALU.mult, [t_pap[b], t_gsb[ot]], [t_m1[b]])
                            dve_tt(m2[b][:], pbp[b][:], gsb[:, 8 + ot, :], ALU.mult, [t_pbp[b], t_gsb[8 + ot]], [t_m2[b]])
                            fw.op("pool", lambda e, ot=ot, b=b, cs_=cs_: e.tensor_tensor(out=uT_own[:, ot, cs_], in0=m1[b][:], in1=m2[b][:], op=ALU.add),
                                  [t_m1[b], t_m2[b]], [t_uown[tb]])
                        late_mm(2 + tb)
                        if tb < 3:
                            late_load(3 + tb)
                    dve_tt(mod[:, 16:48], ps_mod2[:], vcol(V_BADA + 16, 32), ALU.add, [t_psm2, t_vec], [t_modhi])
                    dve_stt(A12[:, 8:16], mod[:, 32:40], 1.0, vcol(V_N2G, 8), ALU.add, ALU.mult, [t_modhi, t_vec], [t_A2])
                    fw.barrier()
                if debug == "mix":
                    o1 = dbg_out("dbg_mix", [128, 8, OWN])
                    td = T("d1")
                    fw.dma("pool", lambda e: e.dma_start(out=o1, in_=uT_own[:]), td, t_uown[0])
                    return finish([td])

        with ExitStack() as ctxB:
            xres = sb(ctxB, "xres", [128, 8, OWN], F32)
            t_x = [[T("x%d_%d" % (tb, k)) for k in range(8)] for tb in range(4)]
            GS = 6
            wI = [sb(ctxB, "wI%d" % i, [128, 8, 2, GS * 128], BF16) for i in range(2)]
            wOu = [sb(ctxB, "wOu%d" % i, [128, GS, D], BF16) for i in range(2)]
            t_wI = [[T("wIg%d" % i), T("wIu%d" % i)] for i in range(2)]
            t_wOu = [T("wOu%d" % i) for i in range(2)]
            w_fi_v = w_fi.rearrange("(k p) n -> p k n", p=128)
            w_fo_v = w_fo.rearrange("(j p) n -> p j n", p=128)

            def load_group(gi):
                j0, gs = FFN_GROUPS[gi]
                wb = gi % 2
                fw.dma("pool", lambda e: e.dma_start(out=wI[wb][:, :, 0, 0:gs * 128], in_=w_fi_v[:, :, j0 * 128:(j0 + gs) * 128]), t_wI[wb][0])
                fw.dma("pool", lambda e: e.dma_start(out=wI[wb][:, :, 1, 0:gs * 128],
                                                      in_=w_fi_v[:, :, DFF + j0 * 128:DFF + (j0 + gs) * 128]), t_wI[wb][1])
                fw.dma("pool", lambda e: e.dma_start(out=wOu[wb][:, 0:gs, :], in_=w_fo_v[:, j0:j0 + gs, :]), t_wOu[wb])

            with ExitStack() as p5b:
                wO = sb(p5b, "wO", [128, 8, D], BF16)
                t_wO = T("wO")
                fw.dma("pool", lambda e: e.dma_start(out=wO[:], in_=w_o.rearrange("(k p) n -> p k n", p=128)), t_wO)
                load_group(0)
                load_group(1)
                ops_ = [ps(p5b, "ops%d" % i, [128, 512]) for i in range(2)]
                t_ops = [T("ops%d" % i) for i in range(2)]
                nb5 = norm_bufs(p5b, "e")
                for tb in range(4):
                    cs_ = slice(tb * 512, (tb + 1) * 512)
                    fw.dma("sp", lambda e, tb=tb: e.dma_start(out=xres[:, :, tb * 512:(tb + 1) * 512],
                                                              in_=xTv[:, :, tb * 512:(tb + 1) * 512]), t_x[tb][0])
                    for k in range(1, 8):
                        t_x[tb][k].lw = t_x[tb][0].lw
                    for ot in range(8):
                        b = ot % 2
                        osl = slice(ot * 128, (ot + 1) * 128)
                        mm(ops_[b][:], [(wO[:, k, osl], uT_own[:, k, cs_]) for k in range(8)], [t_wO, t_uown[tb]], t_ops[b])
                        dve_stt(xres[:, ot, cs_], ops_[b][:], G1(ot), xres[:, ot, cs_], ALU.mult, ALU.add,
                                [t_ops[b], t_modhi, t_x[tb][ot]], [t_x[tb][ot]])
                    norm_block(nb5, xres[:, :, cs_], t_x[tb], lambda k: A12[:, 8 + k:9 + k], SH2, uT_own[:, :, cs_],
                               lambda k, t=t_uown[tb]: t, [t_A2, t_modhi])
                fw.barrier()
            if debug == "x1":
                o1 = dbg_out("dbg_x1", [128, 8, OWN])
                o2 = dbg_out("dbg_u2", [128, 8, OWN])
                td = [T("d1"), T("d2")]
                fw.dma("pool", lambda e: e.dma_start(out=o1, in_=xres[:]), td[0], t_x[0][0])
                fw.dma("pool", lambda e: e.dma_start(out=o2, in_=uT_own[:]), td[1], t_uown[0])
                return finish(td)

            with ExitStack() as p6:
                hbuf = [sb(p6, "hbuf%d" % i, [128, GS, 512], BF16) for i in range(2)]
                t_h = [[T("h%d_%d" % (i, j)) for j in range(GS)] for i in range(2)]
                sg = [sb(p6, "sg%d" % i, [128, 512], F32) for i in range(2)]
                t_sg = [T("sg%d" % i) for i in range(2)]
                hgp = [ps(p6, "hgp%d" % i, [128, 512]) for i in range(2)]
                t_hgp = [T("hgp%d" % i) for i in range(2)]
                hup = [ps(p6, "hup%d" % i, [128, 512]) for i in range(2)]
                t_hup = [T("hup%d" % i) for i in range(2)]
                yps = [ps(p6, "yps%d" % i, [128, 512]) for i in range(2)]
                t_yps = [T("yps%d" % i) for i in range(2)]
                nb6 = norm_bufs(p6, "f")
                t_out = [T("out%d" % i) for i in range(4)]
                cnt = 0
                for gi, (j0, gs) in enumerate(FFN_GROUPS):
                    wb = gi % 2
                    for tb in range(4):
                        cs_ = slice(tb * 512, (tb + 1) * 512)
                        hb = cnt % 2
                        cnt += 1
                        for jj in range(gs):
                            b = jj % 2
                            mm(hgp[b][:], [(wI[wb][:, k, 0, jj * 128:(jj + 1) * 128], uT_own[:, k, cs_]) for k in range(8)],
                               [t_wI[wb][0], t_uown[tb]], t_hgp[b])
                            mm(hup[b][:], [(wI[wb][:, k, 1, jj * 128:(jj + 1) * 128], uT_own[:, k, cs_]) for k in range(8)],
                               [t_wI[wb][1], t_uown[tb]], t_hup[b])
                            act(sg[b][:], hgp[b][:], AF.Silu, [t_hgp[b]], [t_sg[b]])
                            dve_tt(hbuf[hb][:, jj, :], sg[b][:], hup[b][:], ALU.mult, [t_sg[b], t_hup[b]], [t_h[hb][jj]])
                        for ot in range(8):
                            b = ot % 2
                            osl = slice(ot * 128, (ot + 1) * 128)
                            mm(yps[b][:], [(wOu[wb][:, jj, osl], hbuf[hb][:, jj, :]) for jj in range(gs)],
                               [t_wOu[wb]] + t_h[hb][0:gs], t_yps[b])
                            dve_stt(xres[:, ot, cs_], yps[b][:], G2(ot), xres[:, ot, cs_], ALU.mult, ALU.add,
                                    [t_yps[b], t_modhi, t_x[tb][ot]], [t_x[tb][ot]])
                        if gi == len(FFN_GROUPS) - 1:
                            norm_block(nb6, xres[:, :, cs_], t_x[tb], lambda k: vcol(V_FNG + k), None, xres[:, :, cs_],
                                       lambda k, tb=tb: t_x[tb][k], [])
                            for k in range(8):
                                pass
                            tsrc = T("osrc%d" % tb)
                            tsrc.lw = None
                            fw.wait_all("sp", t_x[tb])
                            fw.dma("sp", lambda e, tb=tb: e.dma_start(out=outTv[:, :, tb * 512:(tb + 1) * 512],
                                                                      in_=xres[:, :, tb * 512:(tb + 1) * 512]), t_out[tb], t_x[tb][0])
                    if gi + 2 < len(FFN_GROUPS):
                        load_group(gi + 2)
                fw.wait_all("sp", t_out)
                fw.barrier()
        fw.emit()
    return nc, dbg


def _rope_tab(pos, dim, theta):
    inv = (np.float32(theta) ** (-np.arange(0, dim, 2, dtype=np.float32) / np.float32(dim))).astype(np.float32)
    ang = pos.astype(np.float32)[:, None] * inv[None, :]
    return np.cos(ang).astype(np.float32), np.sin(ang).astype(np.float32)


def _const_mats():
    cmat = np.zeros((128, NCM), np.float32)
    cmat[:, C_ONES:C_ONES + 128] = 1.0
    for hb in (0, 64):
        cmat[hb:hb + 64, C_BO + hb:C_BO + hb + 64] = 1.0
    RA = np.zeros((128, 128), np.float32)
    RB = np.zeros((128, 128), np.float32)
    for hb in (0, 64):
        for half in (0, 32):
            for m in range(16):
                RA[hb + half + m + 16, hb + half + m] = -1.0
                RA[hb + half + m, hb + half + m + 16] = 1.0
        for m in range(8):
            RB[hb + m + 8, hb + m] = -1.0
            RB[hb + m, hb + m + 8] = 1.0
    cmat[:, C_RA:C_RA + 128] = RA
    cmat[:, C_RB:C_RB + 128] = RB
    b = np.arange(128)[:, None]
    a = np.arange(128)[None, :]
    cmat[:, C_MASK + 0:C_MASK + 128] = (b >= a + 64)
    cmat[:, C_MASK + 128:C_MASK + 256] = (np.abs(a - b) <= 64)
    cmat[:, C_MASK + 256:C_MASK + 384] = (a >= b + 64)
    return cmat


def _tables(perm):
    t = perm.astype(np.int64)
    cr, sr = _rope_tab(t // 64, 32, 10000.0)
    cc, sc = _rope_tab(t % 64, 32, 10000.0)
    CA = np.concatenate([cr, cr, cc, cc], axis=1).T
    SA = np.concatenate([sr, sr, sc, sc], axis=1).T
    tabA = np.stack([np.concatenate([CA, CA], 0), np.concatenate([SA, SA], 0)], 0).astype(np.float32)
    tb = t[:3072]
    cp, sp_ = _rope_tab(tb, 16, 500000.0)
    CB = np.ones((64, 3072), np.float32)
    SB = np.zeros((64, 3072), np.float32)
    CB[0:8] = cp.T
    CB[8:16] = cp.T
    SB[0:8] = sp_.T
    SB[8:16] = sp_.T
    tabB = np.stack([np.concatenate([CB, CB], 0), np.concatenate([SB, SB], 0)], 0).astype(np.float32)
    return np.ascontiguousarray(tabA), np.ascontiguousarray(tabB)


def _col(v, n):
    return np.asarray(v, np.float32).reshape(n, 128).T


def make_in_maps(inputs):
    f = lambda k: np.asarray(inputs[k], np.float32)
    x, c = f("x"), f("c")
    cmat = _const_mats()
    shared = {
        "cmat": cmat,
        "w_ada": np.ascontiguousarray(f("w_ada")[0]),
        "w_qkv": np.ascontiguousarray(f("w_qkv")[0]),
        "w_pa": np.ascontiguousarray(f("w_proj_a")[0]),
        "w_pb": np.ascontiguousarray(f("w_proj_b")[0]),
        "w_gate": np.ascontiguousarray(f("w_gate")[0]),
        "w_o": np.ascontiguousarray(f("w_o")[0]),
        "w_fi": np.ascontiguousarray(f("w_ffn_in")[0]),
        "w_fo": np.ascontiguousarray(f("w_ffn_out")[0]),
    }
    vecs = np.zeros((128, NV), np.float32)
    vecs[:, V_BADA:V_BADA + 48] = _col(f("b_ada")[0], 48)
    vecs[:, V_N1G:V_N1G + 8] = _col(f("norm1_g")[0], 8)
    vecs[:, V_N2G:V_N2G + 8] = _col(f("norm2_g")[0], 8)
    vecs[:, V_FNG:V_FNG + 8] = _col(f("final_norm_g"), 8)
    vecs[:, V_BG:V_BG + 16] = _col(f("b_gate")[0], 16)
    vecs[:, V_GQ] = np.tile(f("q_norm_a")[0], 2)
    vecs[:, V_GK] = np.tile(f("k_norm_a")[0], 2)
    perms = [np.arange(S), S - 1 - np.arange(S)]
    tabs = [_tables(p) for p in perms]
    in_maps = []
    for core in range(8):
        b, h = core // 2, core % 2
        m = dict(shared)
        m["xT"] = np.ascontiguousarray(x[b][perms[h]].T)
        m["cvec"] = np.ascontiguousarray(_col(c[b], 8))
        m["vecs"] = vecs
        m["tabA"], m["tabB"] = tabs[h]
        in_maps.append(m)
    return in_maps, perms


def kernel(**inputs):
    in_maps, perms = make_in_maps(inputs)
    nc, _ = build()
    res = run_bass_kernel_spmd(nc, in_maps, core_ids=list(range(8)))
    out = np.zeros((4, S, D), np.float32)
    for core in range(8):
        b, h = core // 2, core % 2
        oT = np.asarray(res.results[core]["outT"], np.float32)
        out[b, perms[h][:OWN], :] = oT.T
    return out
```

```python
import numpy as np
from contextlib import ExitStack
import concourse.bass as bass
import concourse.mybir as mybir
from concourse.bass_utils import run_bass_kernel_spmd

F32 = mybir.dt.float32
BF16 = mybir.dt.bfloat16
AF = mybir.ActivationFunctionType
ALU = mybir.AluOpType

SEM_LIMIT = 30000
S = 4096
OWN = 2048
D = 1024
DFF = 2816
EPS = 1e-6
FFN_GROUPS = [(0, 6), (6, 6), (12, 5), (17, 5)]


class T:
    __slots__ = ("name", "lw", "rd", "dsem", "dcnt", "shared")

    def __init__(self, name, shared=False):
        self.name = name
        self.lw = None
        self.rd = []
        self.dsem = None
        self.dcnt = 0
        self.shared = shared


class FW:
    ENG = ("pe", "act", "dve", "pool", "sp")

    def __init__(self, nc, stack):
        self.nc = nc
        self.stack = stack
        self.ops = {e: [] for e in self.ENG}
        self.sem = {}
        self.cnt = {}
        self.known = {e: {} for e in self.ENG}
        self.snap = {}
        self.nsem = 0
        self.pending_out = []
        for e in self.ENG:
            self._newsem(e)
        self.same_engine_sync = {"pe": False, "act": True, "dve": True, "pool": True, "sp": False}

    def _alloc_sem(self, name):
        self.nsem += 1
        return self.stack.enter_context(self.nc.semaphore("%s_%d" % (name, self.nsem)))

    def _newsem(self, e):
        self.sem[e] = self._alloc_sem("s_" + e)
        self.cnt[e] = 0

    def _learn(self, e, s, v):
        kn = self.known[e]
        kn[s] = v
        sn = self.snap.get((id(s), v))
        if sn:
            for s2, v2 in sn.items():
                if kn.get(s2, 0) < v2:
                    kn[s2] = v2

    def _waits(self, e, reads, writes):
        need = {}

        def add(ev):
            if ev is None:
                return
            s, v = ev
            if need.get(s, 0) < v:
                need[s] = v
        for t in reads:
            add(t.lw)
            if not t.shared:
                for ev in t.rd:
                    if ev[0] is not self.sem[e]:
                        add(ev)
        for t in writes:
            add(t.lw)
            for ev in t.rd:
                add(ev)
        out = []
        kn = self.known[e]
        for s, v in need.items():
            if s is self.sem[e] and not self.same_engine_sync[e]:
                continue
            if kn.get(s, 0) >= v:
                continue
            out.append((s, v))
            self._learn(e, s, v)
        return out

    def op(self, e, fn, reads=(), writes=()):
        waits = self._waits(e, reads, writes)
        if self.cnt[e] >= SEM_LIMIT:
            self._newsem(e)
        self.cnt[e] += 1
        s = self.sem[e]
        ev = (s, self.cnt[e])
        sn = dict(self.known[e])
        sn[s] = self.cnt[e]
        self.snap[(id(s), self.cnt[e])] = sn
        for t in reads:
            t.rd.append(ev)
        for t in writes:
            t.lw = ev
            t.rd = []
        self.ops[e].append((waits, fn, (s, 1)))
        return ev

    def dma(self, q, fn, dst, src=None):
        reads = [src] if src is not None else []
        waits = self._waits(q, reads, [dst])
        if dst.dsem is None:
            dst.dsem = self._alloc_sem("d")
        dst.dcnt += 16
        ev = (dst.dsem, dst.dcnt)
        self.snap[(id(dst.dsem), dst.dcnt)] = dict(self.known[q])
        for t in reads:
            t.rd.append(ev)
            self.pending_out.append(ev)
        dst.lw = ev
        dst.rd = []
        self.ops[q].append((waits, fn, (dst.dsem, 16)))
        return ev

    def wait_all(self, e, tiles):
        waits = self._waits(e, tiles, [])
        self.ops[e].append((waits, None, None))

    def barrier(self):
        evs = [(f, self.sem[f], self.cnt[f]) for f in self.ENG if self.cnt[f] > 0]
        for e in self.ENG:
            waits = []
            for f, s, v in evs:
                if f == e and not self.same_engine_sync[e]:
                    continue
                if self.known[e].get(s, 0) < v:
                    waits.append((s, v))
                    self._learn(e, s, v)
            for s, v in self.pending_out:
                if self.known[e].get(s, 0) < v:
                    waits.append((s, v))
                    self._learn(e, s, v)
            self.ops[e].append((waits, None, None))
        self.pending_out = []

    def emit(self):
        nc = self.nc
        with nc.Block() as block:
            def mk(e):
                def body(eng):
                    for waits, fn, inc in self.ops[e]:
                        for s, v in waits:
                            eng.wait_ge(s, v)
                        if fn is None:
                            continue
                        ins = fn(eng)
                        if inc is not None:
                            ins.then_inc(inc[0], inc[1])
                return body
            block.tensor(mk("pe"))
            block.scalar(mk("act"))
            block.vector(mk("dve"))
            block.gpsimd(mk("pool"))
            block.sync(mk("sp"))


NV = 48 + 8 + 8 + 8 + 16 + 2
V_BADA, V_N1G, V_N2G, V_FNG, V_BG, V_GQ, V_GK = 0, 48, 56, 64, 72, 88, 89
NCM = 4 * 128 + 384
C_ONES, C_BO, C_RA, C_RB, C_MASK = 0, 128, 256, 384, 512


def build(debug=None):
    nc = bass.Bass("TRN2", target_bir_lowering=False)

    def din(name, shape, dt=F32):
        return nc.dram_tensor(name, list(shape), dt, kind="ExternalInput").ap()

    xT = din("xT", [D, S])
    cvec = din("cvec", [128, 8])
    vecs = din("vecs", [128, NV])
    cmat = din("cmat", [128, NCM])
    tabA = din("tabA", [2, 128, S])
    tabB = din("tabB", [2, 128, 3072])
    w_ada = din("w_ada", [D, 6 * D])
    w_qkv = din("w_qkv", [D, 3072])
    w_pa = din("w_pa", [512, D])
    w_pb = din("w_pb", [256, D])
    w_gate = din("w_gate", [D, 2 * D])
    w_o = din("w_o", [D, D])
    w_fi = din("w_fi", [D, 2 * DFF])
    w_fo = din("w_fo", [DFF, D])
    outT = nc.dram_tensor("outT", [D, OWN], F32, kind="ExternalOutput").ap()
    dbg = {}

    def dbg_out(name, shape):
        dbg[name] = nc.dram_tensor(name, list(shape), F32, kind="ExternalOutput").ap()
        return dbg[name]

    xTv = xT.rearrange("(k p) t -> p k t", p=128)
    outTv = outT.rearrange("(k p) t -> p k t", p=128)
    w_qkv_v = w_qkv.rearrange("(k p) n -> p k n", p=128)

    with ExitStack() as top:
        fw = FW(nc, top)

        def sb(ctx, name, shape, dt):
            return ctx.enter_context(nc.sbuf_tensor(name, list(shape), dt))

        def ps(ctx, name, shape, dt=F32):
            return ctx.enter_context(nc.psum_tensor(name, list(shape), dt))

        def mm(out_ap, pairs, reads, twrite):
            def fn(e):
                n = len(pairs)
                ins = None
                for i, (l, r) in enumerate(pairs):
                    ins = e.matmul(out_ap, lhsT=l, rhs=r, start=(i == 0), stop=(i == n - 1))
                return ins
            return fw.op("pe", fn, reads, [twrite])

        def mm_multi(groups, reads, writes):
            def fn(e):
                ins = None
                for out_ap, pairs in groups:
                    n = len(pairs)
                    for i, (l, r) in enumerate(pairs):
                        ins = e.matmul(out_ap, lhsT=l, rhs=r, start=(i == 0), stop=(i == n - 1))
                return ins
            return fw.op("pe", fn, reads, writes)

        def act(out, in_, func, reads, writes, **kw):
            return fw.op("act", lambda e: e.activation(out=out, in_=in_, func=func, **kw), reads, writes)

        def dve_tt(out, in0, in1, op, reads, writes, eng="dve"):
            return fw.op(eng, lambda e: e.tensor_tensor(out=out, in0=in0, in1=in1, op=op), reads, writes)

        def dve_stt(out, in0, scalar, in1, op0, op1, reads, writes):
            return fw.op("dve", lambda e: e.scalar_tensor_tensor(out=out, in0=in0, scalar=scalar, in1=in1,
                                                                op0=op0, op1=op1), reads, writes)

        def dve_recip(out, in_, reads, writes):
            return fw.op("dve", lambda e: e.reciprocal(out=out, in_=in_), reads, writes)

        def sl(start, count, step):
            return slice(start, start + step * (count - 1) + 1, step)

        def finish(tds, q="pool"):
            fw.wait_all(q, tds)
            fw.emit()
            return nc, dbg

        vec_sb = sb(top, "vec_sb", [128, NV], F32)
        cm_f = sb(top, "cm_f", [128, NCM], F32)
        cm = sb(top, "cm", [128, NCM], BF16)
        mod = sb(top, "mod", [128, 48], F32)
        A12 = sb(top, "A12", [128, 16], F32)
        uT_own = sb(top, "uT_own", [128, 8, OWN], BF16)
        t_vec, t_cmf, t_cm, t_mod, t_A = T("vec", shared=True), T("cmf"), T("cm", shared=True), T("mod", shared=True), T("A12", shared=True)
        t_modhi, t_A2 = T("mod_hi", shared=True), T("A12_hi", shared=True)
        cb = sb(top, "cb", [128, 8], BF16)
        t_cb = T("cb", shared=True)
        t_uown = [T("uown%d" % i) for i in range(4)]

        ONES = cm[:, C_ONES:C_ONES + 128]
        BO = cm[:, C_BO:C_BO + 128]
        RA = cm[:, C_RA:C_RA + 128]
        RB = cm[:, C_RB:C_RB + 128]
        MASK = cm[:, C_MASK:C_MASK + 384]

        fw.dma("sp", lambda e: e.dma_start(out=vec_sb[:], in_=vecs), t_vec)
        fw.dma("sp", lambda e: e.dma_start(out=cm_f[:], in_=cmat), t_cmf)
        act(cm[:], cm_f[:], AF.Copy, [t_cmf], [t_cm])

        def vcol(c, n=1):
            return vec_sb[:, c:c + n]

        SH1 = lambda k: mod[:, k:k + 1]
        SH2 = lambda k: mod[:, 24 + k:25 + k]
        G1 = lambda k: mod[:, 16 + k:17 + k]
        G2 = lambda k: mod[:, 40 + k:41 + k]

        def norm_bufs(ctx, tag):
            sq = [sb(ctx, "nrm_sq%s%d" % (tag, i), [128, 512], BF16) for i in range(2)]
            ssp = ps(ctx, "nrm_ssp" + tag, [128, 512])
            rs = sb(ctx, "nrm_rs" + tag, [128, 512], F32)
            rstd = sb(ctx, "nrm_rstd" + tag, [128, 512], F32)
            tmp = [sb(ctx, "ntmp%s%d" % (tag, i), [128, 512], F32) for i in range(2)]
            return dict(sq=sq, t_sq=[T("sq0" + tag), T("sq1" + tag)], ssp=ssp, t_ssp=T("ssp" + tag), rs=rs, t_rs=T("rs" + tag),
                        rstd=rstd, t_rstd=T("rstd" + tag), tmp=tmp, t_tmp=[T("tmp0" + tag), T("tmp1" + tag)])

        def norm_block(nb, src, t_src, a_col, sh_col, dst, t_dst, cdeps):
            for k in range(8):
                i = k % 2
                act(nb["sq"][i][:], src[:, k, :], AF.Square, t_src, [nb["t_sq"][i]])
                fw.op("pe", lambda e, k=k, i=i: e.matmul(nb["ssp"][:], lhsT=ONES, rhs=nb["sq"][i][:], start=(k == 0), stop=(k == 7)),
                      [nb["t_sq"][i], t_cm], [nb["t_ssp"]])
            act(nb["rs"][:], nb["ssp"][:], AF.Sqrt, [nb["t_ssp"]], [nb["t_rs"]], scale=1.0 / D, bias=EPS)
            dve_recip(nb["rstd"][:], nb["rs"][:], [nb["t_rs"]], [nb["t_rstd"]])
            for k in range(8):
                i = k % 2
                rd = t_src + [nb["t_rstd"], t_vec] + cdeps
                if sh_col is None:
                    dve_stt(dst[:, k, :], src[:, k, :], a_col(k), nb["rstd"][:], ALU.mult, ALU.mult, rd, [t_dst(k)])
                else:
                    dve_stt(nb["tmp"][i][:], src[:, k, :], a_col(k), nb["rstd"][:], ALU.mult, ALU.mult, rd, [nb["t_tmp"][i]])
                    act(dst[:, k, :], nb["tmp"][i][:], AF.Identity, [nb["t_tmp"][i]] + cdeps, [t_dst(k)], bias=sh_col(k), scale=1.0)

        with ExitStack() as ctxA:
            yaT = sb(ctxA, "yaT", [128, 4, OWN], BF16)
            ybT = sb(ctxA, "ybT", [128, 2, OWN], BF16)
            t_ya = [[T("ya%d_%d" % (h, q)) for q in range(4)] for h in range(8)]
            t_yb = [[T("yb%d_%d" % (h, q)) for q in range(4)] for h in range(4)]

            with ExitStack() as attn:
                uT_elo = sb(attn, "uT_elo", [128, 8, 1024], BF16)
                t_uelo = [T("uelo%d" % i) for i in range(2)]

                with ExitStack() as pha:
                    uT_ehi = sb(pha, "uT_ehi", [128, 8, 1024], BF16)
                    t_uehi = [T("uehi%d" % i) for i in range(2)]
                    wA = sb(pha, "wA", [128, 8, 768], BF16)
                    t_wA = T("wA")

                    def ublk(blk):
                        if blk < 4:
                            return uT_own[:, :, blk * 512:(blk + 1) * 512], t_uown[blk]
                        if blk < 6:
                            return uT_elo[:, :, (blk - 4) * 512:(blk - 3) * 512], t_uelo[blk - 4]
                        return uT_ehi[:, :, (blk - 6) * 512:(blk - 5) * 512], t_uehi[blk - 6]

                    with ExitStack() as p0:
                        cv = sb(p0, "cv", [128, 8], F32)
                        wad = sb(p0, "wad", [128, 8, 2 * D], BF16)
                        ps_mod = ps(p0, "ps_mod", [128, 512])[:, 0:16]
                        t_cv, t_psm = T("cv"), T("psm")
                        t_wad = [T("wad%d" % i) for i in range(2)]
                        fw.dma("sp", lambda e: e.dma_start(out=cv[:], in_=cvec), t_cv)
                        wadv = w_ada.rearrange("(k p) n -> p k n", p=128)
                        for g in (1, 0):
                            fw.dma("pool", lambda e, g=g: e.dma_start(out=wad[:, :, g * D:(g + 1) * D],
                                                                     in_=wadv[:, :, g * D:(g + 1) * D]), t_wad[g])
                        t_wAq = [T("wAq%d" % i) for i in range(8)]
                        for hd in range(8):
                            slot = (hd % 4) * 2 + hd // 4
                            fw.dma("pool", lambda e, hd=hd, slot=slot: e.dma_start(
                                out=wA[:, :, slot * 64:(slot + 1) * 64], in_=w_qkv_v[:, :, hd * 64:(hd + 1) * 64]), t_wAq[hd])
                        fw.dma("pool", lambda e: e.dma_start(out=wA[:, :, 512:768], in_=w_qkv_v[:, :, 512:768]), t_wA)
                        act(cb[:], cv[:], AF.Silu, [t_cv], [t_cb])
                        for g in (1, 0):
                            groups = []
                            for j in range(g * 8, g * 8 + 8):
                                groups.append((ps_mod[:, j:j + 1],
                                               [(wad[:, k, j * 128:(j + 1) * 128], cb[:, k:k + 1]) for k in range(8)]))
                            mm_multi(groups, [t_cb, t_wad[g]], [t_psm])
                        dve_tt(mod[:, 0:16], ps_mod[:], vcol(V_BADA, 16), ALU.add, [t_psm, t_vec], [t_mod])
                        dve_stt(A12[:, 0:8], mod[:, 8:16], 1.0, vcol(V_N1G, 8), ALU.add, ALU.mult, [t_mod, t_vec], [t_A])
                        fw.barrier()
                    if debug == "mod":
                        o = dbg_out("dbg_mod", [128, 64])
                        td, td2 = T("dbgmod"), T("dbgA")
                        fw.dma("sp", lambda e: e.dma_start(out=o[:, 0:16], in_=mod[:, 0:16]), td, t_mod)
                        fw.dma("sp", lambda e: e.dma_start(out=o[:, 48:56], in_=A12[:, 0:8]), td2, t_A)
                        return finish([td, td2], "sp")

                    with ExitStack() as p1:
                        xb = [sb(p1, "xb%d" % i, [128, 8, 512], F32) for i in range(2)]
                        t_xb = [T("xb%d" % i) for i in range(2)]
                        nb = [norm_bufs(p1, "a"), norm_bufs(p1, "b")]
                        for blk in range(8):
                            i = blk % 2
                            fw.dma("sp", lambda e, blk=blk, i=i: e.dma_start(out=xb[i][:], in_=xTv[:, :, blk * 512:(blk + 1) * 512]), t_xb[i])
                            dst, t_dst = ublk(blk)
                            norm_block(nb[i], xb[i][:], [t_xb[i]], lambda k: A12[:, k:k + 1], SH1, dst, lambda k, t=t_dst: t, [t_A, t_mod])
                        fw.barrier()
                    if debug == "u":
                        o = dbg_out("dbg_u", [128, 8, S])
                        tds = [T("dbgu%d" % i) for i in range(8)]
                        for blk in range(8):
                            src, t_s = ublk(blk)
                            fw.dma("pool", lambda e, blk=blk, src=src: e.dma_start(out=o[:, :, blk * 512:(blk + 1) * 512], in_=src), tds[blk], t_s)
                        return finish(tds)

                    with ExitStack() as p23:
                        QaT = sb(p23, "QaT", [128, 4, OWN], BF16)
                        KaT = sb(p23, "KaT", [128, S], BF16)
                        Va = sb(p23, "Va", [128, 32, 3, 64], BF16)
                        t_Qa = [[T("Qa%d_%d" % (i, q)) for q in range(4)] for i in range(4)]
                        t_Ka = [T("Ka%d" % q) for q in range(8)]
                        t_Va = [T("Va%d" % q) for q in range(8)]
                        t_Vones = T("Vones")
                        fw.op("pool", lambda e: e.memset(Va[:, :, 1, :], 1.0), [], [t_Vones])
                        with ExitStack() as p2:
                            tab = [sb(p2, "tabA%d" % i, [128, 2, 512], F32) for i in range(2)]
                            t_tab = [T("tabA%d" % i) for i in range(2)]

                            def two(name, shape, dt, psum=False):
                                if psum:
                                    return [ps(p2, "%s%d" % (name, i), shape) for i in range(2)], [T("%s%d" % (name, i)) for i in range(2)]
                                return [sb(p2, "%s%d" % (name, i), shape, dt) for i in range(2)], [T("%s%d" % (name, i)) for i in range(2)]
                            qps, t_qps = two("qps", [128, 512], F32, True)
                            ssq, t_ssq = two("ssq", [128, 512], F32, True)
                            rqp, t_rqp = two("rqp", [128, 512], F32, True)
                            vps, t_vps = two("vps", [128, 4, 128], F32, True)
                            sqb, t_sqb = two("sqb", [128, 512], BF16)
                            qg, t_qg = two("qg", [128, 512], BF16)
                            rsq, t_rsq = two("rsq", [128, 512], F32)
                            rq, t_rq = two("rq", [128, 512], F32)
                            t1, t_t1 = two("t1_", [128, 512], F32)
                            t2, t_t2 = two("t2_", [128, 512], F32)
                            cnt = 0
                            for blk in range(8):
                                ti = blk % 2
                                fw.dma("sp", lambda e, blk=blk, ti=ti: e.dma_start(
                                    out=tab[ti][:], in_=tabA[:, :, blk * 512:(blk + 1) * 512].rearrange("c p t -> p c t")), t_tab[ti])
                                u_ap, t_u = ublk(blk)
                                tiles = ([("q", i) for i in range(4)] if blk < 4 else []) + [("k", 0)]
                                for kind, i in tiles:
                                    b = cnt % 2
                                    cnt += 1
                                    if kind == "q":
                                        def lw(k, i=i):
                                            return wA[:, k, i * 128:(i + 1) * 128]
                                        gain = vcol(V_GQ)
                                        dst = QaT[:, i, blk * 512:(blk + 1) * 512]
                                        t_dst = t_Qa[i][blk]
                                    else:
                                        def lw(k):
                                            return wA[:, k, 512:640]
                                        gain = vcol(V_GK)
                                        dst = KaT[:, blk * 512:(blk + 1) * 512]
                                        t_dst = t_Ka[blk]
                                    mm(qps[b][:], [(lw(k), u_ap[:, k, :]) for k in range(8)], [t_wA, t_u] + t_wAq, t_qps[b])
                                    act(sqb[b][:], qps[b][:], AF.Square, [t_qps[b]], [t_sqb[b]])
                                    act(qg[b][:], qps[b][:], AF.Copy, [t_qps[b], t_vec], [t_qg[b]], scale=gain)
                                    mm(ssq[b][:], [(BO, sqb[b][:])], [t_sqb[b], t_cm], t_ssq[b])
                                    mm(rqp[b][:], [(RA, qg[b][:])], [t_qg[b], t_cm], t_rqp[b])
                                    act(rsq[b][:], ssq[b][:], AF.Sqrt, [t_ssq[b]], [t_rsq[b]], scale=1.0 / 64, bias=EPS)
                                    dve_recip(rq[b][:], rsq[b][:], [t_rsq[b]], [t_rq[b]])
                                    dve_stt(t1[b][:], qps[b][:], gain, tab[ti][:, 0, :], ALU.mult, ALU.mult,
                                            [t_qps[b], t_tab[ti], t_vec], [t_t1[b]])
                                    dve_tt(t2[b][:], rqp[b][:], tab[ti][:, 1, :], ALU.mult, [t_rqp[b], t_tab[ti]], [t_t2[b]])
                                    dve_tt(t1[b][:], t1[b][:], t2[b][:], ALU.add, [t_t1[b], t_t2[b]], [t_t1[b]], eng="pool")
                                    dve_tt(dst, t1[b][:], rq[b][:], ALU.mult, [t_t1[b], t_rq[b]], [t_dst], eng="pool")
                                vb = blk % 2
                                groups = []
                                for j in range(4):
                                    groups.append((vps[vb][:, j, :],
                                                   [(u_ap[:, k, j * 128:(j + 1) * 128], wA[:, k, 640:768]) for k in range(8)]))
                                mm_multi(groups, [t_wA, t_u], [t_vps[vb]])
                                for v in range(2):
                                    act(Va[:, blk * 4:blk * 4 + 4, 2 * v, :], vps[vb][:, :, v * 64:(v + 1) * 64], AF.Copy,
                                        [t_vps[vb], t_Vones], [t_Va[blk]])
                            fw.barrier()
                        if debug == "qkv_a":
                            o1 = dbg_out("dbg_Qa", [128, 4, OWN])
                            o2 = dbg_out("dbg_Ka", [128, S])
                            o3 = dbg_out("dbg_Va", [128, 32 * 3 * 64])
                            td = [T("d1"), T("d2"), T("d3")]
                            fw.dma("pool", lambda e: e.dma_start(out=o1, in_=QaT[:]), td[0], t_Qa[0][0])
                            fw.dma("pool", lambda e: e.dma_start(out=o2, in_=KaT[:]), td[1], t_Ka[0])
                            fw.dma("pool", lambda e: e.dma_start(out=o3, in_=Va[:].rearrange("p a b c -> p (a b c)")), td[2], t_Va[0])
                            return finish(td)

                        with ExitStack() as p3:
                            NS = 3
                            Sps = [ps(p3, "Sps%d" % i, [128, 2, 512]) for i in range(NS)]
                            t_S = [T("Sps%d" % i) for i in range(NS)]
                            Ops = ps(p3, "Ops0", [128, 2, 512])
                            t_O = T("Ops0")
                            Osb = [sb(p3, "Osb%d" % i, [128, 2, 512], F32) for i in range(2)]
                            t_Osb = [T("Osb%d" % i) for i in range(2)]
                            NPT = 3
                            PT = [sb(p3, "PT%d" % i, [128, 2, 512], BF16) for i in range(NPT)]
                            t_PT = [T("PT%d" % i) for i in range(NPT)]
                            rz = [sb(p3, "rz%d" % i, [128, 2, 512], F32) for i in range(2)]
                            t_rz = [T("rz%d" % i) for i in range(2)]
                            npair = 1 if debug == "ya1" else 4
                            steps = [(i, qb, kt) for i in range(npair) for qb in range(4) for kt in range(32)]
                            Vav = [Va[:, kt, :, :].rearrange("p a b -> p (a b)") for kt in range(32)]

                            def qk(sidx):
                                i, qb, kt = steps[sidx]
                                b = sidx % NS
                                qs = slice(qb * 512, (qb + 1) * 512)
                                ks = slice(kt * 128, (kt + 1) * 128)
                                groups = [(Sps[b][:, 0, :], [(KaT[0:64, ks], QaT[0:64, i, qs])]),
                                          (Sps[b][:, 1, :], [(KaT[64:128, ks], QaT[64:128, i, qs])])]
                                mm_multi(groups, [t_Ka[kt // 4], t_Qa[i][qb]], [t_S[b]])

                            def ex(sidx):
                                b = sidx % NS
                                p = sidx % NPT
                                act(PT[p][:], Sps[b][:], AF.Exp, [t_S[b]], [t_PT[p]], scale=0.125)

                            def pv(sidx):
                                i, qb, kt = steps[sidx]
                                p = sidx % NPT
                                o2 = (sidx // 32) % 2

                                def fn(e):
                                    ins = None
                                    for j in range(2):
                                        lhs = Vav[kt][:, 64 * j:64 * j + 128]
                                        ins = e.matmul(Ops[:, j, :], lhsT=lhs, rhs=PT[p][:, j, :],
                                                       start=(kt == 0), stop=(kt == 31))
                                    return ins
                                fw.op("pe", fn, [t_Va[kt // 4], t_PT[p]], [t_O])
                                if kt == 31:
                                    fw.op("dve", lambda e: e.tensor_copy(out=Osb[o2][:], in_=Ops[:]), [t_O], [t_Osb[o2]])
                                    for j in range(2):
                                        head = i + 4 * j
                                        orow = slice(64 * j, 64 * j + 64)
                                        zrow = slice(64 - 64 * j, 128 - 64 * j)
                                        dve_recip(rz[o2][orow, j, :], Osb[o2][zrow, j, :], [t_Osb[o2]], [t_rz[o2]])
                                        pb_ = 64 * (head % 2)
                                        dve_tt(yaT[pb_:pb_ + 64, head // 2, qb * 512:(qb + 1) * 512], Osb[o2][orow, j, :],
                                               rz[o2][orow, j, :], ALU.mult, [t_Osb[o2], t_rz[o2]], [t_ya[head][qb]])

                            for sidx in range(min(NS, len(steps))):
                                qk(sidx)
                            for sidx in range(len(steps)):
                                ex(sidx)
                                pv(sidx)
                                if sidx + NS < len(steps):
                                    qk(sidx + NS)
                            fw.barrier()
                    if debug in ("ya", "ya1"):
                        o1 = dbg_out("dbg_ya", [128, 4, OWN])
                        td = T("d1")
                        fw.dma("pool", lambda e: e.dma_start(out=o1, in_=yaT[:]), td, t_ya[0][0])
                        return finish([td])

                acc = sb(attn, "acc", [128, 4, OWN], F32)
                rzb = sb(attn, "rzb", [128, 512], F32)
                with ExitStack() as p4:
                    wB = sb(p4, "wB", [128, 8, 768], BF16)
                    t_wB = [T("wBq"), T("wBk"), T("wBv")]
                    QbT = sb(p4, "QbT", [128, 2, OWN], BF16)
                    KbT = sb(p4, "KbT", [128, 2, 4096], BF16)
                    Vb = sb(p4, "Vb", [128, 32, 6, 64], BF16)
                    t_Qb = [T("Qb%d" % i) for i in range(2)]
                    t_Kb = [T("Kb%d" % i) for i in range(2)]
                    t_Vb = T("Vb")
                    t_acc = [T("acc%d" % i) for i in range(2)]
                    tabb = [sb(p4, "tabB%d" % i, [128, 2, 512], F32) for i in range(2)]
                    t_tabb = [T("tabB%d" % i) for i in range(2)]
                    pps = [ps(p4, "pps%d" % i, [128, 512]) for i in range(2)]
                    t_pps = [T("pps%d" % i) for i in range(2)]
                    rps = [ps(p4, "rps%d" % i, [128, 512]) for i in range(2)]
                    t_rps = [T("rps%d" % i) for i in range(2)]
                    OB_ = [ps(p4, "OB%d" % i, [128, 2, 256]) for i in range(2)]
                    t_OB = [T("OB%d" % i) for i in range(2)]
                    Sbuf = [(pps, t_pps), (rps, t_rps)]
                    pbb = [sb(p4, "pbb%d" % i, [128, 512], BF16) for i in range(2)]
                    t_pbb = [T("pbb%d" % i) for i in range(2)]
                    b1 = [sb(p4, "b1_%d" % i, [128, 512], F32) for i in range(2)]
                    t_b1 = [T("b1_%d" % i) for i in range(2)]
                    b2 = [sb(p4, "b2_%d" % i, [128, 512], F32) for i in range(2)]
                    t_b2 = [T("b2_%d" % i) for i in range(2)]
                    EB = [sb(p4, "EB%d" % i, [128, 2, 384], BF16) for i in range(2)]
                    t_EB = [T("EB%d" % i) for i in range(2)]
                    PB = [sb(p4, "PB%d" % i, [128, 2, 384], BF16) for i in range(2)]
                    t_PB = [T("PB%d" % i) for i in range(2)]
                    t_rzb = T("rzb")
                    fw.op("pool", lambda e: e.memset(KbT[:].rearrange("p f (a b) -> p (f a) b", b=128), 0.0), [], t_Kb)
                    fw.op("pool", lambda e: e.memset(Vb[:], 0.0), [], [t_Vb])
                    fw.op("pool", lambda e: e.memset(Vb[:, :, 1:5:3, :], 1.0), [], [t_Vb])
                    def p4_exit():
                        fw.barrier()
                        o1 = dbg_out("dbg_ya", [128, 4, OWN])
                        td = T("d1")
                        fw.dma("pool", lambda e: e.dma_start(out=o1, in_=yaT[:]), td, t_ya[0][0])
                        return finish([td])
                    if debug == "p4_a":
                        return p4_exit()
                    cnt = 0
                    ucnt = 0
                    glist = [(0, 1), (1, 4), (2, 16)]
                    if debug == "yb0":
                        glist = glist[:1]
                    for g, dil in glist:
                        nq = 16 // dil
                        segw = (nq + 1) * 128
                        for part in range(3):
                            c0 = 768 + part * 768 + g * 256
                            fw.dma("pool", lambda e, part=part, c0=c0: e.dma_start(
                                out=wB[:, :, part * 256:(part + 1) * 256], in_=w_qkv_v[:, :, c0:c0 + 256]), t_wB[part])
                        Kview = [KbT[:, ft, 0:dil * segw].rearrange("p (s c) -> p c s", s=dil) for ft in range(2)]
                        Qview = [QbT[:, ft, :].rearrange("p (s c) -> p c s", s=dil) for ft in range(2)]
                        nhalo = 64 * dil
                        blocks = [("own", tb, 512) for tb in range(4)] + \
                                 [("halo", hb, min(512, nhalo)) for hb in range((nhalo + 511) // 512)]
                        if debug == "yb_pown":
                            blocks = [bk for bk in blocks if bk[0] == "own"]
                        def make_tile(kind, bi, blen, what, ft, u_ap, t_u, ti, b, dil, nq, Qview, Kview):
                            pi = 0 if what == "q" else 1
                            wcol = pi * 256 + ft * 128

                            def proj():
                                mm(pps[b][:, 0:blen], [(wB[:, k, wcol:wcol + 128], u_ap[:, k, :]) for k in range(8)],
                                   [t_wB[pi], t_u], t_pps[b])

                            def rest():
                                act(pbb[b][:, 0:blen], pps[b][:, 0:blen], AF.Copy, [t_pps[b]], [t_pbb[b]])
                                mm(rps[b][:, 0:blen], [(RB, pbb[b][:, 0:blen])], [t_pbb[b], t_cm], t_rps[b])
                                dve_tt(b1[b][:, 0:blen], pps[b][:, 0:blen], tabb[ti][:, 0, 0:blen], ALU.mult,
                                       [t_pps[b], t_tabb[ti]], [t_b1[b]])
                                dve_tt(b2[b][:, 0:blen], rps[b][:, 0:blen], tabb[ti][:, 1, 0:blen], ALU.mult,
                                       [t_rps[b], t_tabb[ti]], [t_b2[b]])
                                j0 = (bi * 512) // dil
                                nj = blen // dil
                                if what == "q":
                                    dst = Qview[ft][:, j0:j0 + nj, :]
                                    t_dst = t_Qb[ft]
                                else:
                                    cbase = j0 if kind == "own" else nq * 128 + j0
                                    dst = Kview[ft][:, cbase:cbase + nj, :]
                                    t_dst = t_Kb[ft]
                                in0 = b1[b][:, 0:blen].rearrange("p (j r) -> p j r", r=dil)
                                in1 = b2[b][:, 0:blen].rearrange("p (j r) -> p j r", r=dil)
                                fw.op("pool", lambda e: e.tensor_tensor(out=dst, in0=in0, in1=in1, op=ALU.add),
                                      [t_b1[b], t_b2[b]], [t_dst])
                            return proj, rest

                        def make_tab_dma(ti, tcol, blen):
                            def go():
                                fw.dma("sp", lambda e: e.dma_start(
                                    out=tabb[ti][:, :, 0:blen], in_=tabB[:, :, tcol:tcol + blen].rearrange("c p t -> p c t")), t_tabb[ti])
                            return go

                        psteps = []
                        for kind, bi, blen in blocks:
                            if kind == "own":
                                u_ap, t_u = uT_own[:, :, bi * 512:(bi + 1) * 512], t_uown[bi]
                                tcol = bi * 512
                            else:
                                u_ap, t_u = uT_elo[:, :, bi * 512:bi * 512 + blen], t_uelo[bi]
                                tcol = OWN + bi * 512
                            ti = cnt % 2
                            psteps.append(("dma", make_tab_dma(ti, tcol, blen)))
                            tiles = ([("q", 0), ("q", 1)] if kind == "own" else []) + [("k", 0), ("k", 1)]
                            for what, ft in tiles:
                                psteps.append(("tile",) + make_tile(kind, bi, blen, what, ft, u_ap, t_u, ti, cnt % 2, dil, nq, Qview, Kview))
                                cnt += 1
                        tile_pos = [i for i, s in enumerate(psteps) if s[0] == "tile"]
                        done_proj = set()
                        for pos, s in enumerate(psteps):
                            if s[0] == "dma":
                                s[1]()
                                continue
                            if pos not in done_proj:
                                s[1]()
                                done_proj.add(pos)
                            nxt = next((p for p in tile_pos if p > pos), None)
                            if nxt is not None and nxt not in done_proj:
                                psteps[nxt][1]()
                                done_proj.add(nxt)
                            s[2]()
                        if debug in ("yb_p", "yb_pown"):
                            fw.barrier()
                            o1 = dbg_out("dbg_Qb", [128, 2, OWN])
                            o2 = dbg_out("dbg_Kb", [128, 2, 4096])
                            td = [T("d1"), T("d2")]
                            fw.dma("pool", lambda e: e.dma_start(out=o1, in_=QbT[:]), td[0], t_Qb[0])
                            fw.dma("pool", lambda e: e.dma_start(out=o2, in_=KbT[:]), td[1], t_Kb[0])
                            return finish(td)
                        vt = 0
                        for seg in range(dil):
                            for m in range(nq + 1):
                                tile = seg * (nq + 1) + m
                                ob = vt % 2
                                vt += 1
                                if m < nq:
                                    st0 = seg + dil * 128 * m
                                    lhs = lambda k, st0=st0, dil=dil: uT_own[:, k, sl(st0, 128, dil)]
                                    t_u = t_uown
                                    mrows = 128
                                else:
                                    lhs = lambda k, seg=seg, dil=dil: uT_elo[:, k, sl(seg, 64, dil)]
                                    t_u = t_uelo
                                    mrows = 64
                                mm(OB_[ob][0:mrows, 0, :], [(lhs(k), wB[:, k, 512:768]) for k in range(8)], [t_wB[2]] + t_u, t_OB[ob])
                                for hp2 in range(2):
                                    act(Vb[0:mrows, tile, 3 * hp2:3 * hp2 + 3:2, :],
                                        OB_[ob][0:mrows, 0, hp2 * 128:(hp2 + 1) * 128].rearrange("p (h d) -> p h d", h=2),
                                        AF.Copy, [t_OB[ob]], [t_Vb])
                        if debug == "yb_v":
                            fw.barrier()
                            o1 = dbg_out("dbg_Vb", [128, 32 * 6 * 64])
                            td = [T("d1")]
                            fw.dma("pool", lambda e: e.dma_start(out=o1, in_=Vb[:].rearrange("p a b c -> p (a b c)")), td[0], t_Vb)
                            return finish(td)
                        def make_unit(g, dil, nq, segw, seg, n, hp, u):
                            St, t_St = Sbuf[u]
                            cs = [c for c in range(3) if n - 1 + c >= 0]
                            c_lo = cs[0] * 128
                            qcol = seg * nq * 128 + n * 128

                            def qk():
                                groups = []
                                for hh in range(2):
                                    pr = slice(64 * hh, 64 * hh + 64)
                                    for c in cs:
                                        m = n - 1 + c
                                        kc = seg * segw + m * 128
                                        groups.append((St[hh][:, c * 128:(c + 1) * 128],
                                                       [(KbT[pr, hp, kc:kc + 128], QbT[pr, hp, qcol:qcol + 128])]))
                                mm_multi(groups, [t_Kb[hp], t_Qb[hp]], t_St)

                            def mid():
                                for hh in range(2):
                                    act(EB[u][:, hh, c_lo:384], St[hh][:, c_lo:384], AF.Exp, [t_St[hh]], [t_EB[u]], scale=0.125)
                                for hh in range(2):
                                    dve_tt(PB[u][:, hh, c_lo:384], EB[u][:, hh, c_lo:384], MASK[:, c_lo:384], ALU.mult,
                                           [t_EB[u], t_cm], [t_PB[u]])

                            def pv():
                                def fn(e):
                                    ins = None
                                    for hh in range(2):
                                        head = 2 * hp + hh
                                        for ci, c in enumerate(cs):
                                            m = n - 1 + c
                                            tile = seg * (nq + 1) + m
                                            kr = 64 if m == nq else 128
                                            sblk = (0, 1, 3, 4)[head]
                                            lhs = Vb[0:kr, tile, sblk:sblk + 2, :].rearrange("p a b -> p (a b)")
                                            ins = e.matmul(OB_[u][:, hh, 0:128], lhsT=lhs,
                                                           rhs=PB[u][0:kr, hh, c * 128:(c + 1) * 128],
                                                           start=(ci == 0), stop=(ci == len(cs) - 1))
                                    return ins
                                fw.op("pe", fn, [t_Vb, t_PB[u]], [t_OB[u]])

                            def accum():
                                st0 = seg + dil * 128 * n
                                av = acc[:, 2 * hp:2 * hp + 2, sl(st0, 128, dil)]
                                if g == 0:
                                    act(av, OB_[u][:, :, 0:128], AF.Copy, [t_OB[u]], [t_acc[hp]])
                                else:
                                    dve_tt(av, av, OB_[u][:, :, 0:128], ALU.add, [t_OB[u], t_acc[hp]], [t_acc[hp]])
                            return qk, mid, pv, accum

                        units = []
                        for seg in range(dil):
                            for n in range(nq):
                                for hp in range(2):
                                    units.append(make_unit(g, dil, nq, segw, seg, n, hp, ucnt % 2))
                                    ucnt += 1
                        units[0][0]()
                        for ui in range(len(units)):
                            units[ui][1]()
                            if ui + 1 < len(units):
                                units[ui + 1][0]()
                            units[ui][2]()
                            if ui >= 1:
                                units[ui - 1][3]()
                        units[len(units) - 1][3]()
                    fw.barrier()

                with ExitStack() as p5:
                    wG = sb(p5, "wG", [128, 8, 2 * D], BF16)
                    wPA = sb(p5, "wPA", [128, 4, D], BF16)
                    wPB = sb(p5, "wPB", [128, 2, D], BF16)
                    t_wG = [T("wG0"), T("wG1")]
                    t_wPA, t_wPB = T("wPA"), T("wPB")
                    w_gate_v = w_gate.rearrange("(k p) n -> p k n", p=128)
                    for i in range(2):
                        fw.dma("pool", lambda e, i=i: e.dma_start(out=wG[:, :, i * D:(i + 1) * D], in_=w_gate_v[:, :, i * D:(i + 1) * D]), t_wG[i])
                    fw.dma("pool", lambda e: e.dma_start(out=wPA[:], in_=w_pa.rearrange("(k p) n -> p k n", p=128)), t_wPA)
                    fw.dma("pool", lambda e: e.dma_start(out=wPB[:], in_=w_pb.rearrange("(k p) n -> p k n", p=128)), t_wPB)
                    gsb = sb(p5, "gsb", [128, 16, 512], BF16)
                    t_gsb = [T("gsb%d" % i) for i in range(16)]
                    m1 = [sb(p5, "m1_%d" % i, [128, 512], F32) for i in range(2)]
                    t_m1 = [T("m1_%d" % i) for i in range(2)]
                    m2 = [sb(p5, "m2_%d" % i, [128, 512], F32) for i in range(2)]
                    t_m2 = [T("m2_%d" % i) for i in range(2)]
                    gps = [ps(p5, "gps%d" % i, [128, 512]) for i in range(2)]
                    t_gps = [T("gps%d" % i) for i in range(2)]
                    pap = [ps(p5, "pap%d" % i, [128, 512]) for i in range(2)]
                    t_pap = [T("pap%d" % i) for i in range(2)]
                    pbp = [ps(p5, "pbp%d" % i, [128, 512]) for i in range(2)]
                    t_pbp = [T("pbp%d" % i) for i in range(2)]
                    wad2 = sb(p5, "wad2", [128, 8, D], BF16)
                    t_wad2 = T("wad2")
                    ps_mod2 = ps(p5, "ps_mod2", [128, 512])[:, 0:32]
                    t_psm2 = T("psm2")
                    wadv2 = w_ada.rearrange("(k p) n -> p k n", p=128)

                    def late_load(g):
                        fw.dma("pool", lambda e: e.dma_start(out=wad2[:], in_=wadv2[:, :, g * D:(g + 1) * D]), t_wad2)

                    def late_mm(g):
                        groups = []
                        for j in range(8):
                            col = (g - 2) * 8 + j
                            groups.append((ps_mod2[:, col:col + 1],
                                           [(wad2[:, k, j * 128:(j + 1) * 128], cb[:, k:k + 1]) for k in range(8)]))
                        mm_multi(groups, [t_cb, t_wad2], [t_psm2])
                    for tb in range(4):
                        for h in range(4):
                            cs_ = slice(tb * 512, (tb + 1) * 512)
                            orow = slice(64 * (h % 2), 64 * (h % 2) + 64)
                            zrow = slice(64 - 64 * (h % 2), 128 - 64 * (h % 2))
                            dve_recip(rzb[orow, :], acc[zrow, h, cs_], [t_acc[h // 2]], [t_rzb])
                            dve_tt(ybT[orow, h // 2, cs_], acc[orow, h, cs_], rzb[orow, :], ALU.mult,
                                   [t_acc[h // 2], t_rzb], [t_yb[h][tb]])
                    if debug in ("yb", "yb0"):
                        fw.barrier()
                        o1 = dbg_out("dbg_yb", [128, 2, OWN])
                        td = T("d1")
                        fw.dma("pool", lambda e: e.dma_start(out=o1, in_=ybT[:]), td, t_yb[0][0])
                        return finish([td])
                    late_load(2)
                    for tb in range(4):
                        cs_ = slice(tb * 512, (tb + 1) * 512)
                        for gt in range(16):
                            b = gt % 2
                            mm(gps[b][:], [(wG[:, k, gt * 128:(gt + 1) * 128], uT_own[:, k, cs_]) for k in range(8)],
                               [t_wG[gt // 8], t_uown[tb]], t_gps[b])
                            act(gsb[:, gt, :], gps[b][:], AF.Sigmoid, [t_gps[b], t_vec], [t_gsb[gt]],
                                bias=vcol(V_BG + gt), scale=1.0)
                        for ot in range(8):
                            b = ot % 2
                            osl = slice(ot * 128, (ot + 1) * 128)
                            mm(pap[b][:], [(wPA[:, kc, osl], yaT[:, kc, cs_]) for kc in range(4)],
                               [t_wPA] + [t_ya[h][tb] for h in range(8)], t_pap[b])
                            mm(pbp[b][:], [(wPB[:, kc, osl], ybT[:, kc, cs_]) for kc in range(2)],
                               [t_wPB] + [t_yb[h][tb] for h in range(4)], t_pbp[b])
                            dve_tt(m1[b][:], pap[b][:], gsb[:, ot, :], ALU.mult, [t_pap[b], t_gsb[ot]], [t_m1[b]])
                            dve_tt(m2[b][:], pbp[b][:], gsb[:, 8 + ot, :], ALU.mult, [t_pbp[b], t_gsb[8 + ot]], [t_m2[b]])
                            fw.op("pool", lambda e, ot=ot, b=b, cs_=cs_: e.tensor_tensor(out=uT_own[:, ot, cs_], in0=m1[b][:], in1=m2[b][:], op=ALU.add),
                                  [t_m1[b], t_m2[b]], [t_uown[tb]])
                        late_mm(2 + tb)
                        if tb < 3:
                            late_load(3 + tb)
                    dve_tt(mod[:, 16:48], ps_mod2[:], vcol(V_BADA + 16, 32), ALU.add, [t_psm2, t_vec], [t_modhi])
                    dve_stt(A12[:, 8:16], mod[:, 32:40], 1.0, vcol(V_N2G, 8), ALU.add, ALU.mult, [t_modhi, t_vec], [t_A2])
                    fw.barrier()
                if debug == "mix":
                    o1 = dbg_out("dbg_mix", [128, 8, OWN])
                    td = T("d1")
                    fw.dma("pool", lambda e: e.dma_start(out=o1, in_=uT_own[:]), td, t_uown[0])
                    return finish([td])

        with ExitStack() as ctxB:
            xres = sb(ctxB, "xres", [128, 8, OWN], F32)
            t_x = [[T("x%d_%d" % (tb, k)) for k in range(8)] for tb in range(4)]
            GS = 6
            wI = [sb(ctxB, "wI%d" % i, [128, 8, 2, GS * 128], BF16) for i in range(2)]
            wOu = [sb(ctxB, "wOu%d" % i, [128, GS, D], BF16) for i in range(2)]
            t_wI = [[T("wIg%d" % i), T("wIu%d" % i)] for i in range(2)]
            t_wOu = [T("wOu%d" % i) for i in range(2)]
            w_fi_v = w_fi.rearrange("(k p) n -> p k n", p=128)
            w_fo_v = w_fo.rearrange("(j p) n -> p j n", p=128)

            def load_group(gi):
                j0, gs = FFN_GROUPS[gi]
                wb = gi % 2
                fw.dma("pool", lambda e: e.dma_start(out=wI[wb][:, :, 0, 0:gs * 128], in_=w_fi_v[:, :, j0 * 128:(j0 + gs) * 128]), t_wI[wb][0])
                fw.dma("pool", lambda e: e.dma_start(out=wI[wb][:, :, 1, 0:gs * 128],
                                                      in_=w_fi_v[:, :, DFF + j0 * 128:DFF + (j0 + gs) * 128]), t_wI[wb][1])
                fw.dma("pool", lambda e: e.dma_start(out=wOu[wb][:, 0:gs, :], in_=w_fo_v[:, j0:j0 + gs, :]), t_wOu[wb])

            with ExitStack() as p5b:
                wO = sb(p5b, "wO", [128, 8, D], BF16)
                t_wO = T("wO")
                fw.dma("pool", lambda e: e.dma_start(out=wO[:], in_=w_o.rearrange("(k p) n -> p k n", p=128)), t_wO)
                load_group(0)
                load_group(1)
                ops_ = [ps(p5b, "ops%d" % i, [128, 512]) for i in range(2)]
                t_ops = [T("ops%d" % i) for i in range(2)]
                nb5 = norm_bufs(p5b, "e")
                for tb in range(4):
                    cs_ = slice(tb * 512, (tb + 1) * 512)
                    fw.dma("sp", lambda e, tb=tb: e.dma_start(out=xres[:, :, tb * 512:(tb + 1) * 512],
                                                              in_=xTv[:, :, tb * 512:(tb + 1) * 512]), t_x[tb][0])
                    for k in range(1, 8):
                        t_x[tb][k].lw = t_x[tb][0].lw
                    for ot in range(8):
                        b = ot % 2
                        osl = slice(ot * 128, (ot + 1) * 128)
                        mm(ops_[b][:], [(wO[:, k, osl], uT_own[:, k, cs_]) for k in range(8)], [t_wO, t_uown[tb]], t_ops[b])
                        dve_stt(xres[:, ot, cs_], ops_[b][:], G1(ot), xres[:, ot, cs_], ALU.mult, ALU.add,
                                [t_ops[b], t_modhi, t_x[tb][ot]], [t_x[tb][ot]])
                    norm_block(nb5, xres[:, :, cs_], t_x[tb], lambda k: A12[:, 8 + k:9 + k], SH2, uT_own[:, :, cs_],
                               lambda k, t=t_uown[tb]: t, [t_A2, t_modhi])
                fw.barrier()
            if debug == "x1":
                o1 = dbg_out("dbg_x1", [128, 8, OWN])
                o2 = dbg_out("dbg_u2", [128, 8, OWN])
                td = [T("d1"), T("d2")]
                fw.dma("pool", lambda e: e.dma_start(out=o1, in_=xres[:]), td[0], t_x[0][0])
                fw.dma("pool", lambda e: e.dma_start(out=o2, in_=uT_own[:]), td[1], t_uown[0])
                return finish(td)

            with ExitStack() as p6:
                hbuf = [sb(p6, "hbuf%d" % i, [128, GS, 512], BF16) for i in range(2)]
                t_h = [[T("h%d_%d" % (i, j)) for j in range(GS)] for i in range(2)]
                sg = [sb(p6, "sg%d" % i, [128, 512], F32) for i in range(2)]
                t_sg = [T("sg%d" % i) for i in range(2)]
                hgp = [ps(p6, "hgp%d" % i, [128, 512]) for i in range(2)]
                t_hgp = [T("hgp%d" % i) for i in range(2)]
                hup = [ps(p6, "hup%d" % i, [128, 512]) for i in range(2)]
                t_hup = [T("hup%d" % i) for i in range(2)]
                yps = [ps(p6, "yps%d" % i, [128, 512]) for i in range(2)]
                t_yps = [T("yps%d" % i) for i in range(2)]
                nb6 = norm_bufs(p6, "f")
                t_out = [T("out%d" % i) for i in range(4)]
                def make_ffn_unit(gi, gs, wb, tb, hb):
                    cs_ = slice(tb * 512, (tb + 1) * 512)

                    def up():
                        for jj in range(gs):
                            b = jj % 2
                            mm(hgp[b][:], [(wI[wb][:, k, 0, jj * 128:(jj + 1) * 128], uT_own[:, k, cs_]) for k in range(8)],
                               [t_wI[wb][0], t_uown[tb]], t_hgp[b])
                            mm(hup[b][:], [(wI[wb][:, k, 1, jj * 128:(jj + 1) * 128], uT_own[:, k, cs_]) for k in range(8)],
                               [t_wI[wb][1], t_uown[tb]], t_hup[b])
                            act(sg[b][:], hgp[b][:], AF.Silu, [t_hgp[b]], [t_sg[b]])
                            dve_tt(hbuf[hb][:, jj, :], sg[b][:], hup[b][:], ALU.mult, [t_sg[b], t_hup[b]], [t_h[hb][jj]])

                    def down():
                        for ot in range(8):
                            b = ot % 2
                            osl = slice(ot * 128, (ot + 1) * 128)
                            mm(yps[b][:], [(wOu[wb][:, jj, osl], hbuf[hb][:, jj, :]) for jj in range(gs)],
                               [t_wOu[wb]] + t_h[hb][0:gs], t_yps[b])
                            dve_stt(xres[:, ot, cs_], yps[b][:], G2(ot), xres[:, ot, cs_], ALU.mult, ALU.add,
                                    [t_yps[b], t_modhi, t_x[tb][ot]], [t_x[tb][ot]])
                        if gi == len(FFN_GROUPS) - 1:
                            norm_block(nb6, xres[:, :, cs_], t_x[tb], lambda k: vcol(V_FNG + k), None, xres[:, :, cs_],
                                       lambda k: t_x[tb][k], [])
                            fw.wait_all("sp", t_x[tb])
                            fw.dma("sp", lambda e: e.dma_start(out=outTv[:, :, tb * 512:(tb + 1) * 512],
                                                              in_=xres[:, :, tb * 512:(tb + 1) * 512]), t_out[tb], t_x[tb][0])
                    return up, down

                funits = []
                cnt = 0
                for gi, (j0, gs) in enumerate(FFN_GROUPS):
                    for tb in range(4):
                        funits.append((gi, tb) + make_ffn_unit(gi, gs, gi % 2, tb, cnt % 2))
                        cnt += 1
                funits[0][2]()
                for ui, (gi, tb, up, down) in enumerate(funits):
                    if ui + 1 < len(funits):
                        funits[ui + 1][2]()
                    down()
                    if tb == 3 and gi + 2 < len(FFN_GROUPS):
                        load_group(gi + 2)
                fw.wait_all("sp", t_out)
                fw.barrier()
        fw.emit()
    return nc, dbg


def _rope_tab(pos, dim, theta):
    inv = (np.float32(theta) ** (-np.arange(0, dim, 2, dtype=np.float32) / np.float32(dim))).astype(np.float32)
    ang = pos.astype(np.float32)[:, None] * inv[None, :]
    return np.cos(ang).astype(np.float32), np.sin(ang).astype(np.float32)


def _const_mats():
    cmat = np.zeros((128, NCM), np.float32)
    cmat[:, C_ONES:C_ONES + 128] = 1.0
    for hb in (0, 64):
        cmat[hb:hb + 64, C_BO + hb:C_BO + hb + 64] = 1.0
    RA = np.zeros((128, 128), np.float32)
    RB = np.zeros((128, 128), np.float32)
    for hb in (0, 64):
        for half in (0, 32):
            for m in range(16):
                RA[hb + half + m + 16, hb + half + m] = -1.0
                RA[hb + half + m, hb + half + m + 16] = 1.0
        for m in range(8):
            RB[hb + m + 8, hb + m] = -1.0
            RB[hb + m, hb + m + 8] = 1.0
    cmat[:, C_RA:C_RA + 128] = RA
    cmat[:, C_RB:C_RB + 128] = RB
    b = np.arange(128)[:, None]
    a = np.arange(128)[None, :]
    cmat[:, C_MASK + 0:C_MASK + 128] = (b >= a + 64)
    cmat[:, C_MASK + 128:C_MASK + 256] = (np.abs(a - b) <= 64)
    cmat[:, C_MASK + 256:C_MASK + 384] = (a >= b + 64)
    return cmat


def _tables(perm):
    t = perm.astype(np.int64)
    cr, sr = _rope_tab(t // 64, 32, 10000.0)
    cc, sc = _rope_tab(t % 64, 32, 10000.0)
    CA = np.concatenate([cr, cr, cc, cc], axis=1).T
    SA = np.concatenate([sr, sr, sc, sc], axis=1).T
    tabA = np.stack([np.concatenate([CA, CA], 0), np.concatenate([SA, SA], 0)], 0).astype(np.float32)
    tb = t[:3072]
    cp, sp_ = _rope_tab(tb, 16, 500000.0)
    CB = np.ones((64, 3072), np.float32)
    SB = np.zeros((64, 3072), np.float32)
    CB[0:8] = cp.T
    CB[8:16] = cp.T
    SB[0:8] = sp_.T
    SB[8:16] = sp_.T
    tabB = np.stack([np.concatenate([CB, CB], 0), np.concatenate([SB, SB], 0)], 0).astype(np.float32)
    return np.ascontiguousarray(tabA), np.ascontiguousarray(tabB)


def _col(v, n):
    return np.asarray(v, np.float32).reshape(n, 128).T


def make_in_maps(inputs):
    f = lambda k: np.asarray(inputs[k], np.float32)
    x, c = f("x"), f("c")
    cmat = _const_mats()
    shared = {
        "cmat": cmat,
        "w_ada": np.ascontiguousarray(f("w_ada")[0]),
        "w_qkv": np.ascontiguousarray(f("w_qkv")[0]),
        "w_pa": np.ascontiguousarray(f("w_proj_a")[0]),
        "w_pb": np.ascontiguousarray(f("w_proj_b")[0]),
        "w_gate": np.ascontiguousarray(f("w_gate")[0]),
        "w_o": np.ascontiguousarray(f("w_o")[0]),
        "w_fi": np.ascontiguousarray(f("w_ffn_in")[0]),
        "w_fo": np.ascontiguousarray(f("w_ffn_out")[0]),
    }
    vecs = np.zeros((128, NV), np.float32)
    vecs[:, V_BADA:V_BADA + 48] = _col(f("b_ada")[0], 48)
    vecs[:, V_N1G:V_N1G + 8] = _col(f("norm1_g")[0], 8)
    vecs[:, V_N2G:V_N2G + 8] = _col(f("norm2_g")[0], 8)
    vecs[:, V_FNG:V_FNG + 8] = _col(f("final_norm_g"), 8)
    vecs[:, V_BG:V_BG + 16] = _col(f("b_gate")[0], 16)
    vecs[:, V_GQ] = np.tile(f("q_norm_a")[0], 2)
    vecs[:, V_GK] = np.tile(f("k_norm_a")[0], 2)
    perms = [np.arange(S), S - 1 - np.arange(S)]
    tabs = [_tables(p) for p in perms]
    in_maps = []
    for core in range(8):
        b, h = core // 2, core % 2
        m = dict(shared)
        m["xT"] = np.ascontiguousarray(x[b][perms[h]].T)
        m["cvec"] = np.ascontiguousarray(_col(c[b], 8))
        m["vecs"] = vecs
        m["tabA"], m["tabB"] = tabs[h]
        in_maps.append(m)
    return in_maps, perms


def kernel(**inputs):
    in_maps, perms = make_in_maps(inputs)
    nc, _ = build()
    res = run_bass_kernel_spmd(nc, in_maps, core_ids=list(range(8)))
    out = np.zeros((4, S, D), np.float32)
    for core in range(8):
        b, h = core // 2, core % 2
        oT = np.asarray(res.results[core]["outT"], np.float32)
        out[b, perms[h][:OWN], :] = oT.T
    return out
```

```python
import numpy as np
from contextlib import ExitStack
import concourse.bass as bass
import concourse.mybir as mybir
from concourse.bass_utils import run_bass_kernel_spmd

F32 = mybir.dt.float32
BF16 = mybir.dt.bfloat16
AF = mybir.ActivationFunctionType
ALU = mybir.AluOpType

SEM_LIMIT = 30000
S = 4096
OWN = 2048
D = 1024
DFF = 2816
EPS = 1e-6
FFN_GROUPS = [(0, 6), (6, 6), (12, 5), (17, 5)]


class T:
    __slots__ = ("name", "lw", "rd", "dsem", "dcnt", "shared")

    def __init__(self, name, shared=False):
        self.name = name
        self.lw = None
        self.rd = []
        self.dsem = None
        self.dcnt = 0
        self.shared = shared


class FW:
    ENG = ("pe", "act", "dve", "pool", "sp")

    def __init__(self, nc, stack):
        self.nc = nc
        self.stack = stack
        self.ops = {e: [] for e in self.ENG}
        self.sem = {}
        self.cnt = {}
        self.known = {e: {} for e in self.ENG}
        self.snap = {}
        self.nsem = 0
        self.pending_out = []
        for e in self.ENG:
            self._newsem(e)
        self.same_engine_sync = {"pe": False, "act": True, "dve": True, "pool": True, "sp": False}

    def _alloc_sem(self, name):
        self.nsem += 1
        return self.stack.enter_context(self.nc.semaphore("%s_%d" % (name, self.nsem)))

    def _newsem(self, e):
        self.sem[e] = self._alloc_sem("s_" + e)
        self.cnt[e] = 0

    def _learn(self, e, s, v):
        kn = self.known[e]
        kn[s] = v
        sn = self.snap.get((id(s), v))
        if sn:
            for s2, v2 in sn.items():
                if kn.get(s2, 0) < v2:
                    kn[s2] = v2

    def _waits(self, e, reads, writes):
        need = {}

        def add(ev):
            if ev is None:
                return
            s, v = ev
            if need.get(s, 0) < v:
                need[s] = v
        for t in reads:
            add(t.lw)
            if not t.shared:
                for ev in t.rd:
                    if ev[0] is not self.sem[e]:
                        add(ev)
        for t in writes:
            add(t.lw)
            for ev in t.rd:
                add(ev)
        out = []
        kn = self.known[e]
        for s, v in need.items():
            if s is self.sem[e] and not self.same_engine_sync[e]:
                continue
            if kn.get(s, 0) >= v:
                continue
            out.append((s, v))
            self._learn(e, s, v)
        return out

    def op(self, e, fn, reads=(), writes=()):
        waits = self._waits(e, reads, writes)
        if self.cnt[e] >= SEM_LIMIT:
            self._newsem(e)
        self.cnt[e] += 1
        s = self.sem[e]
        ev = (s, self.cnt[e])
        sn = dict(self.known[e])
        sn[s] = self.cnt[e]
        self.snap[(id(s), self.cnt[e])] = sn
        for t in reads:
            t.rd.append(ev)
        for t in writes:
            t.lw = ev
            t.rd = []
        self.ops[e].append((waits, fn, (s, 1)))
        return ev

    def dma(self, q, fn, dst, src=None):
        reads = [src] if src is not None else []
        waits = self._waits(q, reads, [dst])
        if dst.dsem is None:
            dst.dsem = self._alloc_sem("d")
        dst.dcnt += 16
        ev = (dst.dsem, dst.dcnt)
        self.snap[(id(dst.dsem), dst.dcnt)] = dict(self.known[q])
        for t in reads:
            t.rd.append(ev)
            self.pending_out.append(ev)
        dst.lw = ev
        dst.rd = []
        self.ops[q].append((waits, fn, (dst.dsem, 16)))
        return ev

    def wait_all(self, e, tiles):
        waits = self._waits(e, tiles, [])
        self.ops[e].append((waits, None, None))

    def barrier(self):
        evs = [(f, self.sem[f], self.cnt[f]) for f in self.ENG if self.cnt[f] > 0]
        for e in self.ENG:
            waits = []
            for f, s, v in evs:
                if f == e and not self.same_engine_sync[e]:
                    continue
                if self.known[e].get(s, 0) < v:
                    waits.append((s, v))
                    self._learn(e, s, v)
            for s, v in self.pending_out:
                if self.known[e].get(s, 0) < v:
                    waits.append((s, v))
                    self._learn(e, s, v)
            self.ops[e].append((waits, None, None))
        self.pending_out = []

    def emit(self):
        nc = self.nc
        with nc.Block() as block:
            def mk(e):
                def body(eng):
                    for waits, fn, inc in self.ops[e]:
                        for s, v in waits:
                            eng.wait_ge(s, v)
                        if fn is None:
                            continue
                        ins = fn(eng)
                        if inc is not None:
                            ins.then_inc(inc[0], inc[1])
                return body
            block.tensor(mk("pe"))
            block.scalar(mk("act"))
            block.vector(mk("dve"))
            block.gpsimd(mk("pool"))
            block.sync(mk("sp"))


NV = 48 + 8 + 8 + 8 + 16 + 2
V_BADA, V_N1G, V_N2G, V_FNG, V_BG, V_GQ, V_GK = 0, 48, 56, 64, 72, 88, 89
NCM = 4 * 128 + 384
C_ONES, C_BO, C_RA, C_RB, C_MASK = 0, 128, 256, 384, 512


def build(debug=None):
    nc = bass.Bass("TRN2", target_bir_lowering=False)

    def din(name, shape, dt=F32):
        return nc.dram_tensor(name, list(shape), dt, kind="ExternalInput").ap()

    xT = din("xT", [D, S])
    cvec = din("cvec", [128, 8])
    vecs = din("vecs", [128, NV])
    cmat = din("cmat", [128, NCM])
    tabA = din("tabA", [2, 128, S])
    tabB = din("tabB", [2, 128, 3072])
    w_ada = din("w_ada", [D, 6 * D])
    w_qkv = din("w_qkv", [D, 3072])
    w_pa = din("w_pa", [512, D])
    w_pb = din("w_pb", [256, D])
    w_gate = din("w_gate", [D, 2 * D])
    w_o = din("w_o", [D, D])
    w_fi = din("w_fi", [D, 2 * DFF])
    w_fo = din("w_fo", [DFF, D])
    outT = nc.dram_tensor("outT", [D, OWN], F32, kind="ExternalOutput").ap()
    dbg = {}

    def dbg_out(name, shape):
        dbg[name] = nc.dram_tensor(name, list(shape), F32, kind="ExternalOutput").ap()
        return dbg[name]

    xTv = xT.rearrange("(k p) t -> p k t", p=128)
    outTv = outT.rearrange("(k p) t -> p k t", p=128)
    w_qkv_v = w_qkv.rearrange("(k p) n -> p k n", p=128)

    with ExitStack() as top:
        fw = FW(nc, top)

        def sb(ctx, name, shape, dt):
            return ctx.enter_context(nc.sbuf_tensor(name, list(shape), dt))

        def ps(ctx, name, shape, dt=F32):
            return ctx.enter_context(nc.psum_tensor(name, list(shape), dt))

        def mm(out_ap, pairs, reads, twrite):
            def fn(e):
                n = len(pairs)
                ins = None
                for i, (l, r) in enumerate(pairs):
                    ins = e.matmul(out_ap, lhsT=l, rhs=r, start=(i == 0), stop=(i == n - 1))
                return ins
            return fw.op("pe", fn, reads, [twrite])

        def mm_multi(groups, reads, writes):
            def fn(e):
                ins = None
                for out_ap, pairs in groups:
                    n = len(pairs)
                    for i, (l, r) in enumerate(pairs):
                        ins = e.matmul(out_ap, lhsT=l, rhs=r, start=(i == 0), stop=(i == n - 1))
                return ins
            return fw.op("pe", fn, reads, writes)

        def act(out, in_, func, reads, writes, **kw):
            return fw.op("act", lambda e: e.activation(out=out, in_=in_, func=func, **kw), reads, writes)

        def dve_tt(out, in0, in1, op, reads, writes, eng="dve"):
            return fw.op(eng, lambda e: e.tensor_tensor(out=out, in0=in0, in1=in1, op=op), reads, writes)

        def dve_stt(out, in0, scalar, in1, op0, op1, reads, writes):
            return fw.op("dve", lambda e: e.scalar_tensor_tensor(out=out, in0=in0, scalar=scalar, in1=in1,
                                                                op0=op0, op1=op1), reads, writes)

        def dve_recip(out, in_, reads, writes):
            return fw.op("dve", lambda e: e.reciprocal(out=out, in_=in_), reads, writes)

        def sl(start, count, step):
            return slice(start, start + step * (count - 1) + 1, step)

        def finish(tds, q="pool"):
            fw.wait_all(q, tds)
            fw.emit()
            return nc, dbg

        vec_sb = sb(top, "vec_sb", [128, NV], F32)
        cm_f = sb(top, "cm_f", [128, NCM], F32)
        cm = sb(top, "cm", [128, NCM], BF16)
        mod = sb(top, "mod", [128, 48], F32)
        A12 = sb(top, "A12", [128, 16], F32)
        uT_own = sb(top, "uT_own", [128, 8, OWN], BF16)
        t_vec, t_cmf, t_cm, t_mod, t_A = T("vec", shared=True), T("cmf"), T("cm", shared=True), T("mod", shared=True), T("A12", shared=True)
        t_modhi, t_A2 = T("mod_hi", shared=True), T("A12_hi", shared=True)
        cb = sb(top, "cb", [128, 8], BF16)
        t_cb = T("cb", shared=True)
        t_uown = [T("uown%d" % i) for i in range(4)]

        ONES = cm[:, C_ONES:C_ONES + 128]
        BO = cm[:, C_BO:C_BO + 128]
        RA = cm[:, C_RA:C_RA + 128]
        RB = cm[:, C_RB:C_RB + 128]
        MASK = cm[:, C_MASK:C_MASK + 384]

        fw.dma("sp", lambda e: e.dma_start(out=vec_sb[:], in_=vecs), t_vec)
        fw.dma("sp", lambda e: e.dma_start(out=cm_f[:], in_=cmat), t_cmf)
        act(cm[:], cm_f[:], AF.Copy, [t_cmf], [t_cm])

        def vcol(c, n=1):
            return vec_sb[:, c:c + n]

        SH1 = lambda k: mod[:, k:k + 1]
        SH2 = lambda k: mod[:, 24 + k:25 + k]
        G1 = lambda k: mod[:, 16 + k:17 + k]
        G2 = lambda k: mod[:, 40 + k:41 + k]

        def norm_bufs(ctx, tag):
            sq = [sb(ctx, "nrm_sq%s%d" % (tag, i), [128, 512], BF16) for i in range(2)]
            ssp = ps(ctx, "nrm_ssp" + tag, [128, 512])
            rs = sb(ctx, "nrm_rs" + tag, [128, 512], F32)
            rstd = sb(ctx, "nrm_rstd" + tag, [128, 512], F32)
            tmp = [sb(ctx, "ntmp%s%d" % (tag, i), [128, 512], F32) for i in range(2)]
            return dict(sq=sq, t_sq=[T("sq0" + tag), T("sq1" + tag)], ssp=ssp, t_ssp=T("ssp" + tag), rs=rs, t_rs=T("rs" + tag),
                        rstd=rstd, t_rstd=T("rstd" + tag), tmp=tmp, t_tmp=[T("tmp0" + tag), T("tmp1" + tag)])

        def norm_block(nb, src, t_src, a_col, sh_col, dst, t_dst, cdeps):
            for k in range(8):
                i = k % 2
                act(nb["sq"][i][:], src[:, k, :], AF.Square, t_src, [nb["t_sq"][i]])
                fw.op("pe", lambda e, k=k, i=i: e.matmul(nb["ssp"][:], lhsT=ONES, rhs=nb["sq"][i][:], start=(k == 0), stop=(k == 7)),
                      [nb["t_sq"][i], t_cm], [nb["t_ssp"]])
            act(nb["rs"][:], nb["ssp"][:], AF.Sqrt, [nb["t_ssp"]], [nb["t_rs"]], scale=1.0 / D, bias=EPS)
            dve_recip(nb["rstd"][:], nb["rs"][:], [nb["t_rs"]], [nb["t_rstd"]])
            for k in range(8):
                i = k % 2
                rd = t_src + [nb["t_rstd"], t_vec] + cdeps
                if sh_col is None:
                    dve_stt(dst[:, k, :], src[:, k, :], a_col(k), nb["rstd"][:], ALU.mult, ALU.mult, rd, [t_dst(k)])
                else:
                    dve_stt(nb["tmp"][i][:], src[:, k, :], a_col(k), nb["rstd"][:], ALU.mult, ALU.mult, rd, [nb["t_tmp"][i]])
                    act(dst[:, k, :], nb["tmp"][i][:], AF.Identity, [nb["t_tmp"][i]] + cdeps, [t_dst(k)], bias=sh_col(k), scale=1.0)

        with ExitStack() as ctxA:
            yaT = sb(ctxA, "yaT", [128, 4, OWN], BF16)
            ybT = sb(ctxA, "ybT", [128, 2, OWN], BF16)
            t_ya = [[T("ya%d_%d" % (h, q)) for q in range(4)] for h in range(8)]
            t_yb = [[T("yb%d_%d" % (h, q)) for q in range(4)] for h in range(4)]

            with ExitStack() as attn:
                uT_elo = sb(attn, "uT_elo", [128, 8, 1024], BF16)
                t_uelo = [T("uelo%d" % i) for i in range(2)]

                with ExitStack() as pha:
                    uT_ehi = sb(pha, "uT_ehi", [128, 8, 1024], BF16)
                    t_uehi = [T("uehi%d" % i) for i in range(2)]
                    wA = sb(pha, "wA", [128, 8, 768], BF16)
                    t_wA = T("wA")

                    def ublk(blk):
                        if blk < 4:
                            return uT_own[:, :, blk * 512:(blk + 1) * 512], t_uown[blk]
                        if blk < 6:
                            return uT_elo[:, :, (blk - 4) * 512:(blk - 3) * 512], t_uelo[blk - 4]
                        return uT_ehi[:, :, (blk - 6) * 512:(blk - 5) * 512], t_uehi[blk - 6]

                    with ExitStack() as p0:
                        cv = sb(p0, "cv", [128, 8], F32)
                        wad = sb(p0, "wad", [128, 8, 2 * D], BF16)
                        ps_mod = ps(p0, "ps_mod", [128, 512])[:, 0:16]
                        t_cv, t_psm = T("cv"), T("psm")
                        t_wad = [T("wad%d" % i) for i in range(2)]
                        fw.dma("sp", lambda e: e.dma_start(out=cv[:], in_=cvec), t_cv)
                        wadv = w_ada.rearrange("(k p) n -> p k n", p=128)
                        for g in (1, 0):
                            fw.dma("pool", lambda e, g=g: e.dma_start(out=wad[:, :, g * D:(g + 1) * D],
                                                                     in_=wadv[:, :, g * D:(g + 1) * D]), t_wad[g])
                        t_wAq = [T("wAq%d" % i) for i in range(8)]
                        for hd in range(8):
                            slot = (hd % 4) * 2 + hd // 4
                            fw.dma("pool", lambda e, hd=hd, slot=slot: e.dma_start(
                                out=wA[:, :, slot * 64:(slot + 1) * 64], in_=w_qkv_v[:, :, hd * 64:(hd + 1) * 64]), t_wAq[hd])
                        fw.dma("pool", lambda e: e.dma_start(out=wA[:, :, 512:768], in_=w_qkv_v[:, :, 512:768]), t_wA)
                        act(cb[:], cv[:], AF.Silu, [t_cv], [t_cb])
                        for g in (1, 0):
                            groups = []
                            for j in range(g * 8, g * 8 + 8):
                                groups.append((ps_mod[:, j:j + 1],
                                               [(wad[:, k, j * 128:(j + 1) * 128], cb[:, k:k + 1]) for k in range(8)]))
                            mm_multi(groups, [t_cb, t_wad[g]], [t_psm])
                        dve_tt(mod[:, 0:16], ps_mod[:], vcol(V_BADA, 16), ALU.add, [t_psm, t_vec], [t_mod])
                        dve_stt(A12[:, 0:8], mod[:, 8:16], 1.0, vcol(V_N1G, 8), ALU.add, ALU.mult, [t_mod, t_vec], [t_A])
                        fw.barrier()
                    if debug == "mod":
                        o = dbg_out("dbg_mod", [128, 64])
                        td, td2 = T("dbgmod"), T("dbgA")
                        fw.dma("sp", lambda e: e.dma_start(out=o[:, 0:16], in_=mod[:, 0:16]), td, t_mod)
                        fw.dma("sp", lambda e: e.dma_start(out=o[:, 48:56], in_=A12[:, 0:8]), td2, t_A)
                        return finish([td, td2], "sp")

                    with ExitStack() as p1:
                        xb = [sb(p1, "xb%d" % i, [128, 8, 512], F32) for i in range(2)]
                        t_xb = [T("xb%d" % i) for i in range(2)]
                        nb = [norm_bufs(p1, "a"), norm_bufs(p1, "b")]
                        for blk in range(8):
                            i = blk % 2
                            fw.dma("sp", lambda e, blk=blk, i=i: e.dma_start(out=xb[i][:], in_=xTv[:, :, blk * 512:(blk + 1) * 512]), t_xb[i])
                            dst, t_dst = ublk(blk)
                            norm_block(nb[i], xb[i][:], [t_xb[i]], lambda k: A12[:, k:k + 1], SH1, dst, lambda k, t=t_dst: t, [t_A, t_mod])
                        fw.barrier()
                    if debug == "u":
                        o = dbg_out("dbg_u", [128, 8, S])
                        tds = [T("dbgu%d" % i) for i in range(8)]
                        for blk in range(8):
                            src, t_s = ublk(blk)
                            fw.dma("pool", lambda e, blk=blk, src=src: e.dma_start(out=o[:, :, blk * 512:(blk + 1) * 512], in_=src), tds[blk], t_s)
                        return finish(tds)

                    with ExitStack() as p23:
                        QaT = sb(p23, "QaT", [128, 4, OWN], BF16)
                        KaT = sb(p23, "KaT", [128, S], BF16)
                        Va = sb(p23, "Va", [128, 32, 3, 64], BF16)
                        t_Qa = [[T("Qa%d_%d" % (i, q)) for q in range(4)] for i in range(4)]
                        t_Ka = [T("Ka%d" % q) for q in range(8)]
                        t_Va = [T("Va%d" % q) for q in range(8)]
                        t_Vones = T("Vones")
                        fw.op("pool", lambda e: e.memset(Va[:, :, 1, :], 1.0), [], [t_Vones])
                        with ExitStack() as p2:
                            tab = [sb(p2, "tabA%d" % i, [128, 2, 512], F32) for i in range(2)]
                            t_tab = [T("tabA%d" % i) for i in range(2)]

                            def two(name, shape, dt, psum=False):
                                if psum:
                                    return [ps(p2, "%s%d" % (name, i), shape) for i in range(2)], [T("%s%d" % (name, i)) for i in range(2)]
                                return [sb(p2, "%s%d" % (name, i), shape, dt) for i in range(2)], [T("%s%d" % (name, i)) for i in range(2)]
                            qps, t_qps = two("qps", [128, 512], F32, True)
                            ssq, t_ssq = two("ssq", [128, 512], F32, True)
                            rqp, t_rqp = two("rqp", [128, 512], F32, True)
                            vps, t_vps = two("vps", [128, 4, 128], F32, True)
                            sqb, t_sqb = two("sqb", [128, 512], BF16)
                            qg, t_qg = two("qg", [128, 512], BF16)
                            rsq, t_rsq = two("rsq", [128, 512], F32)
                            rq, t_rq = two("rq", [128, 512], F32)
                            t1, t_t1 = two("t1_", [128, 512], F32)
                            t2, t_t2 = two("t2_", [128, 512], F32)
                            cnt = 0
                            for blk in range(8):
                                ti = blk % 2
                                fw.dma("sp", lambda e, blk=blk, ti=ti: e.dma_start(
                                    out=tab[ti][:], in_=tabA[:, :, blk * 512:(blk + 1) * 512].rearrange("c p t -> p c t")), t_tab[ti])
                                u_ap, t_u = ublk(blk)
                                tiles = ([("q", i) for i in range(4)] if blk < 4 else []) + [("k", 0)]
                                for kind, i in tiles:
                                    b = cnt % 2
                                    cnt += 1
                                    if kind == "q":
                                        def lw(k, i=i):
                                            return wA[:, k, i * 128:(i + 1) * 128]
                                        gain = vcol(V_GQ)
                                        dst = QaT[:, i, blk * 512:(blk + 1) * 512]
                                        t_dst = t_Qa[i][blk]
                                    else:
                                        def lw(k):
                                            return wA[:, k, 512:640]
                                        gain = vcol(V_GK)
                                        dst = KaT[:, blk * 512:(blk + 1) * 512]
                                        t_dst = t_Ka[blk]
                                    mm(qps[b][:], [(lw(k), u_ap[:, k, :]) for k in range(8)], [t_wA, t_u] + t_wAq, t_qps[b])
                                    act(sqb[b][:], qps[b][:], AF.Square, [t_qps[b]], [t_sqb[b]])
                                    act(qg[b][:], qps[b][:], AF.Copy, [t_qps[b], t_vec], [t_qg[b]], scale=gain)
                                    mm(ssq[b][:], [(BO, sqb[b][:])], [t_sqb[b], t_cm], t_ssq[b])
                                    mm(rqp[b][:], [(RA, qg[b][:])], [t_qg[b], t_cm], t_rqp[b])
                                    act(rsq[b][:], ssq[b][:], AF.Sqrt, [t_ssq[b]], [t_rsq[b]], scale=1.0 / 64, bias=EPS)
                                    dve_recip(rq[b][:], rsq[b][:], [t_rsq[b]], [t_rq[b]])
                                    dve_stt(t1[b][:], qps[b][:], gain, tab[ti][:, 0, :], ALU.mult, ALU.mult,
                                            [t_qps[b], t_tab[ti], t_vec], [t_t1[b]])
                                    dve_tt(t2[b][:], rqp[b][:], tab[ti][:, 1, :], ALU.mult, [t_rqp[b], t_tab[ti]], [t_t2[b]])
                                    dve_tt(t1[b][:], t1[b][:], t2[b][:], ALU.add, [t_t1[b], t_t2[b]], [t_t1[b]], eng="pool")
                                    dve_tt(dst, t1[b][:], rq[b][:], ALU.mult, [t_t1[b], t_rq[b]], [t_dst], eng="pool")
                                vb = blk % 2
                                groups = []
                                for j in range(4):
                                    groups.append((vps[vb][:, j, :],
                                                   [(u_ap[:, k, j * 128:(j + 1) * 128], wA[:, k, 640:768]) for k in range(8)]))
                                mm_multi(groups, [t_wA, t_u], [t_vps[vb]])
                                for v in range(2):
                                    act(Va[:, blk * 4:blk * 4 + 4, 2 * v, :], vps[vb][:, :, v * 64:(v + 1) * 64], AF.Copy,
                                        [t_vps[vb], t_Vones], [t_Va[blk]])
                            fw.barrier()
                        if debug == "qkv_a":
                            o1 = dbg_out("dbg_Qa", [128, 4, OWN])
                            o2 = dbg_out("dbg_Ka", [128, S])
                            o3 = dbg_out("dbg_Va", [128, 32 * 3 * 64])
                            td = [T("d1"), T("d2"), T("d3")]
                            fw.dma("pool", lambda e: e.dma_start(out=o1, in_=QaT[:]), td[0], t_Qa[0][0])
                            fw.dma("pool", lambda e: e.dma_start(out=o2, in_=KaT[:]), td[1], t_Ka[0])
                            fw.dma("pool", lambda e: e.dma_start(out=o3, in_=Va[:].rearrange("p a b c -> p (a b c)")), td[2], t_Va[0])
                            return finish(td)

                        with ExitStack() as p3:
                            NS = 3
                            Sps = [ps(p3, "Sps%d" % i, [128, 2, 512]) for i in range(NS)]
                            t_S = [T("Sps%d" % i) for i in range(NS)]
                            Ops = ps(p3, "Ops0", [128, 2, 512])
                            t_O = T("Ops0")
                            Osb = [sb(p3, "Osb%d" % i, [128, 2, 512], F32) for i in range(2)]
                            t_Osb = [T("Osb%d" % i) for i in range(2)]
                            NPT = 3
                            PT = [sb(p3, "PT%d" % i, [128, 2, 512], BF16) for i in range(NPT)]
                            t_PT = [T("PT%d" % i) for i in range(NPT)]
                            rz = [sb(p3, "rz%d" % i, [128, 2, 512], F32) for i in range(2)]
                            t_rz = [T("rz%d" % i) for i in range(2)]
                            npair = 1 if debug == "ya1" else 4
                            steps = [(i, qb, kt) for i in range(npair) for qb in range(4) for kt in range(32)]
                            Vav = [Va[:, kt, :, :].rearrange("p a b -> p (a b)") for kt in range(32)]

                            def qk(sidx):
                                i, qb, kt = steps[sidx]
                                b = sidx % NS
                                qs = slice(qb * 512, (qb + 1) * 512)
                                ks = slice(kt * 128, (kt + 1) * 128)
                                groups = [(Sps[b][:, 0, :], [(KaT[0:64, ks], QaT[0:64, i, qs])]),
                                          (Sps[b][:, 1, :], [(KaT[64:128, ks], QaT[64:128, i, qs])])]
                                mm_multi(groups, [t_Ka[kt // 4], t_Qa[i][qb]], [t_S[b]])

                            def ex(sidx):
                                b = sidx % NS
                                p = sidx % NPT
                                act(PT[p][:], Sps[b][:], AF.Exp, [t_S[b]], [t_PT[p]], scale=0.125)

                            def pv(sidx):
                                i, qb, kt = steps[sidx]
                                p = sidx % NPT
                                o2 = (sidx // 32) % 2

                                def fn(e):
                                    ins = None
                                    for j in range(2):
                                        lhs = Vav[kt][:, 64 * j:64 * j + 128]
                                        ins = e.matmul(Ops[:, j, :], lhsT=lhs, rhs=PT[p][:, j, :],
                                                       start=(kt == 0), stop=(kt == 31))
                                    return ins
                                fw.op("pe", fn, [t_Va[kt // 4], t_PT[p]], [t_O])
                                if kt == 31:
                                    fw.op("dve", lambda e: e.tensor_copy(out=Osb[o2][:], in_=Ops[:]), [t_O], [t_Osb[o2]])
                                    for j in range(2):
                                        head = i + 4 * j
                                        orow = slice(64 * j, 64 * j + 64)
                                        zrow = slice(64 - 64 * j, 128 - 64 * j)
                                        dve_recip(rz[o2][orow, j, :], Osb[o2][zrow, j, :], [t_Osb[o2]], [t_rz[o2]])
                                        pb_ = 64 * (head % 2)
                                        dve_tt(yaT[pb_:pb_ + 64, head // 2, qb * 512:(qb + 1) * 512], Osb[o2][orow, j, :],
                                               rz[o2][orow, j, :], ALU.mult, [t_Osb[o2], t_rz[o2]], [t_ya[head][qb]])

                            for sidx in range(min(NS, len(steps))):
                                qk(sidx)
                            for sidx in range(len(steps)):
                                ex(sidx)
                                pv(sidx)
                                if sidx + NS < len(steps):
                                    qk(sidx + NS)
                            fw.barrier()
                    if debug in ("ya", "ya1"):
                        o1 = dbg_out("dbg_ya", [128, 4, OWN])
                        td = T("d1")
                        fw.dma("pool", lambda e: e.dma_start(out=o1, in_=yaT[:]), td, t_ya[0][0])
                        return finish([td])

                acc = sb(attn, "acc", [128, 4, OWN], F32)
                rzb = sb(attn, "rzb", [128, 512], F32)
                with ExitStack() as p4:
                    wB = sb(p4, "wB", [128, 8, 768], BF16)
                    t_wB = [T("wBq"), T("wBk"), T("wBv")]
                    QbT = sb(p4, "QbT", [128, 2, OWN], BF16)
                    KbT = sb(p4, "KbT", [128, 2, 4096], BF16)
                    Vb = sb(p4, "Vb", [128, 32, 6, 64], BF16)
                    t_Qb = [T("Qb%d" % i) for i in range(2)]
                    t_Kb = [T("Kb%d" % i) for i in range(2)]
                    t_Vb = T("Vb")
                    t_acc = [T("acc%d" % i) for i in range(2)]
                    tabb = [sb(p4, "tabB%d" % i, [128, 2, 512], F32) for i in range(2)]
                    t_tabb = [T("tabB%d" % i) for i in range(2)]
                    pps = [ps(p4, "pps%d" % i, [128, 512]) for i in range(2)]
                    t_pps = [T("pps%d" % i) for i in range(2)]
                    rps = [ps(p4, "rps%d" % i, [128, 512]) for i in range(2)]
                    t_rps = [T("rps%d" % i) for i in range(2)]
                    OB_ = [ps(p4, "OB%d" % i, [128, 2, 256]) for i in range(2)]
                    t_OB = [T("OB%d" % i) for i in range(2)]
                    Sbuf = [(pps, t_pps), (rps, t_rps)]
                    pbb = [sb(p4, "pbb%d" % i, [128, 512], BF16) for i in range(2)]
                    t_pbb = [T("pbb%d" % i) for i in range(2)]
                    b1 = [sb(p4, "b1_%d" % i, [128, 512], F32) for i in range(2)]
                    t_b1 = [T("b1_%d" % i) for i in range(2)]
                    b2 = [sb(p4, "b2_%d" % i, [128, 512], F32) for i in range(2)]
                    t_b2 = [T("b2_%d" % i) for i in range(2)]
                    EB = [sb(p4, "EB%d" % i, [128, 2, 384], BF16) for i in range(2)]
                    t_EB = [T("EB%d" % i) for i in range(2)]
                    PB = [sb(p4, "PB%d" % i, [128, 2, 384], BF16) for i in range(2)]
                    t_PB = [T("PB%d" % i) for i in range(2)]
                    t_rzb = T("rzb")
                    def p4_exit():
                        fw.barrier()
                        o1 = dbg_out("dbg_ya", [128, 4, OWN])
                        td = T("d1")
                        fw.dma("pool", lambda e: e.dma_start(out=o1, in_=yaT[:]), td, t_ya[0][0])
                        return finish([td])
                    if debug == "p4_a":
                        return p4_exit()
                    cnt = 0
                    ucnt = 0
                    glist = [(0, 1), (1, 4), (2, 16)]
                    if debug == "yb0":
                        glist = glist[:1]
                    for g, dil in glist:
                        nq = 16 // dil
                        segw = (nq + 1) * 128
                        for part in range(3):
                            c0 = 768 + part * 768 + g * 256
                            fw.dma("pool", lambda e, part=part, c0=c0: e.dma_start(
                                out=wB[:, :, part * 256:(part + 1) * 256], in_=w_qkv_v[:, :, c0:c0 + 256]), t_wB[part])
                        if g == 0:
                            fw.op("pool", lambda e: e.memset(KbT[:].rearrange("p f (a b) -> p (f a) b", b=128), 0.0), [], t_Kb)
                            fw.op("pool", lambda e: e.memset(Vb[:], 0.0), [], [t_Vb])
                            fw.op("pool", lambda e: e.memset(Vb[:, :, 1:5:3, :], 1.0), [], [t_Vb])
                        Kview = [KbT[:, ft, 0:dil * segw].rearrange("p (s c) -> p c s", s=dil) for ft in range(2)]
                        Qview = [QbT[:, ft, :].rearrange("p (s c) -> p c s", s=dil) for ft in range(2)]
                        nhalo = 64 * dil
                        blocks = [("own", tb, 512) for tb in range(4)] + \
                                 [("halo", hb, min(512, nhalo)) for hb in range((nhalo + 511) // 512)]
                        if debug == "yb_pown":
                            blocks = [bk for bk in blocks if bk[0] == "own"]
                        def make_tile(kind, bi, blen, what, ft, u_ap, t_u, ti, b, dil, nq, Qview, Kview):
                            pi = 0 if what == "q" else 1
                            wcol = pi * 256 + ft * 128

                            def proj():
                                mm(pps[b][:, 0:blen], [(wB[:, k, wcol:wcol + 128], u_ap[:, k, :]) for k in range(8)],
                                   [t_wB[pi], t_u], t_pps[b])

                            def rest():
                                act(pbb[b][:, 0:blen], pps[b][:, 0:blen], AF.Copy, [t_pps[b]], [t_pbb[b]])
                                mm(rps[b][:, 0:blen], [(RB, pbb[b][:, 0:blen])], [t_pbb[b], t_cm], t_rps[b])
                                dve_tt(b1[b][:, 0:blen], pps[b][:, 0:blen], tabb[ti][:, 0, 0:blen], ALU.mult,
                                       [t_pps[b], t_tabb[ti]], [t_b1[b]])
                                dve_tt(b2[b][:, 0:blen], rps[b][:, 0:blen], tabb[ti][:, 1, 0:blen], ALU.mult,
                                       [t_rps[b], t_tabb[ti]], [t_b2[b]])
                                j0 = (bi * 512) // dil
                                nj = blen // dil
                                if what == "q":
                                    dst = Qview[ft][:, j0:j0 + nj, :]
                                    t_dst = t_Qb[ft]
                                else:
                                    cbase = j0 if kind == "own" else nq * 128 + j0
                                    dst = Kview[ft][:, cbase:cbase + nj, :]
                                    t_dst = t_Kb[ft]
                                in0 = b1[b][:, 0:blen].rearrange("p (j r) -> p j r", r=dil)
                                in1 = b2[b][:, 0:blen].rearrange("p (j r) -> p j r", r=dil)
                                fw.op("pool", lambda e: e.tensor_tensor(out=dst, in0=in0, in1=in1, op=ALU.add),
                                      [t_b1[b], t_b2[b]], [t_dst])
                            return proj, rest

                        def make_tab_dma(ti, tcol, blen):
                            def go():
                                fw.dma("sp", lambda e: e.dma_start(
                                    out=tabb[ti][:, :, 0:blen], in_=tabB[:, :, tcol:tcol + blen].rearrange("c p t -> p c t")), t_tabb[ti])
                            return go

                        psteps = []
                        for kind, bi, blen in blocks:
                            if kind == "own":
                                u_ap, t_u = uT_own[:, :, bi * 512:(bi + 1) * 512], t_uown[bi]
                                tcol = bi * 512
                            else:
                                u_ap, t_u = uT_elo[:, :, bi * 512:bi * 512 + blen], t_uelo[bi]
                                tcol = OWN + bi * 512
                            ti = cnt % 2
                            psteps.append(("dma", make_tab_dma(ti, tcol, blen)))
                            tiles = ([("q", 0), ("q", 1)] if kind == "own" else []) + [("k", 0), ("k", 1)]
                            for what, ft in tiles:
                                psteps.append(("tile",) + make_tile(kind, bi, blen, what, ft, u_ap, t_u, ti, cnt % 2, dil, nq, Qview, Kview))
                                cnt += 1
                        tile_pos = [i for i, s in enumerate(psteps) if s[0] == "tile"]
                        done_proj = set()
                        for pos, s in enumerate(psteps):
                            if s[0] == "dma":
                                s[1]()
                                continue
                            if pos not in done_proj:
                                s[1]()
                                done_proj.add(pos)
                            nxt = next((p for p in tile_pos if p > pos), None)
                            if nxt is not None and nxt not in done_proj:
                                psteps[nxt][1]()
                                done_proj.add(nxt)
                            s[2]()
                        if debug in ("yb_p", "yb_pown"):
                            fw.barrier()
                            o1 = dbg_out("dbg_Qb", [128, 2, OWN])
                            o2 = dbg_out("dbg_Kb", [128, 2, 4096])
                            td = [T("d1"), T("d2")]
                            fw.dma("pool", lambda e: e.dma_start(out=o1, in_=QbT[:]), td[0], t_Qb[0])
                            fw.dma("pool", lambda e: e.dma_start(out=o2, in_=KbT[:]), td[1], t_Kb[0])
                            return finish(td)
                        vt = 0
                        for seg in range(dil):
                            for m in range(nq + 1):
                                tile = seg * (nq + 1) + m
                                ob = vt % 2
                                vt += 1
                                if m < nq:
                                    st0 = seg + dil * 128 * m
                                    lhs = lambda k, st0=st0, dil=dil: uT_own[:, k, sl(st0, 128, dil)]
                                    t_u = t_uown
                                    mrows = 128
                                else:
                                    lhs = lambda k, seg=seg, dil=dil: uT_elo[:, k, sl(seg, 64, dil)]
                                    t_u = t_uelo
                                    mrows = 64
                                mm(OB_[ob][0:mrows, 0, :], [(lhs(k), wB[:, k, 512:768]) for k in range(8)], [t_wB[2]] + t_u, t_OB[ob])
                                for hp2 in range(2):
                                    act(Vb[0:mrows, tile, 3 * hp2:3 * hp2 + 3:2, :],
                                        OB_[ob][0:mrows, 0, hp2 * 128:(hp2 + 1) * 128].rearrange("p (h d) -> p h d", h=2),
                                        AF.Copy, [t_OB[ob]], [t_Vb])
                        if debug == "yb_v":
                            fw.barrier()
                            o1 = dbg_out("dbg_Vb", [128, 32 * 6 * 64])
                            td = [T("d1")]
                            fw.dma("pool", lambda e: e.dma_start(out=o1, in_=Vb[:].rearrange("p a b c -> p (a b c)")), td[0], t_Vb)
                            return finish(td)
                        def make_unit(g, dil, nq, segw, seg, n, hp, u):
                            St, t_St = Sbuf[u]
                            cs = [c for c in range(3) if n - 1 + c >= 0]
                            c_lo = cs[0] * 128
                            qcol = seg * nq * 128 + n * 128

                            def qk():
                                groups = []
                                for hh in range(2):
                                    pr = slice(64 * hh, 64 * hh + 64)
                                    for c in cs:
                                        m = n - 1 + c
                                        kc = seg * segw + m * 128
                                        groups.append((St[hh][:, c * 128:(c + 1) * 128],
                                                       [(KbT[pr, hp, kc:kc + 128], QbT[pr, hp, qcol:qcol + 128])]))
                                mm_multi(groups, [t_Kb[hp], t_Qb[hp]], t_St)

                            def mid():
                                for hh in range(2):
                                    act(EB[u][:, hh, c_lo:384], St[hh][:, c_lo:384], AF.Exp, [t_St[hh]], [t_EB[u]], scale=0.125)
                                for hh in range(2):
                                    dve_tt(PB[u][:, hh, c_lo:384], EB[u][:, hh, c_lo:384], MASK[:, c_lo:384], ALU.mult,
                                           [t_EB[u], t_cm], [t_PB[u]])

                            def pv():
                                def fn(e):
                                    ins = None
                                    for hh in range(2):
                                        head = 2 * hp + hh
                                        for ci, c in enumerate(cs):
                                            m = n - 1 + c
                                            tile = seg * (nq + 1) + m
                                            kr = 64 if m == nq else 128
                                            sblk = (0, 1, 3, 4)[head]
                                            lhs = Vb[0:kr, tile, sblk:sblk + 2, :].rearrange("p a b -> p (a b)")
                                            ins = e.matmul(OB_[u][:, hh, 0:128], lhsT=lhs,
                                                           rhs=PB[u][0:kr, hh, c * 128:(c + 1) * 128],
                                                           start=(ci == 0), stop=(ci == len(cs) - 1))
                                    return ins
                                fw.op("pe", fn, [t_Vb, t_PB[u]], [t_OB[u]])

                            def accum():
                                st0 = seg + dil * 128 * n
                                av = acc[:, 2 * hp:2 * hp + 2, sl(st0, 128, dil)]
                                if g == 0:
                                    act(av, OB_[u][:, :, 0:128], AF.Copy, [t_OB[u]], [t_acc[hp]])
                                else:
                                    dve_tt(av, av, OB_[u][:, :, 0:128], ALU.add, [t_OB[u], t_acc[hp]], [t_acc[hp]])
                            return qk, mid, pv, accum

                        units = []
                        for seg in range(dil):
                            for n in range(nq):
                                for hp in range(2):
                                    units.append(make_unit(g, dil, nq, segw, seg, n, hp, ucnt % 2))
                                    ucnt += 1
                        units[0][0]()
                        for ui in range(len(units)):
                            units[ui][1]()
                            if ui + 1 < len(units):
                                units[ui + 1][0]()
                            units[ui][2]()
                            if ui >= 1:
                                units[ui - 1][3]()
                        units[len(units) - 1][3]()
                    fw.barrier()

                with ExitStack() as p5:
                    wG = sb(p5, "wG", [128, 8, 2 * D], BF16)
                    wPA = sb(p5, "wPA", [128, 4, D], BF16)
                    wPB = sb(p5, "wPB", [128, 2, D], BF16)
                    t_wG = [T("wG0"), T("wG1")]
                    t_wPA, t_wPB = T("wPA"), T("wPB")
                    w_gate_v = w_gate.rearrange("(k p) n -> p k n", p=128)
                    for i in range(2):
                        fw.dma("pool", lambda e, i=i: e.dma_start(out=wG[:, :, i * D:(i + 1) * D], in_=w_gate_v[:, :, i * D:(i + 1) * D]), t_wG[i])
                    fw.dma("pool", lambda e: e.dma_start(out=wPA[:], in_=w_pa.rearrange("(k p) n -> p k n", p=128)), t_wPA)
                    fw.dma("pool", lambda e: e.dma_start(out=wPB[:], in_=w_pb.rearrange("(k p) n -> p k n", p=128)), t_wPB)
                    gsb = sb(p5, "gsb", [128, 16, 512], BF16)
                    t_gsb = [T("gsb%d" % i) for i in range(16)]
                    m1 = [sb(p5, "m1_%d" % i, [128, 512], F32) for i in range(2)]
                    t_m1 = [T("m1_%d" % i) for i in range(2)]
                    m2 = [sb(p5, "m2_%d" % i, [128, 512], F32) for i in range(2)]
                    t_m2 = [T("m2_%d" % i) for i in range(2)]
                    gps = [ps(p5, "gps%d" % i, [128, 512]) for i in range(2)]
                    t_gps = [T("gps%d" % i) for i in range(2)]
                    pap = [ps(p5, "pap%d" % i, [128, 512]) for i in range(2)]
                    t_pap = [T("pap%d" % i) for i in range(2)]
                    pbp = [ps(p5, "pbp%d" % i, [128, 512]) for i in range(2)]
                    t_pbp = [T("pbp%d" % i) for i in range(2)]
                    wad2 = sb(p5, "wad2", [128, 8, D], BF16)
                    t_wad2 = T("wad2")
                    ps_mod2 = ps(p5, "ps_mod2", [128, 512])[:, 0:32]
                    t_psm2 = T("psm2")
                    wadv2 = w_ada.rearrange("(k p) n -> p k n", p=128)

                    def late_load(g):
                        fw.dma("pool", lambda e: e.dma_start(out=wad2[:], in_=wadv2[:, :, g * D:(g + 1) * D]), t_wad2)

                    def late_mm(g):
                        groups = []
                        for j in range(8):
                            col = (g - 2) * 8 + j
                            groups.append((ps_mod2[:, col:col + 1],
                                           [(wad2[:, k, j * 128:(j + 1) * 128], cb[:, k:k + 1]) for k in range(8)]))
                        mm_multi(groups, [t_cb, t_wad2], [t_psm2])
                    for tb in range(4):
                        for h in range(4):
                            cs_ = slice(tb * 512, (tb + 1) * 512)
                            orow = slice(64 * (h % 2), 64 * (h % 2) + 64)
                            zrow = slice(64 - 64 * (h % 2), 128 - 64 * (h % 2))
                            dve_recip(rzb[orow, :], acc[zrow, h, cs_], [t_acc[h // 2]], [t_rzb])
                            dve_tt(ybT[orow, h // 2, cs_], acc[orow, h, cs_], rzb[orow, :], ALU.mult,
                                   [t_acc[h // 2], t_rzb], [t_yb[h][tb]])
                    if debug in ("yb", "yb0"):
                        fw.barrier()
                        o1 = dbg_out("dbg_yb", [128, 2, OWN])
                        td = T("d1")
                        fw.dma("pool", lambda e: e.dma_start(out=o1, in_=ybT[:]), td, t_yb[0][0])
                        return finish([td])
                    late_load(2)
                    for tb in range(4):
                        cs_ = slice(tb * 512, (tb + 1) * 512)
                        for gt in range(16):
                            b = gt % 2
                            mm(gps[b][:], [(wG[:, k, gt * 128:(gt + 1) * 128], uT_own[:, k, cs_]) for k in range(8)],
                               [t_wG[gt // 8], t_uown[tb]], t_gps[b])
                            act(gsb[:, gt, :], gps[b][:], AF.Sigmoid, [t_gps[b], t_vec], [t_gsb[gt]],
                                bias=vcol(V_BG + gt), scale=1.0)
                        for ot in range(8):
                            b = ot % 2
                            osl = slice(ot * 128, (ot + 1) * 128)
                            mm(pap[b][:], [(wPA[:, kc, osl], yaT[:, kc, cs_]) for kc in range(4)],
                               [t_wPA] + [t_ya[h][tb] for h in range(8)], t_pap[b])
                            mm(pbp[b][:], [(wPB[:, kc, osl], ybT[:, kc, cs_]) for kc in range(2)],
                               [t_wPB] + [t_yb[h][tb] for h in range(4)], t_pbp[b])
                            dve_tt(m1[b][:], pap[b][:], gsb[:, ot, :], ALU.mult, [t_pap[b], t_gsb[ot]], [t_m1[b]])
                            dve_tt(m2[b][:], pbp[b][:], gsb[:, 8 + ot, :], ALU.mult, [t_pbp[b], t_gsb[8 + ot]], [t_m2[b]])
                            fw.op("pool", lambda e, ot=ot, b=b, cs_=cs_: e.tensor_tensor(out=uT_own[:, ot, cs_], in0=m1[b][:], in1=m2[b][:], op=ALU.add),
                                  [t_m1[b], t_m2[b]], [t_uown[tb]])
                        late_mm(2 + tb)
                        if tb < 3:
                            late_load(3 + tb)
                    dve_tt(mod[:, 16:48], ps_mod2[:], vcol(V_BADA + 16, 32), ALU.add, [t_psm2, t_vec], [t_modhi])
                    dve_stt(A12[:, 8:16], mod[:, 32:40], 1.0, vcol(V_N2G, 8), ALU.add, ALU.mult, [t_modhi, t_vec], [t_A2])
                    fw.barrier()
                if debug == "mix":
                    o1 = dbg_out("dbg_mix", [128, 8, OWN])
                    td = T("d1")
                    fw.dma("pool", lambda e: e.dma_start(out=o1, in_=uT_own[:]), td, t_uown[0])
                    return finish([td])

        with ExitStack() as ctxB:
            xres = sb(ctxB, "xres", [128, 8, OWN], F32)
            t_x = [[T("x%d_%d" % (tb, k)) for k in range(8)] for tb in range(4)]
            GS = 6
            wI = [sb(ctxB, "wI%d" % i, [128, 8, 2, GS * 128], BF16) for i in range(2)]
            wOu = [sb(ctxB, "wOu%d" % i, [128, GS, D], BF16) for i in range(2)]
            t_wI = [[T("wIg%d" % i), T("wIu%d" % i)] for i in range(2)]
            t_wOu = [T("wOu%d" % i) for i in range(2)]
            w_fi_v = w_fi.rearrange("(k p) n -> p k n", p=128)
            w_fo_v = w_fo.rearrange("(j p) n -> p j n", p=128)

            def load_group(gi):
                j0, gs = FFN_GROUPS[gi]
                wb = gi % 2
                fw.dma("pool", lambda e: e.dma_start(out=wI[wb][:, :, 0, 0:gs * 128], in_=w_fi_v[:, :, j0 * 128:(j0 + gs) * 128]), t_wI[wb][0])
                fw.dma("pool", lambda e: e.dma_start(out=wI[wb][:, :, 1, 0:gs * 128],
                                                      in_=w_fi_v[:, :, DFF + j0 * 128:DFF + (j0 + gs) * 128]), t_wI[wb][1])
                fw.dma("pool", lambda e: e.dma_start(out=wOu[wb][:, 0:gs, :], in_=w_fo_v[:, j0:j0 + gs, :]), t_wOu[wb])

            with ExitStack() as p5b:
                wO = sb(p5b, "wO", [128, 8, D], BF16)
                t_wO = T("wO")
                fw.dma("pool", lambda e: e.dma_start(out=wO[:], in_=w_o.rearrange("(k p) n -> p k n", p=128)), t_wO)
                load_group(0)
                load_group(1)
                ops_ = [ps(p5b, "ops%d" % i, [128, 512]) for i in range(2)]
                t_ops = [T("ops%d" % i) for i in range(2)]
                nb5 = norm_bufs(p5b, "e")
                for tb in range(4):
                    cs_ = slice(tb * 512, (tb + 1) * 512)
                    fw.dma("sp", lambda e, tb=tb: e.dma_start(out=xres[:, :, tb * 512:(tb + 1) * 512],
                                                              in_=xTv[:, :, tb * 512:(tb + 1) * 512]), t_x[tb][0])
                    for k in range(1, 8):
                        t_x[tb][k].lw = t_x[tb][0].lw
                    for ot in range(8):
                        b = ot % 2
                        osl = slice(ot * 128, (ot + 1) * 128)
                        mm(ops_[b][:], [(wO[:, k, osl], uT_own[:, k, cs_]) for k in range(8)], [t_wO, t_uown[tb]], t_ops[b])
                        dve_stt(xres[:, ot, cs_], ops_[b][:], G1(ot), xres[:, ot, cs_], ALU.mult, ALU.add,
                                [t_ops[b], t_modhi, t_x[tb][ot]], [t_x[tb][ot]])
                    norm_block(nb5, xres[:, :, cs_], t_x[tb], lambda k: A12[:, 8 + k:9 + k], SH2, uT_own[:, :, cs_],
                               lambda k, t=t_uown[tb]: t, [t_A2, t_modhi])
                fw.barrier()
            if debug == "x1":
                o1 = dbg_out("dbg_x1", [128, 8, OWN])
                o2 = dbg_out("dbg_u2", [128, 8, OWN])
                td = [T("d1"), T("d2")]
                fw.dma("pool", lambda e: e.dma_start(out=o1, in_=xres[:]), td[0], t_x[0][0])
                fw.dma("pool", lambda e: e.dma_start(out=o2, in_=uT_own[:]), td[1], t_uown[0])
                return finish(td)

            with ExitStack() as p6:
                hbuf = [sb(p6, "hbuf%d" % i, [128, GS, 512], BF16) for i in range(2)]
                t_h = [[T("h%d_%d" % (i, j)) for j in range(GS)] for i in range(2)]
                sg = [sb(p6, "sg%d" % i, [128, 512], F32) for i in range(2)]
                t_sg = [T("sg%d" % i) for i in range(2)]
                hgp = [ps(p6, "hgp%d" % i, [128, 512]) for i in range(2)]
                t_hgp = [T("hgp%d" % i) for i in range(2)]
                hup = [ps(p6, "hup%d" % i, [128, 512]) for i in range(2)]
                t_hup = [T("hup%d" % i) for i in range(2)]
                yps = [ps(p6, "yps%d" % i, [128, 512]) for i in range(2)]
                t_yps = [T("yps%d" % i) for i in range(2)]
                nb6 = norm_bufs(p6, "f")
                t_out = [T("out%d" % i) for i in range(4)]
                cnt = 0
                for gi, (j0, gs) in enumerate(FFN_GROUPS):
                    wb = gi % 2
                    for tb in range(4):
                        cs_ = slice(tb * 512, (tb + 1) * 512)
                        hb = cnt % 2
                        cnt += 1
                        for jj in range(gs):
                            b = jj % 2
                            mm(hgp[b][:], [(wI[wb][:, k, 0, jj * 128:(jj + 1) * 128], uT_own[:, k, cs_]) for k in range(8)],
                               [t_wI[wb][0], t_uown[tb]], t_hgp[b])
                            mm(hup[b][:], [(wI[wb][:, k, 1, jj * 128:(jj + 1) * 128], uT_own[:, k, cs_]) for k in range(8)],
                               [t_wI[wb][1], t_uown[tb]], t_hup[b])
                            act(sg[b][:], hgp[b][:], AF.Silu, [t_hgp[b]], [t_sg[b]])
                            dve_tt(hbuf[hb][:, jj, :], sg[b][:], hup[b][:], ALU.mult, [t_sg[b], t_hup[b]], [t_h[hb][jj]])
                        for ot in range(8):
                            b = ot % 2
                            osl = slice(ot * 128, (ot + 1) * 128)
                            mm(yps[b][:], [(wOu[wb][:, jj, osl], hbuf[hb][:, jj, :]) for jj in range(gs)],
                               [t_wOu[wb]] + t_h[hb][0:gs], t_yps[b])
                            dve_stt(xres[:, ot, cs_], yps[b][:], G2(ot), xres[:, ot, cs_], ALU.mult, ALU.add,
                                    [t_yps[b], t_modhi, t_x[tb][ot]], [t_x[tb][ot]])
                        if gi == len(FFN_GROUPS) - 1:
                            norm_block(nb6, xres[:, :, cs_], t_x[tb], lambda k: vcol(V_FNG + k), None, xres[:, :, cs_],
                                       lambda k, tb=tb: t_x[tb][k], [])
                            for k in range(8):
                                pass
                            tsrc = T("osrc%d" % tb)
                            tsrc.lw = None
                            fw.wait_all("sp", t_x[tb])
                            fw.dma("sp", lambda e, tb=tb: e.dma_start(out=outTv[:, :, tb * 512:(tb + 1) * 512],
                                                                      in_=xres[:, :, tb * 512:(tb + 1) * 512]), t_out[tb], t_x[tb][0])
                    if gi + 2 < len(FFN_GROUPS):
                        load_group(gi + 2)
                fw.wait_all("sp", t_out)
                fw.barrier()
        fw.emit()
    return nc, dbg


def _rope_tab(pos, dim, theta):
    inv = (np.float32(theta) ** (-np.arange(0, dim, 2, dtype=np.float32) / np.float32(dim))).astype(np.float32)
    ang = pos.astype(np.float32)[:, None] * inv[None, :]
    return np.cos(ang).astype(np.float32), np.sin(ang).astype(np.float32)


def _const_mats():
    cmat = np.zeros((128, NCM), np.float32)
    cmat[:, C_ONES:C_ONES + 128] = 1.0
    for hb in (0, 64):
        cmat[hb:hb + 64, C_BO + hb:C_BO + hb + 64] = 1.0
    RA = np.zeros((128, 128), np.float32)
    RB = np.zeros((128, 128), np.float32)
    for hb in (0, 64):
        for half in (0, 32):
            for m in range(16):
                RA[hb + half + m + 16, hb + half + m] = -1.0
                RA[hb + half + m, hb + half + m + 16] = 1.0
        for m in range(8):
            RB[hb + m + 8, hb + m] = -1.0
            RB[hb + m, hb + m + 8] = 1.0
    cmat[:, C_RA:C_RA + 128] = RA
    cmat[:, C_RB:C_RB + 128] = RB
    b = np.arange(128)[:, None]
    a = np.arange(128)[None, :]
    cmat[:, C_MASK + 0:C_MASK + 128] = (b >= a + 64)
    cmat[:, C_MASK + 128:C_MASK + 256] = (np.abs(a - b) <= 64)
    cmat[:, C_MASK + 256:C_MASK + 384] = (a >= b + 64)
    return cmat


def _tables(perm):
    t = perm.astype(np.int64)
    cr, sr = _rope_tab(t // 64, 32, 10000.0)
    cc, sc = _rope_tab(t % 64, 32, 10000.0)
    CA = np.concatenate([cr, cr, cc, cc], axis=1).T
    SA = np.concatenate([sr, sr, sc, sc], axis=1).T
    tabA = np.stack([np.concatenate([CA, CA], 0), np.concatenate([SA, SA], 0)], 0).astype(np.float32)
    tb = t[:3072]
    cp, sp_ = _rope_tab(tb, 16, 500000.0)
    CB = np.ones((64, 3072), np.float32)
    SB = np.zeros((64, 3072), np.float32)
    CB[0:8] = cp.T
    CB[8:16] = cp.T
    SB[0:8] = sp_.T
    SB[8:16] = sp_.T
    tabB = np.stack([np.concatenate([CB, CB], 0), np.concatenate([SB, SB], 0)], 0).astype(np.float32)
    return np.ascontiguousarray(tabA), np.ascontiguousarray(tabB)


def _col(v, n):
    return np.asarray(v, np.float32).reshape(n, 128).T


def make_in_maps(inputs):
    f = lambda k: np.asarray(inputs[k], np.float32)
    x, c = f("x"), f("c")
    cmat = _const_mats()
    shared = {
        "cmat": cmat,
        "w_ada": np.ascontiguousarray(f("w_ada")[0]),
        "w_qkv": np.ascontiguousarray(f("w_qkv")[0]),
        "w_pa": np.ascontiguousarray(f("w_proj_a")[0]),
        "w_pb": np.ascontiguousarray(f("w_proj_b")[0]),
        "w_gate": np.ascontiguousarray(f("w_gate")[0]),
        "w_o": np.ascontiguousarray(f("w_o")[0]),
        "w_fi": np.ascontiguousarray(f("w_ffn_in")[0]),
        "w_fo": np.ascontiguousarray(f("w_ffn_out")[0]),
    }
    vecs = np.zeros((128, NV), np.float32)
    vecs[:, V_BADA:V_BADA + 48] = _col(f("b_ada")[0], 48)
    vecs[:, V_N1G:V_N1G + 8] = _col(f("norm1_g")[0], 8)
    vecs[:, V_N2G:V_N2G + 8] = _col(f("norm2_g")[0], 8)
    vecs[:, V_FNG:V_FNG + 8] = _col(f("final_norm_g"), 8)
    vecs[:, V_BG:V_BG + 16] = _col(f("b_gate")[0], 16)
    vecs[:, V_GQ] = np.tile(f("q_norm_a")[0], 2)
    vecs[:, V_GK] = np.tile(f("k_norm_a")[0], 2)
    perms = [np.arange(S), S - 1 - np.arange(S)]
    tabs = [_tables(p) for p in perms]
    in_maps = []
    for core in range(8):
        b, h = core // 2, core % 2
        m = dict(shared)
        m["xT"] = np.ascontiguousarray(x[b][perms[h]].T)
        m["cvec"] = np.ascontiguousarray(_col(c[b], 8))
        m["vecs"] = vecs
        m["tabA"], m["tabB"] = tabs[h]
        in_maps.append(m)
    return in_maps, perms


def kernel(**inputs):
    in_maps, perms = make_in_maps(inputs)
    nc, _ = build()
    res = run_bass_kernel_spmd(nc, in_maps, core_ids=list(range(8)))
    out = np.zeros((4, S, D), np.float32)
    for core in range(8):
        b, h = core // 2, core % 2
        oT = np.asarray(res.results[core]["outT"], np.float32)
        out[b, perms[h][:OWN], :] = oT.T
    return out
```

```python
import numpy as np
from contextlib import ExitStack
import concourse.bass as bass
import concourse.mybir as mybir
from concourse.bass_utils import run_bass_kernel_spmd

F32 = mybir.dt.float32
BF16 = mybir.dt.bfloat16
AF = mybir.ActivationFunctionType
ALU = mybir.AluOpType

SEM_LIMIT = 30000
S = 4096
OWN = 2048
D = 1024
DFF = 2816
EPS = 1e-6
FFN_GROUPS = [(0, 6), (6, 6), (12, 5), (17, 5)]


class T:
    __slots__ = ("name", "lw", "rd", "dsem", "dcnt", "shared")

    def __init__(self, name, shared=False):
        self.name = name
        self.lw = None
        self.rd = []
        self.dsem = None
        self.dcnt = 0
        self.shared = shared


class FW:
    ENG = ("pe", "act", "dve", "pool", "sp")

    def __init__(self, nc, stack):
        self.nc = nc
        self.stack = stack
        self.ops = {e: [] for e in self.ENG}
        self.sem = {}
        self.cnt = {}
        self.known = {e: {} for e in self.ENG}
        self.snap = {}
        self.nsem = 0
        self.pending_out = []
        for e in self.ENG:
            self._newsem(e)
        self.same_engine_sync = {"pe": False, "act": True, "dve": True, "pool": True, "sp": False}

    def _alloc_sem(self, name):
        self.nsem += 1
        return self.stack.enter_context(self.nc.semaphore("%s_%d" % (name, self.nsem)))

    def _newsem(self, e):
        self.sem[e] = self._alloc_sem("s_" + e)
        self.cnt[e] = 0

    def _learn(self, e, s, v):
        kn = self.known[e]
        kn[s] = v
        sn = self.snap.get((id(s), v))
        if sn:
            for s2, v2 in sn.items():
                if kn.get(s2, 0) < v2:
                    kn[s2] = v2

    def _waits(self, e, reads, writes):
        need = {}

        def add(ev):
            if ev is None:
                return
            s, v = ev
            if need.get(s, 0) < v:
                need[s] = v
        for t in reads:
            add(t.lw)
            if not t.shared:
                for ev in t.rd:
                    if ev[0] is not self.sem[e]:
                        add(ev)
        for t in writes:
            add(t.lw)
            for ev in t.rd:
                add(ev)
        out = []
        kn = self.known[e]
        for s, v in need.items():
            if s is self.sem[e] and not self.same_engine_sync[e]:
                continue
            if kn.get(s, 0) >= v:
                continue
            out.append((s, v))
            self._learn(e, s, v)
        return out

    def op(self, e, fn, reads=(), writes=()):
        waits = self._waits(e, reads, writes)
        if self.cnt[e] >= SEM_LIMIT:
            self._newsem(e)
        self.cnt[e] += 1
        s = self.sem[e]
        ev = (s, self.cnt[e])
        sn = dict(self.known[e])
        sn[s] = self.cnt[e]
        self.snap[(id(s), self.cnt[e])] = sn
        for t in reads:
            t.rd.append(ev)
        for t in writes:
            t.lw = ev
            t.rd = []
        self.ops[e].append((waits, fn, (s, 1)))
        return ev

    def dma(self, q, fn, dst, src=None):
        reads = [src] if src is not None else []
        waits = self._waits(q, reads, [dst])
        if dst.dsem is None:
            dst.dsem = self._alloc_sem("d")
        dst.dcnt += 16
        ev = (dst.dsem, dst.dcnt)
        self.snap[(id(dst.dsem), dst.dcnt)] = dict(self.known[q])
        for t in reads:
            t.rd.append(ev)
            self.pending_out.append(ev)
        dst.lw = ev
        dst.rd = []
        self.ops[q].append((waits, fn, (dst.dsem, 16)))
        return ev

    def wait_all(self, e, tiles):
        waits = self._waits(e, tiles, [])
        self.ops[e].append((waits, None, None))

    def barrier(self):
        evs = [(f, self.sem[f], self.cnt[f]) for f in self.ENG if self.cnt[f] > 0]
        for e in self.ENG:
            waits = []
            for f, s, v in evs:
                if f == e and not self.same_engine_sync[e]:
                    continue
                if self.known[e].get(s, 0) < v:
                    waits.append((s, v))
                    self._learn(e, s, v)
            for s, v in self.pending_out:
                if self.known[e].get(s, 0) < v:
                    waits.append((s, v))
                    self._learn(e, s, v)
            self.ops[e].append((waits, None, None))
        self.pending_out = []

    def emit(self):
        nc = self.nc
        with nc.Block() as block:
            def mk(e):
                def body(eng):
                    for waits, fn, inc in self.ops[e]:
                        for s, v in waits:
                            eng.wait_ge(s, v)
                        if fn is None:
                            continue
                        ins = fn(eng)
                        if inc is not None:
                            ins.then_inc(inc[0], inc[1])
                return body
            block.tensor(mk("pe"))
            block.scalar(mk("act"))
            block.vector(mk("dve"))
            block.gpsimd(mk("pool"))
            block.sync(mk("sp"))


NV = 48 + 8 + 8 + 8 + 16 + 2
V_BADA, V_N1G, V_N2G, V_FNG, V_BG, V_GQ, V_GK = 0, 48, 56, 64, 72, 88, 89
NCM = 4 * 128 + 384
C_ONES, C_BO, C_RA, C_RB, C_MASK = 0, 128, 256, 384, 512


def build(debug=None):
    nc = bass.Bass("TRN2", target_bir_lowering=False)

    def din(name, shape, dt=F32):
        return nc.dram_tensor(name, list(shape), dt, kind="ExternalInput").ap()

    xT = din("xT", [D, S])
    cvec = din("cvec", [128, 8])
    vecs = din("vecs", [128, NV])
    cmat = din("cmat", [128, NCM])
    tabA = din("tabA", [2, 128, S])
    tabB = din("tabB", [2, 128, 3072])
    w_ada = din("w_ada", [D, 6 * D])
    w_qkv = din("w_qkv", [D, 3072])
    w_pa = din("w_pa", [512, D])
    w_pb = din("w_pb", [256, D])
    w_gate = din("w_gate", [D, 2 * D])
    w_o = din("w_o", [D, D])
    w_fi = din("w_fi", [D, 2 * DFF])
    w_fo = din("w_fo", [DFF, D])
    outT = nc.dram_tensor("outT", [D, OWN], F32, kind="ExternalOutput").ap()
    dbg = {}

    def dbg_out(name, shape):
        dbg[name] = nc.dram_tensor(name, list(shape), F32, kind="ExternalOutput").ap()
        return dbg[name]

    xTv = xT.rearrange("(k p) t -> p k t", p=128)
    outTv = outT.rearrange("(k p) t -> p k t", p=128)
    w_qkv_v = w_qkv.rearrange("(k p) n -> p k n", p=128)

    with ExitStack() as top:
        fw = FW(nc, top)

        def sb(ctx, name, shape, dt):
            return ctx.enter_context(nc.sbuf_tensor(name, list(shape), dt))

        def ps(ctx, name, shape, dt=F32):
            return ctx.enter_context(nc.psum_tensor(name, list(shape), dt))

        def mm(out_ap, pairs, reads, twrite):
            def fn(e):
                n = len(pairs)
                ins = None
                for i, (l, r) in enumerate(pairs):
                    ins = e.matmul(out_ap, lhsT=l, rhs=r, start=(i == 0), stop=(i == n - 1))
                return ins
            return fw.op("pe", fn, reads, [twrite])

        def mm_multi(groups, reads, writes):
            def fn(e):
                ins = None
                for out_ap, pairs in groups:
                    n = len(pairs)
                    for i, (l, r) in enumerate(pairs):
                        ins = e.matmul(out_ap, lhsT=l, rhs=r, start=(i == 0), stop=(i == n - 1))
                return ins
            return fw.op("pe", fn, reads, writes)

        def act(out, in_, func, reads, writes, **kw):
            return fw.op("act", lambda e: e.activation(out=out, in_=in_, func=func, **kw), reads, writes)

        def dve_tt(out, in0, in1, op, reads, writes, eng="dve"):
            return fw.op(eng, lambda e: e.tensor_tensor(out=out, in0=in0, in1=in1, op=op), reads, writes)

        def dve_stt(out, in0, scalar, in1, op0, op1, reads, writes):
            return fw.op("dve", lambda e: e.scalar_tensor_tensor(out=out, in0=in0, scalar=scalar, in1=in1,
                                                                op0=op0, op1=op1), reads, writes)

        def dve_recip(out, in_, reads, writes):
            return fw.op("dve", lambda e: e.reciprocal(out=out, in_=in_), reads, writes)

        def sl(start, count, step):
            return slice(start, start + step * (count - 1) + 1, step)

        def finish(tds, q="pool"):
            fw.wait_all(q, tds)
            fw.emit()
            return nc, dbg

        vec_sb = sb(top, "vec_sb", [128, NV], F32)
        cm_f = sb(top, "cm_f", [128, NCM], F32)
        cm = sb(top, "cm", [128, NCM], BF16)
        mod = sb(top, "mod", [128, 48], F32)
        A12 = sb(top, "A12", [128, 16], F32)
        uT_own = sb(top, "uT_own", [128, 8, OWN], BF16)
        t_vec, t_cmf, t_cm, t_mod, t_A = T("vec", shared=True), T("cmf"), T("cm", shared=True), T("mod", shared=True), T("A12", shared=True)
        t_modhi, t_A2 = T("mod_hi", shared=True), T("A12_hi", shared=True)
        cb = sb(top, "cb", [128, 8], BF16)
        t_cb = T("cb", shared=True)
        t_uown = [T("uown%d" % i) for i in range(4)]

        ONES = cm[:, C_ONES:C_ONES + 128]
        BO = cm[:, C_BO:C_BO + 128]
        RA = cm[:, C_RA:C_RA + 128]
        RB = cm[:, C_RB:C_RB + 128]
        MASK = cm[:, C_MASK:C_MASK + 384]

        fw.dma("sp", lambda e: e.dma_start(out=vec_sb[:], in_=vecs), t_vec)
        fw.dma("sp", lambda e: e.dma_start(out=cm_f[:], in_=cmat), t_cmf)
        act(cm[:], cm_f[:], AF.Copy, [t_cmf], [t_cm])

        def vcol(c, n=1):
            return vec_sb[:, c:c + n]

        SH1 = lambda k: mod[:, k:k + 1]
        SH2 = lambda k: mod[:, 24 + k:25 + k]
        G1 = lambda k: mod[:, 16 + k:17 + k]
        G2 = lambda k: mod[:, 40 + k:41 + k]

        def norm_bufs(ctx, tag):
            sq = [sb(ctx, "nrm_sq%s%d" % (tag, i), [128, 512], BF16) for i in range(2)]
            ssp = ps(ctx, "nrm_ssp" + tag, [128, 512])
            rs = sb(ctx, "nrm_rs" + tag, [128, 512], F32)
            rstd = sb(ctx, "nrm_rstd" + tag, [128, 512], F32)
            tmp = [sb(ctx, "ntmp%s%d" % (tag, i), [128, 512], F32) for i in range(2)]
            return dict(sq=sq, t_sq=[T("sq0" + tag), T("sq1" + tag)], ssp=ssp, t_ssp=T("ssp" + tag), rs=rs, t_rs=T("rs" + tag),
                        rstd=rstd, t_rstd=T("rstd" + tag), tmp=tmp, t_tmp=[T("tmp0" + tag), T("tmp1" + tag)])

        def norm_block(nb, src, t_src, a_col, sh_col, dst, t_dst, cdeps):
            for k in range(8):
                i = k % 2
                act(nb["sq"][i][:], src[:, k, :], AF.Square, t_src, [nb["t_sq"][i]])
                fw.op("pe", lambda e, k=k, i=i: e.matmul(nb["ssp"][:], lhsT=ONES, rhs=nb["sq"][i][:], start=(k == 0), stop=(k == 7)),
                      [nb["t_sq"][i], t_cm], [nb["t_ssp"]])
            act(nb["rs"][:], nb["ssp"][:], AF.Sqrt, [nb["t_ssp"]], [nb["t_rs"]], scale=1.0 / D, bias=EPS)
            dve_recip(nb["rstd"][:], nb["rs"][:], [nb["t_rs"]], [nb["t_rstd"]])
            for k in range(8):
                i = k % 2
                rd = t_src + [nb["t_rstd"], t_vec] + cdeps
                if sh_col is None:
                    dve_stt(dst[:, k, :], src[:, k, :], a_col(k), nb["rstd"][:], ALU.mult, ALU.mult, rd, [t_dst(k)])
                else:
                    dve_stt(nb["tmp"][i][:], src[:, k, :], a_col(k), nb["rstd"][:], ALU.mult, ALU.mult, rd, [nb["t_tmp"][i]])
                    act(dst[:, k, :], nb["tmp"][i][:], AF.Identity, [nb["t_tmp"][i]] + cdeps, [t_dst(k)], bias=sh_col(k), scale=1.0)

        with ExitStack() as ctxA:
            yaT = sb(ctxA, "yaT", [128, 4, OWN], BF16)
            ybT = sb(ctxA, "ybT", [128, 2, OWN], BF16)
            t_ya = [[T("ya%d_%d" % (h, q)) for q in range(4)] for h in range(8)]
            t_yb = [[T("yb%d_%d" % (h, q)) for q in range(4)] for h in range(4)]

            with ExitStack() as attn:
                uT_elo = sb(attn, "uT_elo", [128, 8, 1024], BF16)
                t_uelo = [T("uelo%d" % i) for i in range(2)]

                with ExitStack() as pha:
                    uT_ehi = sb(pha, "uT_ehi", [128, 8, 1024], BF16)
                    t_uehi = [T("uehi%d" % i) for i in range(2)]
                    wA = sb(pha, "wA", [128, 8, 768], BF16)
                    t_wA = T("wA")

                    def ublk(blk):
                        if blk < 4:
                            return uT_own[:, :, blk * 512:(blk + 1) * 512], t_uown[blk]
                        if blk < 6:
                            return uT_elo[:, :, (blk - 4) * 512:(blk - 3) * 512], t_uelo[blk - 4]
                        return uT_ehi[:, :, (blk - 6) * 512:(blk - 5) * 512], t_uehi[blk - 6]

                    with ExitStack() as p0:
                        cv = sb(p0, "cv", [128, 8], F32)
                        wad = sb(p0, "wad", [128, 8, 2 * D], BF16)
                        ps_mod = ps(p0, "ps_mod", [128, 512])[:, 0:16]
                        t_cv, t_psm = T("cv"), T("psm")
                        t_wad = [T("wad%d" % i) for i in range(2)]
                        fw.dma("sp", lambda e: e.dma_start(out=cv[:], in_=cvec), t_cv)
                        wadv = w_ada.rearrange("(k p) n -> p k n", p=128)
                        for g in (1, 0):
                            fw.dma("pool", lambda e, g=g: e.dma_start(out=wad[:, :, g * D:(g + 1) * D],
                                                                     in_=wadv[:, :, g * D:(g + 1) * D]), t_wad[g])
                        t_wAq = [T("wAq%d" % i) for i in range(8)]
                        for hd in range(8):
                            slot = (hd % 4) * 2 + hd // 4
                            fw.dma("pool", lambda e, hd=hd, slot=slot: e.dma_start(
                                out=wA[:, :, slot * 64:(slot + 1) * 64], in_=w_qkv_v[:, :, hd * 64:(hd + 1) * 64]), t_wAq[hd])
                        fw.dma("pool", lambda e: e.dma_start(out=wA[:, :, 512:768], in_=w_qkv_v[:, :, 512:768]), t_wA)
                        act(cb[:], cv[:], AF.Silu, [t_cv], [t_cb])
                        for g in (1, 0):
                            groups = []
                            for j in range(g * 8, g * 8 + 8):
                                groups.append((ps_mod[:, j:j + 1],
                                               [(wad[:, k, j * 128:(j + 1) * 128], cb[:, k:k + 1]) for k in range(8)]))
                            mm_multi(groups, [t_cb, t_wad[g]], [t_psm])
                        dve_tt(mod[:, 0:16], ps_mod[:], vcol(V_BADA, 16), ALU.add, [t_psm, t_vec], [t_mod])
                        dve_stt(A12[:, 0:8], mod[:, 8:16], 1.0, vcol(V_N1G, 8), ALU.add, ALU.mult, [t_mod, t_vec], [t_A])
                        fw.barrier()
                    if debug == "mod":
                        o = dbg_out("dbg_mod", [128, 64])
                        td, td2 = T("dbgmod"), T("dbgA")
                        fw.dma("sp", lambda e: e.dma_start(out=o[:, 0:16], in_=mod[:, 0:16]), td, t_mod)
                        fw.dma("sp", lambda e: e.dma_start(out=o[:, 48:56], in_=A12[:, 0:8]), td2, t_A)
                        return finish([td, td2], "sp")

                    with ExitStack() as p1:
                        xb = [sb(p1, "xb%d" % i, [128, 8, 512], F32) for i in range(2)]
                        t_xb = [T("xb%d" % i) for i in range(2)]
                        nb = [norm_bufs(p1, "a"), norm_bufs(p1, "b")]
                        for blk in range(8):
                            i = blk % 2
                            fw.dma("sp", lambda e, blk=blk, i=i: e.dma_start(out=xb[i][:], in_=xTv[:, :, blk * 512:(blk + 1) * 512]), t_xb[i])
                            dst, t_dst = ublk(blk)
                            norm_block(nb[i], xb[i][:], [t_xb[i]], lambda k: A12[:, k:k + 1], SH1, dst, lambda k, t=t_dst: t, [t_A, t_mod])
                        fw.barrier()
                    if debug == "u":
                        o = dbg_out("dbg_u", [128, 8, S])
                        tds = [T("dbgu%d" % i) for i in range(8)]
                        for blk in range(8):
                            src, t_s = ublk(blk)
                            fw.dma("pool", lambda e, blk=blk, src=src: e.dma_start(out=o[:, :, blk * 512:(blk + 1) * 512], in_=src), tds[blk], t_s)
                        return finish(tds)

                    with ExitStack() as p23:
                        QaT = sb(p23, "QaT", [128, 4, OWN], BF16)
                        KaT = sb(p23, "KaT", [128, S], BF16)
                        Va = sb(p23, "Va", [128, 32, 3, 64], BF16)
                        t_Qa = [[T("Qa%d_%d" % (i, q)) for q in range(4)] for i in range(4)]
                        t_Ka = [T("Ka%d" % q) for q in range(8)]
                        t_Va = [T("Va%d" % q) for q in range(8)]
                        t_Vones = T("Vones")
                        fw.op("pool", lambda e: e.memset(Va[:, :, 1, :], 1.0), [], [t_Vones])
                        with ExitStack() as p2:
                            tab = [sb(p2, "tabA%d" % i, [128, 2, 512], F32) for i in range(2)]
                            t_tab = [T("tabA%d" % i) for i in range(2)]

                            def two(name, shape, dt, psum=False):
                                if psum:
                                    return [ps(p2, "%s%d" % (name, i), shape) for i in range(2)], [T("%s%d" % (name, i)) for i in range(2)]
                                return [sb(p2, "%s%d" % (name, i), shape, dt) for i in range(2)], [T("%s%d" % (name, i)) for i in range(2)]
                            qps, t_qps = two("qps", [128, 512], F32, True)
                            ssq, t_ssq = two("ssq", [128, 512], F32, True)
                            rqp, t_rqp = two("rqp", [128, 512], F32, True)
                            vps, t_vps = two("vps", [128, 4, 128], F32, True)
                            sqb, t_sqb = two("sqb", [128, 512], BF16)
                            qg, t_qg = two("qg", [128, 512], BF16)
                            rsq, t_rsq = two("rsq", [128, 512], F32)
                            rq, t_rq = two("rq", [128, 512], F32)
                            t1, t_t1 = two("t1_", [128, 512], F32)
                            t2, t_t2 = two("t2_", [128, 512], F32)
                            cnt = 0
                            for blk in range(8):
                                ti = blk % 2
                                fw.dma("sp", lambda e, blk=blk, ti=ti: e.dma_start(
                                    out=tab[ti][:], in_=tabA[:, :, blk * 512:(blk + 1) * 512].rearrange("c p t -> p c t")), t_tab[ti])
                                u_ap, t_u = ublk(blk)
                                tiles = ([("q", i) for i in range(4)] if blk < 4 else []) + [("k", 0)]
                                for kind, i in tiles:
                                    b = cnt % 2
                                    cnt += 1
                                    if kind == "q":
                                        def lw(k, i=i):
                                            return wA[:, k, i * 128:(i + 1) * 128]
                                        gain = vcol(V_GQ)
                                        dst = QaT[:, i, blk * 512:(blk + 1) * 512]
                                        t_dst = t_Qa[i][blk]
                                    else:
                                        def lw(k):
                                            return wA[:, k, 512:640]
                                        gain = vcol(V_GK)
                                        dst = KaT[:, blk * 512:(blk + 1) * 512]
                                        t_dst = t_Ka[blk]
                                    mm(qps[b][:], [(lw(k), u_ap[:, k, :]) for k in range(8)], [t_wA, t_u] + t_wAq, t_qps[b])
                                    act(sqb[b][:], qps[b][:], AF.Square, [t_qps[b]], [t_sqb[b]])
                                    act(qg[b][:], qps[b][:], AF.Copy, [t_qps[b], t_vec], [t_qg[b]], scale=gain)
                                    mm(ssq[b][:], [(BO, sqb[b][:])], [t_sqb[b], t_cm], t_ssq[b])
                                    mm(rqp[b][:], [(RA, qg[b][:])], [t_qg[b], t_cm], t_rqp[b])
                                    act(rsq[b][:], ssq[b][:], AF.Sqrt, [t_ssq[b]], [t_rsq[b]], scale=1.0 / 64, bias=EPS)
                                    dve_recip(rq[b][:], rsq[b][:], [t_rsq[b]], [t_rq[b]])
                                    dve_stt(t1[b][:], qps[b][:], gain, tab[ti][:, 0, :], ALU.mult, ALU.mult,
                                            [t_qps[b], t_tab[ti], t_vec], [t_t1[b]])
                                    dve_tt(t2[b][:], rqp[b][:], tab[ti][:, 1, :], ALU.mult, [t_rqp[b], t_tab[ti]], [t_t2[b]])
                                    dve_tt(t1[b][:], t1[b][:], t2[b][:], ALU.add, [t_t1[b], t_t2[b]], [t_t1[b]], eng="pool")
                                    dve_tt(dst, t1[b][:], rq[b][:], ALU.mult, [t_t1[b], t_rq[b]], [t_dst], eng="pool")
                                vb = blk % 2
                                groups = []
                                for j in range(4):
                                    groups.append((vps[vb][:, j, :],
                                                   [(u_ap[:, k, j * 128:(j + 1) * 128], wA[:, k, 640:768]) for k in range(8)]))
                                mm_multi(groups, [t_wA, t_u], [t_vps[vb]])
                                for v in range(2):
                                    act(Va[:, blk * 4:blk * 4 + 4, 2 * v, :], vps[vb][:, :, v * 64:(v + 1) * 64], AF.Copy,
                                        [t_vps[vb], t_Vones], [t_Va[blk]])
                            fw.barrier()
                        if debug == "qkv_a":
                            o1 = dbg_out("dbg_Qa", [128, 4, OWN])
                            o2 = dbg_out("dbg_Ka", [128, S])
                            o3 = dbg_out("dbg_Va", [128, 32 * 3 * 64])
                            td = [T("d1"), T("d2"), T("d3")]
                            fw.dma("pool", lambda e: e.dma_start(out=o1, in_=QaT[:]), td[0], t_Qa[0][0])
                            fw.dma("pool", lambda e: e.dma_start(out=o2, in_=KaT[:]), td[1], t_Ka[0])
                            fw.dma("pool", lambda e: e.dma_start(out=o3, in_=Va[:].rearrange("p a b c -> p (a b c)")), td[2], t_Va[0])
                            return finish(td)

                        with ExitStack() as p3:
                            NS = 3
                            Sps = [ps(p3, "Sps%d" % i, [128, 2, 512]) for i in range(NS)]
                            t_S = [T("Sps%d" % i) for i in range(NS)]
                            Ops = ps(p3, "Ops0", [128, 2, 512])
                            t_O = T("Ops0")
                            Osb = [sb(p3, "Osb%d" % i, [128, 2, 512], F32) for i in range(2)]
                            t_Osb = [T("Osb%d" % i) for i in range(2)]
                            NPT = 3
                            PT = [sb(p3, "PT%d" % i, [128, 2, 512], BF16) for i in range(NPT)]
                            t_PT = [T("PT%d" % i) for i in range(NPT)]
                            rz = [sb(p3, "rz%d" % i, [128, 2, 512], F32) for i in range(2)]
                            t_rz = [T("rz%d" % i) for i in range(2)]
                            npair = 1 if debug == "ya1" else 4
                            steps = [(i, qb, kt) for i in range(npair) for qb in range(4) for kt in range(32)]
                            Vav = [Va[:, kt, :, :].rearrange("p a b -> p (a b)") for kt in range(32)]

                            def qk(sidx):
                                i, qb, kt = steps[sidx]
                                b = sidx % NS
                                qs = slice(qb * 512, (qb + 1) * 512)
                                ks = slice(kt * 128, (kt + 1) * 128)
                                groups = [(Sps[b][:, 0, :], [(KaT[0:64, ks], QaT[0:64, i, qs])]),
                                          (Sps[b][:, 1, :], [(KaT[64:128, ks], QaT[64:128, i, qs])])]
                                mm_multi(groups, [t_Ka[kt // 4], t_Qa[i][qb]], [t_S[b]])

                            def ex(sidx):
                                b = sidx % NS
                                p = sidx % NPT
                                act(PT[p][:], Sps[b][:], AF.Exp, [t_S[b]], [t_PT[p]], scale=0.125)

                            def pv(sidx):
                                i, qb, kt = steps[sidx]
                                p = sidx % NPT
                                o2 = (sidx // 32) % 2

                                def fn(e):
                                    ins = None
                                    for j in range(2):
                                        lhs = Vav[kt][:, 64 * j:64 * j + 128]
                                        ins = e.matmul(Ops[:, j, :], lhsT=lhs, rhs=PT[p][:, j, :],
                                                       start=(kt == 0), stop=(kt == 31))
                                    return ins
                                fw.op("pe", fn, [t_Va[kt // 4], t_PT[p]], [t_O])
                                if kt == 31:
                                    fw.op("dve", lambda e: e.tensor_copy(out=Osb[o2][:], in_=Ops[:]), [t_O], [t_Osb[o2]])
                                    for j in range(2):
                                        head = i + 4 * j
                                        orow = slice(64 * j, 64 * j + 64)
                                        zrow = slice(64 - 64 * j, 128 - 64 * j)
                                        dve_recip(rz[o2][orow, j, :], Osb[o2][zrow, j, :], [t_Osb[o2]], [t_rz[o2]])
                                        pb_ = 64 * (head % 2)
                                        dve_tt(yaT[pb_:pb_ + 64, head // 2, qb * 512:(qb + 1) * 512], Osb[o2][orow, j, :],
                                               rz[o2][orow, j, :], ALU.mult, [t_Osb[o2], t_rz[o2]], [t_ya[head][qb]])

                            for sidx in range(min(NS, len(steps))):
                                qk(sidx)
                            for sidx in range(len(steps)):
                                ex(sidx)
                                pv(sidx)
                                if sidx + NS < len(steps):
                                    qk(sidx + NS)
                            fw.barrier()
                    if debug in ("ya", "ya1"):
                        o1 = dbg_out("dbg_ya", [128, 4, OWN])
                        td = T("d1")
                        fw.dma("pool", lambda e: e.dma_start(out=o1, in_=yaT[:]), td, t_ya[0][0])
                        return finish([td])

                acc = sb(attn, "acc", [128, 4, OWN], F32)
                rzb = sb(attn, "rzb", [128, 512], F32)
                with ExitStack() as p4:
                    wB = sb(p4, "wB", [128, 8, 768], BF16)
                    t_wB = [T("wBq"), T("wBk"), T("wBv")]
                    QbT = sb(p4, "QbT", [128, 2, OWN], BF16)
                    KbT = sb(p4, "KbT", [128, 2, 4096], BF16)
                    Vb = sb(p4, "Vb", [128, 32, 6, 64], BF16)
                    t_Qb = [T("Qb%d" % i) for i in range(2)]
                    t_Kb = [T("Kb%d" % i) for i in range(2)]
                    t_Vb = T("Vb")
                    t_acc = [T("acc%d" % i) for i in range(2)]
                    tabb = [sb(p4, "tabB%d" % i, [128, 2, 512], F32) for i in range(2)]
                    t_tabb = [T("tabB%d" % i) for i in range(2)]
                    pps = [ps(p4, "pps%d" % i, [128, 512]) for i in range(2)]
                    t_pps = [T("pps%d" % i) for i in range(2)]
                    rps = [ps(p4, "rps%d" % i, [128, 512]) for i in range(2)]
                    t_rps = [T("rps%d" % i) for i in range(2)]
                    OB_ = [ps(p4, "OB%d" % i, [128, 2, 256]) for i in range(2)]
                    t_OB = [T("OB%d" % i) for i in range(2)]
                    Sbuf = [(pps, t_pps), (rps, t_rps)]
                    pbb = [sb(p4, "pbb%d" % i, [128, 512], BF16) for i in range(2)]
                    t_pbb = [T("pbb%d" % i) for i in range(2)]
                    b1 = [sb(p4, "b1_%d" % i, [128, 512], F32) for i in range(2)]
                    t_b1 = [T("b1_%d" % i) for i in range(2)]
                    b2 = [sb(p4, "b2_%d" % i, [128, 512], F32) for i in range(2)]
                    t_b2 = [T("b2_%d" % i) for i in range(2)]
                    EB = [sb(p4, "EB%d" % i, [128, 2, 384], BF16) for i in range(2)]
                    t_EB = [T("EB%d" % i) for i in range(2)]
                    PB = [sb(p4, "PB%d" % i, [128, 2, 384], BF16) for i in range(2)]
                    t_PB = [T("PB%d" % i) for i in range(2)]
                    t_rzb = T("rzb")
                    fw.op("pool", lambda e: e.memset(Vb[:, :, 1:5:3, :], 1.0), [], [t_Vb])
                    def p4_exit():
                        fw.barrier()
                        o1 = dbg_out("dbg_ya", [128, 4, OWN])
                        td = T("d1")
                        fw.dma("pool", lambda e: e.dma_start(out=o1, in_=yaT[:]), td, t_ya[0][0])
                        return finish([td])
                    if debug == "p4_a":
                        return p4_exit()
                    cnt = 0
                    ucnt = 0
                    glist = [(0, 1), (1, 4), (2, 16)]
                    if debug == "yb0":
                        glist = glist[:1]
                    for g, dil in glist:
                        nq = 16 // dil
                        segw = (nq + 1) * 128
                        for part in range(3):
                            c0 = 768 + part * 768 + g * 256
                            fw.dma("pool", lambda e, part=part, c0=c0: e.dma_start(
                                out=wB[:, :, part * 256:(part + 1) * 256], in_=w_qkv_v[:, :, c0:c0 + 256]), t_wB[part])
                        Kview = [KbT[:, ft, 0:dil * segw].rearrange("p (s c) -> p c s", s=dil) for ft in range(2)]
                        for ft in range(2):
                            padv = KbT[:, ft, 0:dil * segw].rearrange("p (s c) -> p s c", s=dil)[:, :, nq * 128 + 64:nq * 128 + 128]
                            fw.op("pool", lambda e, padv=padv: e.memset(padv, 0.0), [], [t_Kb[ft]])
                        Qview = [QbT[:, ft, :].rearrange("p (s c) -> p c s", s=dil) for ft in range(2)]
                        nhalo = 64 * dil
                        blocks = [("own", tb, 512) for tb in range(4)] + \
                                 [("halo", hb, min(512, nhalo)) for hb in range((nhalo + 511) // 512)]
                        if debug == "yb_pown":
                            blocks = [bk for bk in blocks if bk[0] == "own"]
                        def make_tile(kind, bi, blen, what, ft, u_ap, t_u, ti, b, dil, nq, Qview, Kview):
                            pi = 0 if what == "q" else 1
                            wcol = pi * 256 + ft * 128

                            def proj():
                                mm(pps[b][:, 0:blen], [(wB[:, k, wcol:wcol + 128], u_ap[:, k, :]) for k in range(8)],
                                   [t_wB[pi], t_u], t_pps[b])

                            def rest():
                                act(pbb[b][:, 0:blen], pps[b][:, 0:blen], AF.Copy, [t_pps[b]], [t_pbb[b]])
                                mm(rps[b][:, 0:blen], [(RB, pbb[b][:, 0:blen])], [t_pbb[b], t_cm], t_rps[b])
                                dve_tt(b1[b][:, 0:blen], pps[b][:, 0:blen], tabb[ti][:, 0, 0:blen], ALU.mult,
                                       [t_pps[b], t_tabb[ti]], [t_b1[b]])
                                dve_tt(b2[b][:, 0:blen], rps[b][:, 0:blen], tabb[ti][:, 1, 0:blen], ALU.mult,
                                       [t_rps[b], t_tabb[ti]], [t_b2[b]])
                                j0 = (bi * 512) // dil
                                nj = blen // dil
                                if what == "q":
                                    dst = Qview[ft][:, j0:j0 + nj, :]
                                    t_dst = t_Qb[ft]
                                else:
                                    cbase = j0 if kind == "own" else nq * 128 + j0
                                    dst = Kview[ft][:, cbase:cbase + nj, :]
                                    t_dst = t_Kb[ft]
                                in0 = b1[b][:, 0:blen].rearrange("p (j r) -> p j r", r=dil)
                                in1 = b2[b][:, 0:blen].rearrange("p (j r) -> p j r", r=dil)
                                fw.op("pool", lambda e: e.tensor_tensor(out=dst, in0=in0, in1=in1, op=ALU.add),
                                      [t_b1[b], t_b2[b]], [t_dst])
                            return proj, rest

                        def make_tab_dma(ti, tcol, blen):
                            def go():
                                fw.dma("sp", lambda e: e.dma_start(
                                    out=tabb[ti][:, :, 0:blen], in_=tabB[:, :, tcol:tcol + blen].rearrange("c p t -> p c t")), t_tabb[ti])
                            return go

                        psteps = []
                        for kind, bi, blen in blocks:
                            if kind == "own":
                                u_ap, t_u = uT_own[:, :, bi * 512:(bi + 1) * 512], t_uown[bi]
                                tcol = bi * 512
                            else:
                                u_ap, t_u = uT_elo[:, :, bi * 512:bi * 512 + blen], t_uelo[bi]
                                tcol = OWN + bi * 512
                            ti = cnt % 2
                            psteps.append(("dma", make_tab_dma(ti, tcol, blen)))
                            tiles = ([("q", 0), ("q", 1)] if kind == "own" else []) + [("k", 0), ("k", 1)]
                            for what, ft in tiles:
                                psteps.append(("tile",) + make_tile(kind, bi, blen, what, ft, u_ap, t_u, ti, cnt % 2, dil, nq, Qview, Kview))
                                cnt += 1
                        tile_pos = [i for i, s in enumerate(psteps) if s[0] == "tile"]
                        done_proj = set()
                        for pos, s in enumerate(psteps):
                            if s[0] == "dma":
                                s[1]()
                                continue
                            if pos not in done_proj:
                                s[1]()
                                done_proj.add(pos)
                            nxt = next((p for p in tile_pos if p > pos), None)
                            if nxt is not None and nxt not in done_proj:
                                psteps[nxt][1]()
                                done_proj.add(nxt)
                            s[2]()
                        if debug in ("yb_p", "yb_pown"):
                            fw.barrier()
                            o1 = dbg_out("dbg_Qb", [128, 2, OWN])
                            o2 = dbg_out("dbg_Kb", [128, 2, 4096])
                            td = [T("d1"), T("d2")]
                            fw.dma("pool", lambda e: e.dma_start(out=o1, in_=QbT[:]), td[0], t_Qb[0])
                            fw.dma("pool", lambda e: e.dma_start(out=o2, in_=KbT[:]), td[1], t_Kb[0])
                            return finish(td)
                        vt = 0
                        for seg in range(dil):
                            for m in range(nq + 1):
                                tile = seg * (nq + 1) + m
                                ob = vt % 2
                                vt += 1
                                if m < nq:
                                    st0 = seg + dil * 128 * m
                                    lhs = lambda k, st0=st0, dil=dil: uT_own[:, k, sl(st0, 128, dil)]
                                    t_u = t_uown
                                    mrows = 128
                                else:
                                    lhs = lambda k, seg=seg, dil=dil: uT_elo[:, k, sl(seg, 64, dil)]
                                    t_u = t_uelo
                                    mrows = 64
                                mm(OB_[ob][0:mrows, 0, :], [(lhs(k), wB[:, k, 512:768]) for k in range(8)], [t_wB[2]] + t_u, t_OB[ob])
                                for hp2 in range(2):
                                    act(Vb[0:mrows, tile, 3 * hp2:3 * hp2 + 3:2, :],
                                        OB_[ob][0:mrows, 0, hp2 * 128:(hp2 + 1) * 128].rearrange("p (h d) -> p h d", h=2),
                                        AF.Copy, [t_OB[ob]], [t_Vb])
                        if debug == "yb_v":
                            fw.barrier()
                            o1 = dbg_out("dbg_Vb", [128, 32 * 6 * 64])
                            td = [T("d1")]
                            fw.dma("pool", lambda e: e.dma_start(out=o1, in_=Vb[:].rearrange("p a b c -> p (a b c)")), td[0], t_Vb)
                            return finish(td)
                        def make_unit(g, dil, nq, segw, seg, n, hp, u):
                            St, t_St = Sbuf[u]
                            cs = [c for c in range(3) if n - 1 + c >= 0]
                            c_lo = cs[0] * 128
                            qcol = seg * nq * 128 + n * 128

                            def qk():
                                groups = []
                                for hh in range(2):
                                    pr = slice(64 * hh, 64 * hh + 64)
                                    for c in cs:
                                        m = n - 1 + c
                                        kc = seg * segw + m * 128
                                        groups.append((St[hh][:, c * 128:(c + 1) * 128],
                                                       [(KbT[pr, hp, kc:kc + 128], QbT[pr, hp, qcol:qcol + 128])]))
                                mm_multi(groups, [t_Kb[hp], t_Qb[hp]], t_St)

                            def mid():
                                for hh in range(2):
                                    act(EB[u][:, hh, c_lo:384], St[hh][:, c_lo:384], AF.Exp, [t_St[hh]], [t_EB[u]], scale=0.125)
                                for hh in range(2):
                                    dve_tt(PB[u][:, hh, c_lo:384], EB[u][:, hh, c_lo:384], MASK[:, c_lo:384], ALU.mult,
                                           [t_EB[u], t_cm], [t_PB[u]])

                            def pv():
                                def fn(e):
                                    ins = None
                                    for hh in range(2):
                                        head = 2 * hp + hh
                                        for ci, c in enumerate(cs):
                                            m = n - 1 + c
                                            tile = seg * (nq + 1) + m
                                            kr = 64 if m == nq else 128
                                            sblk = (0, 1, 3, 4)[head]
                                            lhs = Vb[0:kr, tile, sblk:sblk + 2, :].rearrange("p a b -> p (a b)")
                                            ins = e.matmul(OB_[u][:, hh, 0:128], lhsT=lhs,
                                                           rhs=PB[u][0:kr, hh, c * 128:(c + 1) * 128],
                                                           start=(ci == 0), stop=(ci == len(cs) - 1))
                                    return ins
                                fw.op("pe", fn, [t_Vb, t_PB[u]], [t_OB[u]])

                            def accum():
                                st0 = seg + dil * 128 * n
                                av = acc[:, 2 * hp:2 * hp + 2, sl(st0, 128, dil)]
                                if g == 0:
                                    act(av, OB_[u][:, :, 0:128], AF.Copy, [t_OB[u]], [t_acc[hp]])
                                else:
                                    dve_tt(av, av, OB_[u][:, :, 0:128], ALU.add, [t_OB[u], t_acc[hp]], [t_acc[hp]])
                            return qk, mid, pv, accum

                        units = []
                        for seg in range(dil):
                            for n in range(nq):
                                for hp in range(2):
                                    units.append(make_unit(g, dil, nq, segw, seg, n, hp, ucnt % 2))
                                    ucnt += 1
                        units[0][0]()
                        for ui in range(len(units)):
                            units[ui][1]()
                            if ui + 1 < len(units):
                                units[ui + 1][0]()
                            units[ui][2]()
                            if ui >= 1:
                                units[ui - 1][3]()
                        units[len(units) - 1][3]()
                    fw.barrier()

                with ExitStack() as p5:
                    wG = sb(p5, "wG", [128, 8, 2 * D], BF16)
                    wPA = sb(p5, "wPA", [128, 4, D], BF16)
                    wPB = sb(p5, "wPB", [128, 2, D], BF16)
                    t_wG = [T("wG0"), T("wG1")]
                    t_wPA, t_wPB = T("wPA"), T("wPB")
                    w_gate_v = w_gate.rearrange("(k p) n -> p k n", p=128)
                    for i in range(2):
                        fw.dma("pool", lambda e, i=i: e.dma_start(out=wG[:, :, i * D:(i + 1) * D], in_=w_gate_v[:, :, i * D:(i + 1) * D]), t_wG[i])
                    fw.dma("pool", lambda e: e.dma_start(out=wPA[:], in_=w_pa.rearrange("(k p) n -> p k n", p=128)), t_wPA)
                    fw.dma("pool", lambda e: e.dma_start(out=wPB[:], in_=w_pb.rearrange("(k p) n -> p k n", p=128)), t_wPB)
                    gsb = sb(p5, "gsb", [128, 16, 512], BF16)
                    t_gsb = [T("gsb%d" % i) for i in range(16)]
                    m1 = [sb(p5, "m1_%d" % i, [128, 512], F32) for i in range(2)]
                    t_m1 = [T("m1_%d" % i) for i in range(2)]
                    m2 = [sb(p5, "m2_%d" % i, [128, 512], F32) for i in range(2)]
                    t_m2 = [T("m2_%d" % i) for i in range(2)]
                    gps = [ps(p5, "gps%d" % i, [128, 512]) for i in range(2)]
                    t_gps = [T("gps%d" % i) for i in range(2)]
                    pap = [ps(p5, "pap%d" % i, [128, 512]) for i in range(2)]
                    t_pap = [T("pap%d" % i) for i in range(2)]
                    pbp = [ps(p5, "pbp%d" % i, [128, 512]) for i in range(2)]
                    t_pbp = [T("pbp%d" % i) for i in range(2)]
                    wad2 = sb(p5, "wad2", [128, 8, D], BF16)
                    t_wad2 = T("wad2")
                    ps_mod2 = ps(p5, "ps_mod2", [128, 512])[:, 0:32]
                    t_psm2 = T("psm2")
                    wadv2 = w_ada.rearrange("(k p) n -> p k n", p=128)

                    def late_load(g):
                        fw.dma("pool", lambda e: e.dma_start(out=wad2[:], in_=wadv2[:, :, g * D:(g + 1) * D]), t_wad2)

                    def late_mm(g):
                        groups = []
                        for j in range(8):
                            col = (g - 2) * 8 + j
                            groups.append((ps_mod2[:, col:col + 1],
                                           [(wad2[:, k, j * 128:(j + 1) * 128], cb[:, k:k + 1]) for k in range(8)]))
                        mm_multi(groups, [t_cb, t_wad2], [t_psm2])
                    for tb in range(4):
                        for h in range(4):
                            cs_ = slice(tb * 512, (tb + 1) * 512)
                            orow = slice(64 * (h % 2), 64 * (h % 2) + 64)
                            zrow = slice(64 - 64 * (h % 2), 128 - 64 * (h % 2))
                            dve_recip(rzb[orow, :], acc[zrow, h, cs_], [t_acc[h // 2]], [t_rzb])
                            dve_tt(ybT[orow, h // 2, cs_], acc[orow, h, cs_], rzb[orow, :], ALU.mult,
                                   [t_acc[h // 2], t_rzb], [t_yb[h][tb]])
                    if debug in ("yb", "yb0"):
                        fw.barrier()
                        o1 = dbg_out("dbg_yb", [128, 2, OWN])
                        td = T("d1")
                        fw.dma("pool", lambda e: e.dma_start(out=o1, in_=ybT[:]), td, t_yb[0][0])
                        return finish([td])
                    late_load(2)
                    for tb in range(4):
                        cs_ = slice(tb * 512, (tb + 1) * 512)
                        for gt in range(16):
                            b = gt % 2
                            mm(gps[b][:], [(wG[:, k, gt * 128:(gt + 1) * 128], uT_own[:, k, cs_]) for k in range(8)],
                               [t_wG[gt // 8], t_uown[tb]], t_gps[b])
                            act(gsb[:, gt, :], gps[b][:], AF.Sigmoid, [t_gps[b], t_vec], [t_gsb[gt]],
                                bias=vcol(V_BG + gt), scale=1.0)
                        for ot in range(8):
                            b = ot % 2
                            osl = slice(ot * 128, (ot + 1) * 128)
                            mm(pap[b][:], [(wPA[:, kc, osl], yaT[:, kc, cs_]) for kc in range(4)],
                               [t_wPA] + [t_ya[h][tb] for h in range(8)], t_pap[b])
                            mm(pbp[b][:], [(wPB[:, kc, osl], ybT[:, kc, cs_]) for kc in range(2)],
                               [t_wPB] + [t_yb[h][tb] for h in range(4)], t_pbp[b])
                            dve_tt(m1[b][:], pap[b][:], gsb[:, ot, :], ALU.mult, [t_pap[b], t_gsb[ot]], [t_m1[b]])
                            dve_tt(m2[b][:], pbp[b][:], gsb[:, 8 + ot, :], ALU.mult, [t_pbp[b], t_gsb[8 + ot]], [t_m2[b]])
                            fw.op("pool", lambda e, ot=ot, b=b, cs_=cs_: e.tensor_tensor(out=uT_own[:, ot, cs_], in0=m1[b][:], in1=m2[b][:], op=ALU.add),
                                  [t_m1[b], t_m2[b]], [t_uown[tb]])
                        late_mm(2 + tb)
                        if tb < 3:
                            late_load(3 + tb)
                    dve_tt(mod[:, 16:48], ps_mod2[:], vcol(V_BADA + 16, 32), ALU.add, [t_psm2, t_vec], [t_modhi])
                    dve_stt(A12[:, 8:16], mod[:, 32:40], 1.0, vcol(V_N2G, 8), ALU.add, ALU.mult, [t_modhi, t_vec], [t_A2])
                    fw.barrier()
                if debug == "mix":
                    o1 = dbg_out("dbg_mix", [128, 8, OWN])
                    td = T("d1")
                    fw.dma("pool", lambda e: e.dma_start(out=o1, in_=uT_own[:]), td, t_uown[0])
                    return finish([td])

        with ExitStack() as ctxB:
            xres = sb(ctxB, "xres", [128, 8, OWN], F32)
            t_x = [[T("x%d_%d" % (tb, k)) for k in range(8)] for tb in range(4)]
            GS = 6
            wI = [sb(ctxB, "wI%d" % i, [128, 8, 2, GS * 128], BF16) for i in range(2)]
            wOu = [sb(ctxB, "wOu%d" % i, [128, GS, D], BF16) for i in range(2)]
            t_wI = [[T("wIg%d" % i), T("wIu%d" % i)] for i in range(2)]
            t_wOu = [T("wOu%d" % i) for i in range(2)]
            w_fi_v = w_fi.rearrange("(k p) n -> p k n", p=128)
            w_fo_v = w_fo.rearrange("(j p) n -> p j n", p=128)

            def load_group(gi):
                j0, gs = FFN_GROUPS[gi]
                wb = gi % 2
                fw.dma("pool", lambda e: e.dma_start(out=wI[wb][:, :, 0, 0:gs * 128], in_=w_fi_v[:, :, j0 * 128:(j0 + gs) * 128]), t_wI[wb][0])
                fw.dma("pool", lambda e: e.dma_start(out=wI[wb][:, :, 1, 0:gs * 128],
                                                      in_=w_fi_v[:, :, DFF + j0 * 128:DFF + (j0 + gs) * 128]), t_wI[wb][1])
                fw.dma("pool", lambda e: e.dma_start(out=wOu[wb][:, 0:gs, :], in_=w_fo_v[:, j0:j0 + gs, :]), t_wOu[wb])

            with ExitStack() as p5b:
                wO = sb(p5b, "wO", [128, 8, D], BF16)
                t_wO = T("wO")
                fw.dma("pool", lambda e: e.dma_start(out=wO[:], in_=w_o.rearrange("(k p) n -> p k n", p=128)), t_wO)
                load_group(0)
                load_group(1)
                ops_ = [ps(p5b, "ops%d" % i, [128, 512]) for i in range(2)]
                t_ops = [T("ops%d" % i) for i in range(2)]
                nb5 = norm_bufs(p5b, "e")
                for tb in range(4):
                    cs_ = slice(tb * 512, (tb + 1) * 512)
                    fw.dma("sp", lambda e, tb=tb: e.dma_start(out=xres[:, :, tb * 512:(tb + 1) * 512],
                                                              in_=xTv[:, :, tb * 512:(tb + 1) * 512]), t_x[tb][0])
                    for k in range(1, 8):
                        t_x[tb][k].lw = t_x[tb][0].lw
                    for ot in range(8):
                        b = ot % 2
                        osl = slice(ot * 128, (ot + 1) * 128)
                        mm(ops_[b][:], [(wO[:, k, osl], uT_own[:, k, cs_]) for k in range(8)], [t_wO, t_uown[tb]], t_ops[b])
                        dve_stt(xres[:, ot, cs_], ops_[b][:], G1(ot), xres[:, ot, cs_], ALU.mult, ALU.add,
                                [t_ops[b], t_modhi, t_x[tb][ot]], [t_x[tb][ot]])
                    norm_block(nb5, xres[:, :, cs_], t_x[tb], lambda k: A12[:, 8 + k:9 + k], SH2, uT_own[:, :, cs_],
                               lambda k, t=t_uown[tb]: t, [t_A2, t_modhi])
                fw.barrier()
            if debug == "x1":
                o1 = dbg_out("dbg_x1", [128, 8, OWN])
                o2 = dbg_out("dbg_u2", [128, 8, OWN])
                td = [T("d1"), T("d2")]
                fw.dma("pool", lambda e: e.dma_start(out=o1, in_=xres[:]), td[0], t_x[0][0])
                fw.dma("pool", lambda e: e.dma_start(out=o2, in_=uT_own[:]), td[1], t_uown[0])
                return finish(td)

            with ExitStack() as p6:
                hbuf = [sb(p6, "hbuf%d" % i, [128, GS, 512], BF16) for i in range(2)]
                t_h = [[T("h%d_%d" % (i, j)) for j in range(GS)] for i in range(2)]
                sg = [sb(p6, "sg%d" % i, [128, 512], F32) for i in range(2)]
                t_sg = [T("sg%d" % i) for i in range(2)]
                hgp = [ps(p6, "hgp%d" % i, [128, 512]) for i in range(2)]
                t_hgp = [T("hgp%d" % i) for i in range(2)]
                hup = [ps(p6, "hup%d" % i, [128, 512]) for i in range(2)]
                t_hup = [T("hup%d" % i) for i in range(2)]
                yps = [ps(p6, "yps%d" % i, [128, 512]) for i in range(2)]
                t_yps = [T("yps%d" % i) for i in range(2)]
                nb6 = norm_bufs(p6, "f")
                t_out = [T("out%d" % i) for i in range(4)]
                cnt = 0
                for gi, (j0, gs) in enumerate(FFN_GROUPS):
                    wb = gi % 2
                    for tb in range(4):
                        cs_ = slice(tb * 512, (tb + 1) * 512)
                        hb = cnt % 2
                        cnt += 1
                        for jj in range(gs):
                            b = jj % 2
                            mm(hgp[b][:], [(wI[wb][:, k, 0, jj * 128:(jj + 1) * 128], uT_own[:, k, cs_]) for k in range(8)],
                               [t_wI[wb][0], t_uown[tb]], t_hgp[b])
                            mm(hup[b][:], [(wI[wb][:, k, 1, jj * 128:(jj + 1) * 128], uT_own[:, k, cs_]) for k in range(8)],
                               [t_wI[wb][1], t_uown[tb]], t_hup[b])
                            act(sg[b][:], hgp[b][:], AF.Silu, [t_hgp[b]], [t_sg[b]])
                            dve_tt(hbuf[hb][:, jj, :], sg[b][:], hup[b][:], ALU.mult, [t_sg[b], t_hup[b]], [t_h[hb][jj]])
                        for ot in range(8):
                            b = ot % 2
                            osl = slice(ot * 128, (ot + 1) * 128)
                            mm(yps[b][:], [(wOu[wb][:, jj, osl], hbuf[hb][:, jj, :]) for jj in range(gs)],
                               [t_wOu[wb]] + t_h[hb][0:gs], t_yps[b])
                            dve_stt(xres[:, ot, cs_], yps[b][:], G2(ot), xres[:, ot, cs_], ALU.mult, ALU.add,
                                    [t_yps[b], t_modhi, t_x[tb][ot]], [t_x[tb][ot]])
                        if gi == len(FFN_GROUPS) - 1:
                            norm_block(nb6, xres[:, :, cs_], t_x[tb], lambda k: vcol(V_FNG + k), None, xres[:, :, cs_],
                                       lambda k, tb=tb: t_x[tb][k], [])
                            for k in range(8):
                                pass
                            tsrc = T("osrc%d" % tb)
                            tsrc.lw = None
                            fw.wait_all("sp", t_x[tb])
                            fw.dma("sp", lambda e, tb=tb: e.dma_start(out=outTv[:, :, tb * 512:(tb + 1) * 512],
                                                                      in_=xres[:, :, tb * 512:(tb + 1) * 512]), t_out[tb], t_x[tb][0])
                    if gi + 2 < len(FFN_GROUPS):
                        load_group(gi + 2)
                fw.wait_all("sp", t_out)
                fw.barrier()
        fw.emit()
    return nc, dbg


def _rope_tab(pos, dim, theta):
    inv = (np.float32(theta) ** (-np.arange(0, dim, 2, dtype=np.float32) / np.float32(dim))).astype(np.float32)
    ang = pos.astype(np.float32)[:, None] * inv[None, :]
    return np.cos(ang).astype(np.float32), np.sin(ang).astype(np.float32)


def _const_mats():
    cmat = np.zeros((128, NCM), np.float32)
    cmat[:, C_ONES:C_ONES + 128] = 1.0
    for hb in (0, 64):
        cmat[hb:hb + 64, C_BO + hb:C_BO + hb + 64] = 1.0
    RA = np.zeros((128, 128), np.float32)
    RB = np.zeros((128, 128), np.float32)
    for hb in (0, 64):
        for half in (0, 32):
            for m in range(16):
                RA[hb + half + m + 16, hb + half + m] = -1.0
                RA[hb + half + m, hb + half + m + 16] = 1.0
        for m in range(8):
            RB[hb + m + 8, hb + m] = -1.0
            RB[hb + m, hb + m + 8] = 1.0
    cmat[:, C_RA:C_RA + 128] = RA
    cmat[:, C_RB:C_RB + 128] = RB
    b = np.arange(128)[:, None]
    a = np.arange(128)[None, :]
    cmat[:, C_MASK + 0:C_MASK + 128] = (b >= a + 64)
    cmat[:, C_MASK + 128:C_MASK + 256] = (np.abs(a - b) <= 64)
    cmat[:, C_MASK + 256:C_MASK + 384] = (a >= b + 64)
    return cmat


def _tables(perm):
    t = perm.astype(np.int64)
    cr, sr = _rope_tab(t // 64, 32, 10000.0)
    cc, sc = _rope_tab(t % 64, 32, 10000.0)
    CA = np.concatenate([cr, cr, cc, cc], axis=1).T
    SA = np.concatenate([sr, sr, sc, sc], axis=1).T
    tabA = np.stack([np.concatenate([CA, CA], 0), np.concatenate([SA, SA], 0)], 0).astype(np.float32)
    tb = t[:3072]
    cp, sp_ = _rope_tab(tb, 16, 500000.0)
    CB = np.ones((64, 3072), np.float32)
    SB = np.zeros((64, 3072), np.float32)
    CB[0:8] = cp.T
    CB[8:16] = cp.T
    SB[0:8] = sp_.T
    SB[8:16] = sp_.T
    tabB = np.stack([np.concatenate([CB, CB], 0), np.concatenate([SB, SB], 0)], 0).astype(np.float32)
    return np.ascontiguousarray(tabA), np.ascontiguousarray(tabB)


def _col(v, n):
    return np.asarray(v, np.float32).reshape(n, 128).T


def make_in_maps(inputs):
    f = lambda k: np.asarray(inputs[k], np.float32)
    x, c = f("x"), f("c")
    cmat = _const_mats()
    shared = {
        "cmat": cmat,
        "w_ada": np.ascontiguousarray(f("w_ada")[0]),
        "w_qkv": np.ascontiguousarray(f("w_qkv")[0]),
        "w_pa": np.ascontiguousarray(f("w_proj_a")[0]),
        "w_pb": np.ascontiguousarray(f("w_proj_b")[0]),
        "w_gate": np.ascontiguousarray(f("w_gate")[0]),
        "w_o": np.ascontiguousarray(f("w_o")[0]),
        "w_fi": np.ascontiguousarray(f("w_ffn_in")[0]),
        "w_fo": np.ascontiguousarray(f("w_ffn_out")[0]),
    }
    vecs = np.zeros((128, NV), np.float32)
    vecs[:, V_BADA:V_BADA + 48] = _col(f("b_ada")[0], 48)
    vecs[:, V_N1G:V_N1G + 8] = _col(f("norm1_g")[0], 8)
    vecs[:, V_N2G:V_N2G + 8] = _col(f("norm2_g")[0], 8)
    vecs[:, V_FNG:V_FNG + 8] = _col(f("final_norm_g"), 8)
    vecs[:, V_BG:V_BG + 16] = _col(f("b_gate")[0], 16)
    vecs[:, V_GQ] = np.tile(f("q_norm_a")[0], 2)
    vecs[:, V_GK] = np.tile(f("k_norm_a")[0], 2)
    perms = [np.arange(S), S - 1 - np.arange(S)]
    tabs = [_tables(p) for p in perms]
    in_maps = []
    for core in range(8):
        b, h = core // 2, core % 2
        m = dict(shared)
        m["xT"] = np.ascontiguousarray(x[b][perms[h]].T)
        m["cvec"] = np.ascontiguousarray(_col(c[b], 8))
        m["vecs"] = vecs
        m["tabA"], m["tabB"] = tabs[h]
        in_maps.append(m)
    return in_maps, perms


def kernel(**inputs):
    in_maps, perms = make_in_maps(inputs)
    nc, _ = build()
    res = run_bass_kernel_spmd(nc, in_maps, core_ids=list(range(8)))
    out = np.zeros((4, S, D), np.float32)
    for core in range(8):
        b, h = core // 2, core % 2
        oT = np.asarray(res.results[core]["outT"], np.float32)
        out[b, perms[h][:OWN], :] = oT.T
    return out
```

```python
import numpy as np
from contextlib import ExitStack
import concourse.bass as bass
import concourse.mybir as mybir
from concourse.bass_utils import run_bass_kernel_spmd

F32 = mybir.dt.float32
BF16 = mybir.dt.bfloat16
AF = mybir.ActivationFunctionType
ALU = mybir.AluOpType

SEM_LIMIT = 30000
S = 4096
OWN = 2048
D = 1024
DFF = 2816
EPS = 1e-6
FFN_GROUPS = [(0, 6), (6, 6), (12, 5), (17, 5)]


class T:
    __slots__ = ("name", "lw", "rd", "dsem", "dcnt", "shared")

    def __init__(self, name, shared=False):
        self.name = name
        self.lw = None
        self.rd = []
        self.dsem = None
        self.dcnt = 0
        self.shared = shared


class FW:
    ENG = ("pe", "act", "dve", "pool", "sp")

    def __init__(self, nc, stack):
        self.nc = nc
        self.stack = stack
        self.ops = {e: [] for e in self.ENG}
        self.sem = {}
        self.cnt = {}
        self.known = {e: {} for e in self.ENG}
        self.snap = {}
        self.nsem = 0
        self.pending_out = []
        for e in self.ENG:
            self._newsem(e)
        self.same_engine_sync = {"pe": False, "act": True, "dve": True, "pool": True, "sp": False}

    def _alloc_sem(self, name):
        self.nsem += 1
        return self.stack.enter_context(self.nc.semaphore("%s_%d" % (name, self.nsem)))

    def _newsem(self, e):
        self.sem[e] = self._alloc_sem("s_" + e)
        self.cnt[e] = 0

    def _learn(self, e, s, v):
        kn = self.known[e]
        kn[s] = v
        sn = self.snap.get((id(s), v))
        if sn:
            for s2, v2 in sn.items():
                if kn.get(s2, 0) < v2:
                    kn[s2] = v2

    def _waits(self, e, reads, writes):
        need = {}

        def add(ev):
            if ev is None:
                return
            s, v = ev
            if need.get(s, 0) < v:
                need[s] = v
        for t in reads:
            add(t.lw)
            if not t.shared:
                for ev in t.rd:
                    if ev[0] is not self.sem[e]:
                        add(ev)
        for t in writes:
            add(t.lw)
            for ev in t.rd:
                add(ev)
        out = []
        kn = self.known[e]
        for s, v in need.items():
            if s is self.sem[e] and not self.same_engine_sync[e]:
                continue
            if kn.get(s, 0) >= v:
                continue
            out.append((s, v))
            self._learn(e, s, v)
        return out

    def op(self, e, fn, reads=(), writes=()):
        waits = self._waits(e, reads, writes)
        if self.cnt[e] >= SEM_LIMIT:
            self._newsem(e)
        self.cnt[e] += 1
        s = self.sem[e]
        ev = (s, self.cnt[e])
        sn = dict(self.known[e])
        sn[s] = self.cnt[e]
        self.snap[(id(s), self.cnt[e])] = sn
        for t in reads:
            t.rd.append(ev)
        for t in writes:
            t.lw = ev
            t.rd = []
        self.ops[e].append((waits, fn, (s, 1)))
        return ev

    def dma(self, q, fn, dst, src=None):
        reads = [src] if src is not None else []
        waits = self._waits(q, reads, [dst])
        if dst.dsem is None:
            dst.dsem = self._alloc_sem("d")
        dst.dcnt += 16
        ev = (dst.dsem, dst.dcnt)
        self.snap[(id(dst.dsem), dst.dcnt)] = dict(self.known[q])
        for t in reads:
            t.rd.append(ev)
            self.pending_out.append(ev)
        dst.lw = ev
        dst.rd = []
        self.ops[q].append((waits, fn, (dst.dsem, 16)))
        return ev

    def wait_all(self, e, tiles):
        waits = self._waits(e, tiles, [])
        self.ops[e].append((waits, None, None))

    def barrier(self):
        evs = [(f, self.sem[f], self.cnt[f]) for f in self.ENG if self.cnt[f] > 0]
        for e in self.ENG:
            waits = []
            for f, s, v in evs:
                if f == e and not self.same_engine_sync[e]:
                    continue
                if self.known[e].get(s, 0) < v:
                    waits.append((s, v))
                    self._learn(e, s, v)
            for s, v in self.pending_out:
                if self.known[e].get(s, 0) < v:
                    waits.append((s, v))
                    self._learn(e, s, v)
            self.ops[e].append((waits, None, None))
        self.pending_out = []

    def emit(self):
        nc = self.nc
        with nc.Block() as block:
            def mk(e):
                def body(eng):
                    for waits, fn, inc in self.ops[e]:
                        for s, v in waits:
                            eng.wait_ge(s, v)
                        if fn is None:
                            continue
                        ins = fn(eng)
                        if inc is not None:
                            ins.then_inc(inc[0], inc[1])
                return body
            block.tensor(mk("pe"))
            block.scalar(mk("act"))
            block.vector(mk("dve"))
            block.gpsimd(mk("pool"))
            block.sync(mk("sp"))


NV = 48 + 8 + 8 + 8 + 16 + 2
V_BADA, V_N1G, V_N2G, V_FNG, V_BG, V_GQ, V_GK = 0, 48, 56, 64, 72, 88, 89
NCM = 4 * 128 + 384
C_ONES, C_BO, C_RA, C_RB, C_MASK = 0, 128, 256, 384, 512


def build(debug=None):
    nc = bass.Bass("TRN2", target_bir_lowering=False)

    def din(name, shape, dt=F32):
        return nc.dram_tensor(name, list(shape), dt, kind="ExternalInput").ap()

    xT = din("xT", [D, S])
    cvec = din("cvec", [128, 8])
    vecs = din("vecs", [128, NV])
    cmat = din("cmat", [128, NCM])
    tabA = din("tabA", [2, 128, S])
    tabB = din("tabB", [2, 128, 3072])
    w_ada = din("w_ada", [D, 6 * D])
    w_qkv = din("w_qkv", [D, 3072])
    w_pa = din("w_pa", [512, D])
    w_pb = din("w_pb", [256, D])
    w_gate = din("w_gate", [D, 2 * D])
    w_o = din("w_o", [D, D])
    w_fi = din("w_fi", [D, 2 * DFF])
    w_fo = din("w_fo", [DFF, D])
    outT = nc.dram_tensor("outT", [D, OWN], F32, kind="ExternalOutput").ap()
    dbg = {}

    def dbg_out(name, shape):
        dbg[name] = nc.dram_tensor(name, list(shape), F32, kind="ExternalOutput").ap()
        return dbg[name]

    xTv = xT.rearrange("(k p) t -> p k t", p=128)
    outTv = outT.rearrange("(k p) t -> p k t", p=128)
    w_qkv_v = w_qkv.rearrange("(k p) n -> p k n", p=128)

    with ExitStack() as top:
        fw = FW(nc, top)

        def sb(ctx, name, shape, dt):
            return ctx.enter_context(nc.sbuf_tensor(name, list(shape), dt))

        def ps(ctx, name, shape, dt=F32):
            return ctx.enter_context(nc.psum_tensor(name, list(shape), dt))

        def mm(out_ap, pairs, reads, twrite):
            def fn(e):
                n = len(pairs)
                ins = None
                for i, (l, r) in enumerate(pairs):
                    ins = e.matmul(out_ap, lhsT=l, rhs=r, start=(i == 0), stop=(i == n - 1))
                return ins
            return fw.op("pe", fn, reads, [twrite])

        def mm_multi(groups, reads, writes):
            def fn(e):
                ins = None
                for out_ap, pairs in groups:
                    n = len(pairs)
                    for i, (l, r) in enumerate(pairs):
                        ins = e.matmul(out_ap, lhsT=l, rhs=r, start=(i == 0), stop=(i == n - 1))
                return ins
            return fw.op("pe", fn, reads, writes)

        def act(out, in_, func, reads, writes, **kw):
            return fw.op("act", lambda e: e.activation(out=out, in_=in_, func=func, **kw), reads, writes)

        def dve_tt(out, in0, in1, op, reads, writes, eng="dve"):
            return fw.op(eng, lambda e: e.tensor_tensor(out=out, in0=in0, in1=in1, op=op), reads, writes)

        def dve_stt(out, in0, scalar, in1, op0, op1, reads, writes):
            return fw.op("dve", lambda e: e.scalar_tensor_tensor(out=out, in0=in0, scalar=scalar, in1=in1,
                                                                op0=op0, op1=op1), reads, writes)

        def dve_recip(out, in_, reads, writes):
            return fw.op("dve", lambda e: e.reciprocal(out=out, in_=in_), reads, writes)

        def sl(start, count, step):
            return slice(start, start + step * (count - 1) + 1, step)

        def finish(tds, q="pool"):
            fw.wait_all(q, tds)
            fw.emit()
            return nc, dbg

        vec_sb = sb(top, "vec_sb", [128, NV], F32)
        cm_f = sb(top, "cm_f", [128, NCM], F32)
        cm = sb(top, "cm", [128, NCM], BF16)
        mod = sb(top, "mod", [128, 48], F32)
        A12 = sb(top, "A12", [128, 16], F32)
        uT_own = sb(top, "uT_own", [128, 8, OWN], BF16)
        t_vec, t_cmf, t_cm, t_mod, t_A = T("vec", shared=True), T("cmf"), T("cm", shared=True), T("mod", shared=True), T("A12", shared=True)
        t_modhi, t_A2 = T("mod_hi", shared=True), T("A12_hi", shared=True)
        cb = sb(top, "cb", [128, 8], BF16)
        t_cb = T("cb", shared=True)
        t_uown = [T("uown%d" % i) for i in range(4)]

        ONES = cm[:, C_ONES:C_ONES + 128]
        BO = cm[:, C_BO:C_BO + 128]
        RA = cm[:, C_RA:C_RA + 128]
        RB = cm[:, C_RB:C_RB + 128]
        MASK = cm[:, C_MASK:C_MASK + 384]

        fw.dma("sp", lambda e: e.dma_start(out=vec_sb[:], in_=vecs), t_vec)
        fw.dma("sp", lambda e: e.dma_start(out=cm_f[:], in_=cmat), t_cmf)
        act(cm[:], cm_f[:], AF.Copy, [t_cmf], [t_cm])

        def vcol(c, n=1):
            return vec_sb[:, c:c + n]

        SH1 = lambda k: mod[:, k:k + 1]
        SH2 = lambda k: mod[:, 24 + k:25 + k]
        G1 = lambda k: mod[:, 16 + k:17 + k]
        G2 = lambda k: mod[:, 40 + k:41 + k]

        def norm_bufs(ctx, tag):
            sq = [sb(ctx, "nrm_sq%s%d" % (tag, i), [128, 512], BF16) for i in range(2)]
            ssp = ps(ctx, "nrm_ssp" + tag, [128, 512])
            rs = sb(ctx, "nrm_rs" + tag, [128, 512], F32)
            rstd = sb(ctx, "nrm_rstd" + tag, [128, 512], F32)
            tmp = [sb(ctx, "ntmp%s%d" % (tag, i), [128, 512], F32) for i in range(2)]
            return dict(sq=sq, t_sq=[T("sq0" + tag), T("sq1" + tag)], ssp=ssp, t_ssp=T("ssp" + tag), rs=rs, t_rs=T("rs" + tag),
                        rstd=rstd, t_rstd=T("rstd" + tag), tmp=tmp, t_tmp=[T("tmp0" + tag), T("tmp1" + tag)])

        def norm_block(nb, src, t_src, a_col, sh_col, dst, t_dst, cdeps):
            for k in range(8):
                i = k % 2
                act(nb["sq"][i][:], src[:, k, :], AF.Square, t_src, [nb["t_sq"][i]])
                fw.op("pe", lambda e, k=k, i=i: e.matmul(nb["ssp"][:], lhsT=ONES, rhs=nb["sq"][i][:], start=(k == 0), stop=(k == 7)),
                      [nb["t_sq"][i], t_cm], [nb["t_ssp"]])
            act(nb["rs"][:], nb["ssp"][:], AF.Sqrt, [nb["t_ssp"]], [nb["t_rs"]], scale=1.0 / D, bias=EPS)
            dve_recip(nb["rstd"][:], nb["rs"][:], [nb["t_rs"]], [nb["t_rstd"]])
            for k in range(8):
                i = k % 2
                rd = t_src + [nb["t_rstd"], t_vec] + cdeps
                if sh_col is None:
                    dve_stt(dst[:, k, :], src[:, k, :], a_col(k), nb["rstd"][:], ALU.mult, ALU.mult, rd, [t_dst(k)])
                else:
                    dve_stt(nb["tmp"][i][:], src[:, k, :], a_col(k), nb["rstd"][:], ALU.mult, ALU.mult, rd, [nb["t_tmp"][i]])
                    act(dst[:, k, :], nb["tmp"][i][:], AF.Identity, [nb["t_tmp"][i]] + cdeps, [t_dst(k)], bias=sh_col(k), scale=1.0)

        with ExitStack() as ctxA:
            yaT = sb(ctxA, "yaT", [128, 4, OWN], BF16)
            ybT = sb(ctxA, "ybT", [128, 2, OWN], BF16)
            t_ya = [[T("ya%d_%d" % (h, q)) for q in range(4)] for h in range(8)]
            t_yb = [[T("yb%d_%d" % (h, q)) for q in range(4)] for h in range(4)]

            with ExitStack() as attn:
                uT_elo = sb(attn, "uT_elo", [128, 8, 1024], BF16)
                t_uelo = [T("uelo%d" % i) for i in range(2)]

                with ExitStack() as pha:
                    uT_ehi = sb(pha, "uT_ehi", [128, 8, 1024], BF16)
                    t_uehi = [T("uehi%d" % i) for i in range(2)]
                    wA = sb(pha, "wA", [128, 8, 768], BF16)
                    t_wA = T("wA")

                    def ublk(blk):
                        if blk < 4:
                            return uT_own[:, :, blk * 512:(blk + 1) * 512], t_uown[blk]
                        if blk < 6:
                            return uT_elo[:, :, (blk - 4) * 512:(blk - 3) * 512], t_uelo[blk - 4]
                        return uT_ehi[:, :, (blk - 6) * 512:(blk - 5) * 512], t_uehi[blk - 6]

                    with ExitStack() as p0:
                        cv = sb(p0, "cv", [128, 8], F32)
                        wad = sb(p0, "wad", [128, 8, 2 * D], BF16)
                        ps_mod = ps(p0, "ps_mod", [128, 512])[:, 0:16]
                        t_cv, t_psm = T("cv"), T("psm")
                        t_wad = [T("wad%d" % i) for i in range(2)]
                        fw.dma("sp", lambda e: e.dma_start(out=cv[:], in_=cvec), t_cv)
                        wadv = w_ada.rearrange("(k p) n -> p k n", p=128)
                        for g in (1, 0):
                            fw.dma("pool", lambda e, g=g: e.dma_start(out=wad[:, :, g * D:(g + 1) * D],
                                                                     in_=wadv[:, :, g * D:(g + 1) * D]), t_wad[g])
                        t_wAq = [T("wAq%d" % i) for i in range(8)]
                        for hd in range(8):
                            slot = (hd % 4) * 2 + hd // 4
                            fw.dma("pool", lambda e, hd=hd, slot=slot: e.dma_start(
                                out=wA[:, :, slot * 64:(slot + 1) * 64], in_=w_qkv_v[:, :, hd * 64:(hd + 1) * 64]), t_wAq[hd])
                        fw.dma("pool", lambda e: e.dma_start(out=wA[:, :, 512:768], in_=w_qkv_v[:, :, 512:768]), t_wA)
                        act(cb[:], cv[:], AF.Silu, [t_cv], [t_cb])
                        for g in (1, 0):
                            groups = []
                            for j in range(g * 8, g * 8 + 8):
                                groups.append((ps_mod[:, j:j + 1],
                                               [(wad[:, k, j * 128:(j + 1) * 128], cb[:, k:k + 1]) for k in range(8)]))
                            mm_multi(groups, [t_cb, t_wad[g]], [t_psm])
                        dve_tt(mod[:, 0:16], ps_mod[:], vcol(V_BADA, 16), ALU.add, [t_psm, t_vec], [t_mod])
                        dve_stt(A12[:, 0:8], mod[:, 8:16], 1.0, vcol(V_N1G, 8), ALU.add, ALU.mult, [t_mod, t_vec], [t_A])
                        fw.barrier()
                    if debug == "mod":
                        o = dbg_out("dbg_mod", [128, 64])
                        td, td2 = T("dbgmod"), T("dbgA")
                        fw.dma("sp", lambda e: e.dma_start(out=o[:, 0:16], in_=mod[:, 0:16]), td, t_mod)
                        fw.dma("sp", lambda e: e.dma_start(out=o[:, 48:56], in_=A12[:, 0:8]), td2, t_A)
                        return finish([td, td2], "sp")

                    with ExitStack() as p1:
                        xb = [sb(p1, "xb%d" % i, [128, 8, 512], F32) for i in range(2)]
                        t_xb = [T("xb%d" % i) for i in range(2)]
                        nb = [norm_bufs(p1, "a"), norm_bufs(p1, "b")]
                        for blk in range(8):
                            i = blk % 2
                            fw.dma("sp", lambda e, blk=blk, i=i: e.dma_start(out=xb[i][:], in_=xTv[:, :, blk * 512:(blk + 1) * 512]), t_xb[i])
                            dst, t_dst = ublk(blk)
                            norm_block(nb[i], xb[i][:], [t_xb[i]], lambda k: A12[:, k:k + 1], SH1, dst, lambda k, t=t_dst: t, [t_A, t_mod])
                        fw.barrier()
                    if debug == "u":
                        o = dbg_out("dbg_u", [128, 8, S])
                        tds = [T("dbgu%d" % i) for i in range(8)]
                        for blk in range(8):
                            src, t_s = ublk(blk)
                            fw.dma("pool", lambda e, blk=blk, src=src: e.dma_start(out=o[:, :, blk * 512:(blk + 1) * 512], in_=src), tds[blk], t_s)
                        return finish(tds)

                    with ExitStack() as p23:
                        QaT = sb(p23, "QaT", [128, 4, OWN], BF16)
                        KaT = sb(p23, "KaT", [128, S], BF16)
                        Va = sb(p23, "Va", [128, 32, 3, 64], BF16)
                        t_Qa = [[T("Qa%d_%d" % (i, q)) for q in range(4)] for i in range(4)]
                        t_Ka = [T("Ka%d" % q) for q in range(8)]
                        t_Va = [T("Va%d" % q) for q in range(8)]
                        t_Vones = T("Vones")
                        fw.op("pool", lambda e: e.memset(Va[:, :, 1, :], 1.0), [], [t_Vones])
                        with ExitStack() as p2:
                            tab = [sb(p2, "tabA%d" % i, [128, 2, 512], F32) for i in range(2)]
                            t_tab = [T("tabA%d" % i) for i in range(2)]

                            def two(name, shape, dt, psum=False):
                                if psum:
                                    return [ps(p2, "%s%d" % (name, i), shape) for i in range(2)], [T("%s%d" % (name, i)) for i in range(2)]
                                return [sb(p2, "%s%d" % (name, i), shape, dt) for i in range(2)], [T("%s%d" % (name, i)) for i in range(2)]
                            qps, t_qps = two("qps", [128, 512], F32, True)
                            ssq, t_ssq = two("ssq", [128, 512], F32, True)
                            rqp, t_rqp = two("rqp", [128, 512], F32, True)
                            vps, t_vps = two("vps", [128, 4, 128], F32, True)
                            sqb, t_sqb = two("sqb", [128, 512], BF16)
                            qg, t_qg = two("qg", [128, 512], BF16)
                            rsq, t_rsq = two("rsq", [128, 512], F32)
                            rq, t_rq = two("rq", [128, 512], F32)
                            t1, t_t1 = two("t1_", [128, 512], F32)
                            t2, t_t2 = two("t2_", [128, 512], F32)
                            cnt = 0
                            for blk in range(8):
                                ti = blk % 2
                                fw.dma("sp", lambda e, blk=blk, ti=ti: e.dma_start(
                                    out=tab[ti][:], in_=tabA[:, :, blk * 512:(blk + 1) * 512].rearrange("c p t -> p c t")), t_tab[ti])
                                u_ap, t_u = ublk(blk)
                                tiles = ([("q", i) for i in range(4)] if blk < 4 else []) + [("k", 0)]
                                for kind, i in tiles:
                                    b = cnt % 2
                                    cnt += 1
                                    if kind == "q":
                                        def lw(k, i=i):
                                            return wA[:, k, i * 128:(i + 1) * 128]
                                        gain = vcol(V_GQ)
                                        dst = QaT[:, i, blk * 512:(blk + 1) * 512]
                                        t_dst = t_Qa[i][blk]
                                    else:
                                        def lw(k):
                                            return wA[:, k, 512:640]
                                        gain = vcol(V_GK)
                                        dst = KaT[:, blk * 512:(blk + 1) * 512]
                                        t_dst = t_Ka[blk]
                                    mm(qps[b][:], [(lw(k), u_ap[:, k, :]) for k in range(8)], [t_wA, t_u] + t_wAq, t_qps[b])
                                    act(sqb[b][:], qps[b][:], AF.Square, [t_qps[b]], [t_sqb[b]])
                                    act(qg[b][:], qps[b][:], AF.Copy, [t_qps[b], t_vec], [t_qg[b]], scale=gain)
                                    mm(ssq[b][:], [(BO, sqb[b][:])], [t_sqb[b], t_cm], t_ssq[b])
                                    mm(rqp[b][:], [(RA, qg[b][:])], [t_qg[b], t_cm], t_rqp[b])
                                    act(rsq[b][:], ssq[b][:], AF.Sqrt, [t_ssq[b]], [t_rsq[b]], scale=1.0 / 64, bias=EPS)
                                    dve_recip(rq[b][:], rsq[b][:], [t_rsq[b]], [t_rq[b]])
                                    dve_stt(t1[b][:], qps[b][:], gain, tab[ti][:, 0, :], ALU.mult, ALU.mult,
                                            [t_qps[b], t_tab[ti], t_vec], [t_t1[b]])
                                    dve_tt(t2[b][:], rqp[b][:], tab[ti][:, 1, :], ALU.mult, [t_rqp[b], t_tab[ti]], [t_t2[b]])
                                    dve_tt(t1[b][:], t1[b][:], t2[b][:], ALU.add, [t_t1[b], t_t2[b]], [t_t1[b]], eng="pool")
                                    dve_tt(dst, t1[b][:], rq[b][:], ALU.mult, [t_t1[b], t_rq[b]], [t_dst], eng="pool")
                                vb = blk % 2
                                groups = []
                                for j in range(4):
                                    groups.append((vps[vb][:, j, :],
                                                   [(u_ap[:, k, j * 128:(j + 1) * 128], wA[:, k, 640:768]) for k in range(8)]))
                                mm_multi(groups, [t_wA, t_u], [t_vps[vb]])
                                for v in range(2):
                                    act(Va[:, blk * 4:blk * 4 + 4, 2 * v, :], vps[vb][:, :, v * 64:(v + 1) * 64], AF.Copy,
                                        [t_vps[vb], t_Vones], [t_Va[blk]])
                            fw.barrier()
                        if debug == "qkv_a":
                            o1 = dbg_out("dbg_Qa", [128, 4, OWN])
                            o2 = dbg_out("dbg_Ka", [128, S])
                            o3 = dbg_out("dbg_Va", [128, 32 * 3 * 64])
                            td = [T("d1"), T("d2"), T("d3")]
                            fw.dma("pool", lambda e: e.dma_start(out=o1, in_=QaT[:]), td[0], t_Qa[0][0])
                            fw.dma("pool", lambda e: e.dma_start(out=o2, in_=KaT[:]), td[1], t_Ka[0])
                            fw.dma("pool", lambda e: e.dma_start(out=o3, in_=Va[:].rearrange("p a b c -> p (a b c)")), td[2], t_Va[0])
                            return finish(td)

                        with ExitStack() as p3:
                            NS = 3
                            Sps = [ps(p3, "Sps%d" % i, [128, 2, 512]) for i in range(NS)]
                            t_S = [T("Sps%d" % i) for i in range(NS)]
                            Ops = ps(p3, "Ops0", [128, 2, 512])
                            t_O = T("Ops0")
                            Osb = [sb(p3, "Osb%d" % i, [128, 2, 512], F32) for i in range(2)]
                            t_Osb = [T("Osb%d" % i) for i in range(2)]
                            NPT = 3
                            PT = [sb(p3, "PT%d" % i, [128, 2, 512], BF16) for i in range(NPT)]
                            t_PT = [T("PT%d" % i) for i in range(NPT)]
                            rz = [sb(p3, "rz%d" % i, [128, 2, 512], F32) for i in range(2)]
                            t_rz = [T("rz%d" % i) for i in range(2)]
                            npair = 1 if debug == "ya1" else 4
                            steps = [(i, qb, kt) for i in range(npair) for qb in range(4) for kt in range(32)]
                            Vav = [Va[:, kt, :, :].rearrange("p a b -> p (a b)") for kt in range(32)]

                            def qk(sidx):
                                i, qb, kt = steps[sidx]
                                b = sidx % NS
                                qs = slice(qb * 512, (qb + 1) * 512)
                                ks = slice(kt * 128, (kt + 1) * 128)
                                groups = [(Sps[b][:, 0, :], [(KaT[0:64, ks], QaT[0:64, i, qs])]),
                                          (Sps[b][:, 1, :], [(KaT[64:128, ks], QaT[64:128, i, qs])])]
                                mm_multi(groups, [t_Ka[kt // 4], t_Qa[i][qb]], [t_S[b]])

                            def ex(sidx):
                                b = sidx % NS
                                p = sidx % NPT
                                act(PT[p][:], Sps[b][:], AF.Exp, [t_S[b]], [t_PT[p]], scale=0.125)

                            def pv(sidx):
                                i, qb, kt = steps[sidx]
                                p = sidx % NPT
                                o2 = (sidx // 32) % 2

                                def fn(e):
                                    ins = None
                                    for j in range(2):
                                        lhs = Vav[kt][:, 64 * j:64 * j + 128]
                                        ins = e.matmul(Ops[:, j, :], lhsT=lhs, rhs=PT[p][:, j, :],
                                                       start=(kt == 0), stop=(kt == 31))
                                    return ins
                                fw.op("pe", fn, [t_Va[kt // 4], t_PT[p]], [t_O])
                                if kt == 31:
                                    fw.op("dve", lambda e: e.tensor_copy(out=Osb[o2][:], in_=Ops[:]), [t_O], [t_Osb[o2]])
                                    for j in range(2):
                                        head = i + 4 * j
                                        orow = slice(64 * j, 64 * j + 64)
                                        zrow = slice(64 - 64 * j, 128 - 64 * j)
                                        dve_recip(rz[o2][orow, j, :], Osb[o2][zrow, j, :], [t_Osb[o2]], [t_rz[o2]])
                                        pb_ = 64 * (head % 2)
                                        dve_tt(yaT[pb_:pb_ + 64, head // 2, qb * 512:(qb + 1) * 512], Osb[o2][orow, j, :],
                                               rz[o2][orow, j, :], ALU.mult, [t_Osb[o2], t_rz[o2]], [t_ya[head][qb]])

                            for sidx in range(min(NS, len(steps))):
                                qk(sidx)
                            for sidx in range(len(steps)):
                                ex(sidx)
                                pv(sidx)
                                if sidx + NS < len(steps):
                                    qk(sidx + NS)
                            fw.barrier()
                    if debug in ("ya", "ya1"):
                        o1 = dbg_out("dbg_ya", [128, 4, OWN])
                        td = T("d1")
                        fw.dma("pool", lambda e: e.dma_start(out=o1, in_=yaT[:]), td, t_ya[0][0])
                        return finish([td])

                acc = sb(attn, "acc", [128, 4, OWN], F32)
                rzb = sb(attn, "rzb", [128, 512], F32)
                with ExitStack() as p4:
                    wB = sb(p4, "wB", [128, 8, 768], BF16)
                    t_wB = [T("wBq"), T("wBk"), T("wBv")]
                    QbT = sb(p4, "QbT", [128, 2, OWN], BF16)
                    KbT = sb(p4, "KbT", [128, 2, 4096], BF16)
                    Vb = sb(p4, "Vb", [128, 32, 6, 64], BF16)
                    t_Qb = [T("Qb%d" % i) for i in range(2)]
                    t_Kb = [T("Kb%d" % i) for i in range(2)]
                    t_Vb = T("Vb")
                    t_acc = [T("acc%d" % i) for i in range(2)]
                    tabb = [sb(p4, "tabB%d" % i, [128, 2, 512], F32) for i in range(2)]
                    t_tabb = [T("tabB%d" % i) for i in range(2)]
                    pps = [ps(p4, "pps%d" % i, [128, 512]) for i in range(2)]
                    t_pps = [T("pps%d" % i) for i in range(2)]
                    rps = [ps(p4, "rps%d" % i, [128, 512]) for i in range(2)]
                    t_rps = [T("rps%d" % i) for i in range(2)]
                    OB_ = [ps(p4, "OB%d" % i, [128, 2, 256]) for i in range(2)]
                    t_OB = [T("OB%d" % i) for i in range(2)]
                    Sbuf = [(pps, t_pps), (rps, t_rps)]
                    pbb = [sb(p4, "pbb%d" % i, [128, 512], BF16) for i in range(2)]
                    t_pbb = [T("pbb%d" % i) for i in range(2)]
                    b1 = [sb(p4, "b1_%d" % i, [128, 512], F32) for i in range(2)]
                    t_b1 = [T("b1_%d" % i) for i in range(2)]
                    b2 = [sb(p4, "b2_%d" % i, [128, 512], F32) for i in range(2)]
                    t_b2 = [T("b2_%d" % i) for i in range(2)]
                    EB = [sb(p4, "EB%d" % i, [128, 2, 384], BF16) for i in range(2)]
                    t_EB = [[T("EB%d_%d" % (i, h)) for h in range(2)] for i in range(2)]
                    PB = [sb(p4, "PB%d" % i, [128, 2, 384], BF16) for i in range(2)]
                    t_PB = [[T("PB%d_%d" % (i, h)) for h in range(2)] for i in range(2)]
                    t_rzb = T("rzb")
                    fw.op("pool", lambda e: e.memset(Vb[:, :, 1:5:3, :], 1.0), [], [t_Vb])
                    def p4_exit():
                        fw.barrier()
                        o1 = dbg_out("dbg_ya", [128, 4, OWN])
                        td = T("d1")
                        fw.dma("pool", lambda e: e.dma_start(out=o1, in_=yaT[:]), td, t_ya[0][0])
                        return finish([td])
                    if debug == "p4_a":
                        return p4_exit()
                    cnt = 0
                    ucnt = 0
                    glist = [(0, 1), (1, 4), (2, 16)]
                    if debug == "yb0":
                        glist = glist[:1]
                    for g, dil in glist:
                        nq = 16 // dil
                        segw = (nq + 1) * 128
                        for part in range(3):
                            c0 = 768 + part * 768 + g * 256
                            fw.dma("pool", lambda e, part=part, c0=c0: e.dma_start(
                                out=wB[:, :, part * 256:(part + 1) * 256], in_=w_qkv_v[:, :, c0:c0 + 256]), t_wB[part])
                        Kview = [KbT[:, ft, 0:dil * segw].rearrange("p (s c) -> p c s", s=dil) for ft in range(2)]
                        for ft in range(2):
                            padv = KbT[:, ft, 0:dil * segw].rearrange("p (s c) -> p s c", s=dil)[:, :, nq * 128 + 64:nq * 128 + 128]
                            fw.op("pool", lambda e, padv=padv: e.memset(padv, 0.0), [], [t_Kb[ft]])
                        Qview = [QbT[:, ft, :].rearrange("p (s c) -> p c s", s=dil) for ft in range(2)]
                        nhalo = 64 * dil
                        blocks = [("own", tb, 512) for tb in range(4)] + \
                                 [("halo", hb, min(512, nhalo)) for hb in range((nhalo + 511) // 512)]
                        if debug == "yb_pown":
                            blocks = [bk for bk in blocks if bk[0] == "own"]
                        def make_tile(kind, bi, blen, what, ft, u_ap, t_u, ti, b, dil, nq, Qview, Kview):
                            pi = 0 if what == "q" else 1
                            wcol = pi * 256 + ft * 128

                            def proj():
                                mm(pps[b][:, 0:blen], [(wB[:, k, wcol:wcol + 128], u_ap[:, k, :]) for k in range(8)],
                                   [t_wB[pi], t_u], t_pps[b])

                            def rest():
                                act(pbb[b][:, 0:blen], pps[b][:, 0:blen], AF.Copy, [t_pps[b]], [t_pbb[b]])
                                mm(rps[b][:, 0:blen], [(RB, pbb[b][:, 0:blen])], [t_pbb[b], t_cm], t_rps[b])
                                dve_tt(b1[b][:, 0:blen], pps[b][:, 0:blen], tabb[ti][:, 0, 0:blen], ALU.mult,
                                       [t_pps[b], t_tabb[ti]], [t_b1[b]])
                                dve_tt(b2[b][:, 0:blen], rps[b][:, 0:blen], tabb[ti][:, 1, 0:blen], ALU.mult,
                                       [t_rps[b], t_tabb[ti]], [t_b2[b]])
                                j0 = (bi * 512) // dil
                                nj = blen // dil
                                if what == "q":
                                    dst = Qview[ft][:, j0:j0 + nj, :]
                                    t_dst = t_Qb[ft]
                                else:
                                    cbase = j0 if kind == "own" else nq * 128 + j0
                                    dst = Kview[ft][:, cbase:cbase + nj, :]
                                    t_dst = t_Kb[ft]
                                in0 = b1[b][:, 0:blen].rearrange("p (j r) -> p j r", r=dil)
                                in1 = b2[b][:, 0:blen].rearrange("p (j r) -> p j r", r=dil)
                                fw.op("pool", lambda e: e.tensor_tensor(out=dst, in0=in0, in1=in1, op=ALU.add),
                                      [t_b1[b], t_b2[b]], [t_dst])
                            return proj, rest

                        def make_tab_dma(ti, tcol, blen):
                            def go():
                                fw.dma("sp", lambda e: e.dma_start(
                                    out=tabb[ti][:, :, 0:blen], in_=tabB[:, :, tcol:tcol + blen].rearrange("c p t -> p c t")), t_tabb[ti])
                            return go

                        psteps = []
                        for kind, bi, blen in blocks:
                            if kind == "own":
                                u_ap, t_u = uT_own[:, :, bi * 512:(bi + 1) * 512], t_uown[bi]
                                tcol = bi * 512
                            else:
                                u_ap, t_u = uT_elo[:, :, bi * 512:bi * 512 + blen], t_uelo[bi]
                                tcol = OWN + bi * 512
                            ti = cnt % 2
                            psteps.append(("dma", make_tab_dma(ti, tcol, blen)))
                            tiles = ([("q", 0), ("q", 1)] if kind == "own" else []) + [("k", 0), ("k", 1)]
                            for what, ft in tiles:
                                psteps.append(("tile",) + make_tile(kind, bi, blen, what, ft, u_ap, t_u, ti, cnt % 2, dil, nq, Qview, Kview))
                                cnt += 1
                        tile_pos = [i for i, s in enumerate(psteps) if s[0] == "tile"]
                        done_proj = set()
                        for pos, s in enumerate(psteps):
                            if s[0] == "dma":
                                s[1]()
                                continue
                            if pos not in done_proj:
                                s[1]()
                                done_proj.add(pos)
                            nxt = next((p for p in tile_pos if p > pos), None)
                            if nxt is not None and nxt not in done_proj:
                                psteps[nxt][1]()
                                done_proj.add(nxt)
                            s[2]()
                        if debug in ("yb_p", "yb_pown"):
                            fw.barrier()
                            o1 = dbg_out("dbg_Qb", [128, 2, OWN])
                            o2 = dbg_out("dbg_Kb", [128, 2, 4096])
                            td = [T("d1"), T("d2")]
                            fw.dma("pool", lambda e: e.dma_start(out=o1, in_=QbT[:]), td[0], t_Qb[0])
                            fw.dma("pool", lambda e: e.dma_start(out=o2, in_=KbT[:]), td[1], t_Kb[0])
                            return finish(td)
                        vt = 0
                        for seg in range(dil):
                            for m in range(nq + 1):
                                tile = seg * (nq + 1) + m
                                ob = vt % 2
                                vt += 1
                                if m < nq:
                                    st0 = seg + dil * 128 * m
                                    lhs = lambda k, st0=st0, dil=dil: uT_own[:, k, sl(st0, 128, dil)]
                                    t_u = t_uown
                                    mrows = 128
                                else:
                                    lhs = lambda k, seg=seg, dil=dil: uT_elo[:, k, sl(seg, 64, dil)]
                                    t_u = t_uelo
                                    mrows = 64
                                mm(OB_[ob][0:mrows, 0, :], [(lhs(k), wB[:, k, 512:768]) for k in range(8)], [t_wB[2]] + t_u, t_OB[ob])
                                for hp2 in range(2):
                                    act(Vb[0:mrows, tile, 3 * hp2:3 * hp2 + 3:2, :],
                                        OB_[ob][0:mrows, 0, hp2 * 128:(hp2 + 1) * 128].rearrange("p (h d) -> p h d", h=2),
                                        AF.Copy, [t_OB[ob]], [t_Vb])
                        if debug == "yb_v":
                            fw.barrier()
                            o1 = dbg_out("dbg_Vb", [128, 32 * 6 * 64])
                            td = [T("d1")]
                            fw.dma("pool", lambda e: e.dma_start(out=o1, in_=Vb[:].rearrange("p a b c -> p (a b c)")), td[0], t_Vb)
                            return finish(td)
                        def make_unit(g, dil, nq, segw, seg, n, hp, u):
                            St, t_St = Sbuf[u]
                            cs = [c for c in range(3) if n - 1 + c >= 0]
                            c_lo = cs[0] * 128
                            qcol = seg * nq * 128 + n * 128

                            def qk():
                                groups = []
                                for hh in range(2):
                                    pr = slice(64 * hh, 64 * hh + 64)
                                    for c in cs:
                                        m = n - 1 + c
                                        kc = seg * segw + m * 128
                                        groups.append((St[hh][:, c * 128:(c + 1) * 128],
                                                       [(KbT[pr, hp, kc:kc + 128], QbT[pr, hp, qcol:qcol + 128])]))
                                mm_multi(groups, [t_Kb[hp], t_Qb[hp]], t_St)

                            def mid():
                                for hh in range(2):
                                    act(EB[u][:, hh, c_lo:384], St[hh][:, c_lo:384], AF.Exp, [t_St[hh]], [t_EB[u][hh]], scale=0.125)
                                for hh in range(2):
                                    dve_tt(PB[u][:, hh, c_lo:384], EB[u][:, hh, c_lo:384], MASK[:, c_lo:384], ALU.mult,
                                           [t_EB[u][hh], t_cm], [t_PB[u][hh]])

                            def pv():
                                def fn(e):
                                    ins = None
                                    for hh in range(2):
                                        head = 2 * hp + hh
                                        for ci, c in enumerate(cs):
                                            m = n - 1 + c
                                            tile = seg * (nq + 1) + m
                                            kr = 64 if m == nq else 128
                                            sblk = (0, 1, 3, 4)[head]
                                            lhs = Vb[0:kr, tile, sblk:sblk + 2, :].rearrange("p a b -> p (a b)")
                                            ins = e.matmul(OB_[u][:, hh, 0:128], lhsT=lhs,
                                                           rhs=PB[u][0:kr, hh, c * 128:(c + 1) * 128],
                                                           start=(ci == 0), stop=(ci == len(cs) - 1))
                                    return ins
                                fw.op("pe", fn, [t_Vb] + t_PB[u], [t_OB[u]])

                            def accum():
                                st0 = seg + dil * 128 * n
                                av = acc[:, 2 * hp:2 * hp + 2, sl(st0, 128, dil)]
                                if g == 0:
                                    act(av, OB_[u][:, :, 0:128], AF.Copy, [t_OB[u]], [t_acc[hp]])
                                else:
                                    dve_tt(av, av, OB_[u][:, :, 0:128], ALU.add, [t_OB[u], t_acc[hp]], [t_acc[hp]])
                            return qk, mid, pv, accum

                        units = []
                        for seg in range(dil):
                            for n in range(nq):
                                for hp in range(2):
                                    units.append(make_unit(g, dil, nq, segw, seg, n, hp, ucnt % 2))
                                    ucnt += 1
                        units[0][0]()
                        for ui in range(len(units)):
                            units[ui][1]()
                            if ui + 1 < len(units):
                                units[ui + 1][0]()
                            units[ui][2]()
                            if ui >= 1:
                                units[ui - 1][3]()
                        units[len(units) - 1][3]()
                    fw.barrier()

                with ExitStack() as p5:
                    wG = sb(p5, "wG", [128, 8, 2 * D], BF16)
                    wPA = sb(p5, "wPA", [128, 4, D], BF16)
                    wPB = sb(p5, "wPB", [128, 2, D], BF16)
                    t_wG = [T("wG0"), T("wG1")]
                    t_wPA, t_wPB = T("wPA"), T("wPB")
                    w_gate_v = w_gate.rearrange("(k p) n -> p k n", p=128)
                    for i in range(2):
                        fw.dma("pool", lambda e, i=i: e.dma_start(out=wG[:, :, i * D:(i + 1) * D], in_=w_gate_v[:, :, i * D:(i + 1) * D]), t_wG[i])
                    fw.dma("pool", lambda e: e.dma_start(out=wPA[:], in_=w_pa.rearrange("(k p) n -> p k n", p=128)), t_wPA)
                    fw.dma("pool", lambda e: e.dma_start(out=wPB[:], in_=w_pb.rearrange("(k p) n -> p k n", p=128)), t_wPB)
                    gsb = sb(p5, "gsb", [128, 16, 512], BF16)
                    t_gsb = [T("gsb%d" % i) for i in range(16)]
                    m1 = [sb(p5, "m1_%d" % i, [128, 512], F32) for i in range(2)]
                    t_m1 = [T("m1_%d" % i) for i in range(2)]
                    m2 = [sb(p5, "m2_%d" % i, [128, 512], F32) for i in range(2)]
                    t_m2 = [T("m2_%d" % i) for i in range(2)]
                    gps = [ps(p5, "gps%d" % i, [128, 512]) for i in range(2)]
                    t_gps = [T("gps%d" % i) for i in range(2)]
                    pap = [ps(p5, "pap%d" % i, [128, 512]) for i in range(2)]
                    t_pap = [T("pap%d" % i) for i in range(2)]
                    pbp = [ps(p5, "pbp%d" % i, [128, 512]) for i in range(2)]
                    t_pbp = [T("pbp%d" % i) for i in range(2)]
                    wad2 = sb(p5, "wad2", [128, 8, D], BF16)
                    t_wad2 = T("wad2")
                    ps_mod2 = ps(p5, "ps_mod2", [128, 512])[:, 0:32]
                    t_psm2 = T("psm2")
                    wadv2 = w_ada.rearrange("(k p) n -> p k n", p=128)

                    def late_load(g):
                        fw.dma("pool", lambda e: e.dma_start(out=wad2[:], in_=wadv2[:, :, g * D:(g + 1) * D]), t_wad2)

                    def late_mm(g):
                        groups = []
                        for j in range(8):
                            col = (g - 2) * 8 + j
                            groups.append((ps_mod2[:, col:col + 1],
                                           [(wad2[:, k, j * 128:(j + 1) * 128], cb[:, k:k + 1]) for k in range(8)]))
                        mm_multi(groups, [t_cb, t_wad2], [t_psm2])
                    for tb in range(4):
                        for h in range(4):
                            cs_ = slice(tb * 512, (tb + 1) * 512)
                            orow = slice(64 * (h % 2), 64 * (h % 2) + 64)
                            zrow = slice(64 - 64 * (h % 2), 128 - 64 * (h % 2))
                            dve_recip(rzb[orow, :], acc[zrow, h, cs_], [t_acc[h // 2]], [t_rzb])
                            dve_tt(ybT[orow, h // 2, cs_], acc[orow, h, cs_], rzb[orow, :], ALU.mult,
                                   [t_acc[h // 2], t_rzb], [t_yb[h][tb]])
                    if debug in ("yb", "yb0"):
                        fw.barrier()
                        o1 = dbg_out("dbg_yb", [128, 2, OWN])
                        td = T("d1")
                        fw.dma("pool", lambda e: e.dma_start(out=o1, in_=ybT[:]), td, t_yb[0][0])
                        return finish([td])
                    late_load(2)
                    for tb in range(4):
                        cs_ = slice(tb * 512, (tb + 1) * 512)
                        for gt in range(16):
                            b = gt % 2
                            mm(gps[b][:], [(wG[:, k, gt * 128:(gt + 1) * 128], uT_own[:, k, cs_]) for k in range(8)],
                               [t_wG[gt // 8], t_uown[tb]], t_gps[b])
                            act(gsb[:, gt, :], gps[b][:], AF.Sigmoid, [t_gps[b], t_vec], [t_gsb[gt]],
                                bias=vcol(V_BG + gt), scale=1.0)
                        for ot in range(8):
                            b = ot % 2
                            osl = slice(ot * 128, (ot + 1) * 128)
                            mm(pap[b][:], [(wPA[:, kc, osl], yaT[:, kc, cs_]) for kc in range(4)],
                               [t_wPA] + [t_ya[h][tb] for h in range(8)], t_pap[b])
                            mm(pbp[b][:], [(wPB[:, kc, osl], ybT[:, kc, cs_]) for kc in range(2)],
                               [t_wPB] + [t_yb[h][tb] for h in range(4)], t_pbp[b])
                            dve_tt(m1[b][:], pap[b][:], gsb[:, ot, :], ALU.mult, [t_pap[b], t_gsb[ot]], [t_m1[b]])
                            dve_tt(m2[b][:], pbp[b][:], gsb[:, 8 + ot, :], ALU.mult, [t_pbp[b], t_gsb[8 + ot]], [t_m2[b]])
                            fw.op("pool", lambda e, ot=ot, b=b, cs_=cs_: e.tensor_tensor(out=uT_own[:, ot, cs_], in0=m1[b][:], in1=m2[b][:], op=ALU.add),
                                  [t_m1[b], t_m2[b]], [t_uown[tb]])
                        late_mm(2 + tb)
                        if tb < 3:
                            late_load(3 + tb)
                    dve_tt(mod[:, 16:48], ps_mod2[:], vcol(V_BADA + 16, 32), ALU.add, [t_psm2, t_vec], [t_modhi])
                    dve_stt(A12[:, 8:16], mod[:, 32:40], 1.0, vcol(V_N2G, 8), ALU.add, ALU.mult, [t_modhi, t_vec], [t_A2])
                    fw.barrier()
                if debug == "mix":
                    o1 = dbg_out("dbg_mix", [128, 8, OWN])
                    td = T("d1")
                    fw.dma("pool", lambda e: e.dma_start(out=o1, in_=uT_own[:]), td, t_uown[0])
                    return finish([td])

        with ExitStack() as ctxB:
            xres = sb(ctxB, "xres", [128, 8, OWN], F32)
            t_x = [[T("x%d_%d" % (tb, k)) for k in range(8)] for tb in range(4)]
            GS = 6
            wI = [sb(ctxB, "wI%d" % i, [128, 8, 2, GS * 128], BF16) for i in range(2)]
            wOu = [sb(ctxB, "wOu%d" % i, [128, GS, D], BF16) for i in range(2)]
            t_wI = [[T("wIg%d" % i), T("wIu%d" % i)] for i in range(2)]
            t_wOu = [T("wOu%d" % i) for i in range(2)]
            w_fi_v = w_fi.rearrange("(k p) n -> p k n", p=128)
            w_fo_v = w_fo.rearrange("(j p) n -> p j n", p=128)

            def load_group(gi):
                j0, gs = FFN_GROUPS[gi]
                wb = gi % 2
                fw.dma("pool", lambda e: e.dma_start(out=wI[wb][:, :, 0, 0:gs * 128], in_=w_fi_v[:, :, j0 * 128:(j0 + gs) * 128]), t_wI[wb][0])
                fw.dma("pool", lambda e: e.dma_start(out=wI[wb][:, :, 1, 0:gs * 128],
                                                      in_=w_fi_v[:, :, DFF + j0 * 128:DFF + (j0 + gs) * 128]), t_wI[wb][1])
                fw.dma("pool", lambda e: e.dma_start(out=wOu[wb][:, 0:gs, :], in_=w_fo_v[:, j0:j0 + gs, :]), t_wOu[wb])

            with ExitStack() as p5b:
                wO = sb(p5b, "wO", [128, 8, D], BF16)
                t_wO = T("wO")
                fw.dma("pool", lambda e: e.dma_start(out=wO[:], in_=w_o.rearrange("(k p) n -> p k n", p=128)), t_wO)
                load_group(0)
                load_group(1)
                ops_ = [ps(p5b, "ops%d" % i, [128, 512]) for i in range(2)]
                t_ops = [T("ops%d" % i) for i in range(2)]
                nb5 = norm_bufs(p5b, "e")
                for tb in range(4):
                    cs_ = slice(tb * 512, (tb + 1) * 512)
                    fw.dma("sp", lambda e, tb=tb: e.dma_start(out=xres[:, :, tb * 512:(tb + 1) * 512],
                                                              in_=xTv[:, :, tb * 512:(tb + 1) * 512]), t_x[tb][0])
                    for k in range(1, 8):
                        t_x[tb][k].lw = t_x[tb][0].lw
                    for ot in range(8):
                        b = ot % 2
                        osl = slice(ot * 128, (ot + 1) * 128)
                        mm(ops_[b][:], [(wO[:, k, osl], uT_own[:, k, cs_]) for k in range(8)], [t_wO, t_uown[tb]], t_ops[b])
                        dve_stt(xres[:, ot, cs_], ops_[b][:], G1(ot), xres[:, ot, cs_], ALU.mult, ALU.add,
                                [t_ops[b], t_modhi, t_x[tb][ot]], [t_x[tb][ot]])
                    norm_block(nb5, xres[:, :, cs_], t_x[tb], lambda k: A12[:, 8 + k:9 + k], SH2, uT_own[:, :, cs_],
                               lambda k, t=t_uown[tb]: t, [t_A2, t_modhi])
                fw.barrier()
            if debug == "x1":
                o1 = dbg_out("dbg_x1", [128, 8, OWN])
                o2 = dbg_out("dbg_u2", [128, 8, OWN])
                td = [T("d1"), T("d2")]
                fw.dma("pool", lambda e: e.dma_start(out=o1, in_=xres[:]), td[0], t_x[0][0])
                fw.dma("pool", lambda e: e.dma_start(out=o2, in_=uT_own[:]), td[1], t_uown[0])
                return finish(td)

            with ExitStack() as p6:
                hbuf = [sb(p6, "hbuf%d" % i, [128, GS, 512], BF16) for i in range(2)]
                t_h = [[T("h%d_%d" % (i, j)) for j in range(GS)] for i in range(2)]
                sg = [sb(p6, "sg%d" % i, [128, 512], F32) for i in range(2)]
                t_sg = [T("sg%d" % i) for i in range(2)]
                hgp = [ps(p6, "hgp%d" % i, [128, 512]) for i in range(2)]
                t_hgp = [T("hgp%d" % i) for i in range(2)]
                hup = [ps(p6, "hup%d" % i, [128, 512]) for i in range(2)]
                t_hup = [T("hup%d" % i) for i in range(2)]
                yps = [ps(p6, "yps%d" % i, [128, 512]) for i in range(2)]
                t_yps = [T("yps%d" % i) for i in range(2)]
                nb6 = norm_bufs(p6, "f")
                t_out = [T("out%d" % i) for i in range(4)]
                cnt = 0
                for gi, (j0, gs) in enumerate(FFN_GROUPS):
                    wb = gi % 2
                    for tb in range(4):
                        cs_ = slice(tb * 512, (tb + 1) * 512)
                        hb = cnt % 2
                        cnt += 1
                        for jj in range(gs):
                            b = jj % 2
                            mm(hgp[b][:], [(wI[wb][:, k, 0, jj * 128:(jj + 1) * 128], uT_own[:, k, cs_]) for k in range(8)],
                               [t_wI[wb][0], t_uown[tb]], t_hgp[b])
                            mm(hup[b][:], [(wI[wb][:, k, 1, jj * 128:(jj + 1) * 128], uT_own[:, k, cs_]) for k in range(8)],
                               [t_wI[wb][1], t_uown[tb]], t_hup[b])
                            act(sg[b][:], hgp[b][:], AF.Silu, [t_hgp[b]], [t_sg[b]])
                            dve_tt(hbuf[hb][:, jj, :], sg[b][:], hup[b][:], ALU.mult, [t_sg[b], t_hup[b]], [t_h[hb][jj]])
                        for ot in range(8):
                            b = ot % 2
                            osl = slice(ot * 128, (ot + 1) * 128)
                            mm(yps[b][:], [(wOu[wb][:, jj, osl], hbuf[hb][:, jj, :]) for jj in range(gs)],
                               [t_wOu[wb]] + t_h[hb][0:gs], t_yps[b])
                            dve_stt(xres[:, ot, cs_], yps[b][:], G2(ot), xres[:, ot, cs_], ALU.mult, ALU.add,
                                    [t_yps[b], t_modhi, t_x[tb][ot]], [t_x[tb][ot]])
                        if gi == len(FFN_GROUPS) - 1:
                            norm_block(nb6, xres[:, :, cs_], t_x[tb], lambda k: vcol(V_FNG + k), None, xres[:, :, cs_],
                                       lambda k, tb=tb: t_x[tb][k], [])
                            for k in range(8):
                                pass
                            tsrc = T("osrc%d" % tb)
                            tsrc.lw = None
                            fw.wait_all("sp", t_x[tb])
                            fw.dma("sp", lambda e, tb=tb: e.dma_start(out=outTv[:, :, tb * 512:(tb + 1) * 512],
                                                                      in_=xres[:, :, tb * 512:(tb + 1) * 512]), t_out[tb], t_x[tb][0])
                    if gi + 2 < len(FFN_GROUPS):
                        load_group(gi + 2)
                fw.wait_all("sp", t_out)
                fw.barrier()
        fw.emit()
    return nc, dbg


def _rope_tab(pos, dim, theta):
    inv = (np.float32(theta) ** (-np.arange(0, dim, 2, dtype=np.float32) / np.float32(dim))).astype(np.float32)
    ang = pos.astype(np.float32)[:, None] * inv[None, :]
    return np.cos(ang).astype(np.float32), np.sin(ang).astype(np.float32)


def _const_mats():
    cmat = np.zeros((128, NCM), np.float32)
    cmat[:, C_ONES:C_ONES + 128] = 1.0
    for hb in (0, 64):
        cmat[hb:hb + 64, C_BO + hb:C_BO + hb + 64] = 1.0
    RA = np.zeros((128, 128), np.float32)
    RB = np.zeros((128, 128), np.float32)
    for hb in (0, 64):
        for half in (0, 32):
            for m in range(16):
                RA[hb + half + m + 16, hb + half + m] = -1.0
                RA[hb + half + m, hb + half + m + 16] = 1.0
        for m in range(8):
            RB[hb + m + 8, hb + m] = -1.0
            RB[hb + m, hb + m + 8] = 1.0
    cmat[:, C_RA:C_RA + 128] = RA
    cmat[:, C_RB:C_RB + 128] = RB
    b = np.arange(128)[:, None]
    a = np.arange(128)[None, :]
    cmat[:, C_MASK + 0:C_MASK + 128] = (b >= a + 64)
    cmat[:, C_MASK + 128:C_MASK + 256] = (np.abs(a - b) <= 64)
    cmat[:, C_MASK + 256:C_MASK + 384] = (a >= b + 64)
    return cmat


def _tables(perm):
    t = perm.astype(np.int64)
    cr, sr = _rope_tab(t // 64, 32, 10000.0)
    cc, sc = _rope_tab(t % 64, 32, 10000.0)
    CA = np.concatenate([cr, cr, cc, cc], axis=1).T
    SA = np.concatenate([sr, sr, sc, sc], axis=1).T
    tabA = np.stack([np.concatenate([CA, CA], 0), np.concatenate([SA, SA], 0)], 0).astype(np.float32)
    tb = t[:3072]
    cp, sp_ = _rope_tab(tb, 16, 500000.0)
    CB = np.ones((64, 3072), np.float32)
    SB = np.zeros((64, 3072), np.float32)
    CB[0:8] = cp.T
    CB[8:16] = cp.T
    SB[0:8] = sp_.T
    SB[8:16] = sp_.T
    tabB = np.stack([np.concatenate([CB, CB], 0), np.concatenate([SB, SB], 0)], 0).astype(np.float32)
    return np.ascontiguousarray(tabA), np.ascontiguousarray(tabB)


def _col(v, n):
    return np.asarray(v, np.float32).reshape(n, 128).T


def make_in_maps(inputs):
    f = lambda k: np.asarray(inputs[k], np.float32)
    x, c = f("x"), f("c")
    cmat = _const_mats()
    shared = {
        "cmat": cmat,
        "w_ada": np.ascontiguousarray(f("w_ada")[0]),
        "w_qkv": np.ascontiguousarray(f("w_qkv")[0]),
        "w_pa": np.ascontiguousarray(f("w_proj_a")[0]),
        "w_pb": np.ascontiguousarray(f("w_proj_b")[0]),
        "w_gate": np.ascontiguousarray(f("w_gate")[0]),
        "w_o": np.ascontiguousarray(f("w_o")[0]),
        "w_fi": np.ascontiguousarray(f("w_ffn_in")[0]),
        "w_fo": np.ascontiguousarray(f("w_ffn_out")[0]),
    }
    vecs = np.zeros((128, NV), np.float32)
    vecs[:, V_BADA:V_BADA + 48] = _col(f("b_ada")[0], 48)
    vecs[:, V_N1G:V_N1G + 8] = _col(f("norm1_g")[0], 8)
    vecs[:, V_N2G:V_N2G + 8] = _col(f("norm2_g")[0], 8)
    vecs[:, V_FNG:V_FNG + 8] = _col(f("final_norm_g"), 8)
    vecs[:, V_BG:V_BG + 16] = _col(f("b_gate")[0], 16)
    vecs[:, V_GQ] = np.tile(f("q_norm_a")[0], 2)
    vecs[:, V_GK] = np.tile(f("k_norm_a")[0], 2)
    perms = [np.arange(S), S - 1 - np.arange(S)]
    tabs = [_tables(p) for p in perms]
    in_maps = []
    for core in range(8):
        b, h = core // 2, core % 2
        m = dict(shared)
        m["xT"] = np.ascontiguousarray(x[b][perms[h]].T)
        m["cvec"] = np.ascontiguousarray(_col(c[b], 8))
        m["vecs"] = vecs
        m["tabA"], m["tabB"] = tabs[h]
        in_maps.append(m)
    return in_maps, perms


def kernel(**inputs):
    in_maps, perms = make_in_maps(inputs)
    nc, _ = build()
    res = run_bass_kernel_spmd(nc, in_maps, core_ids=list(range(8)))
    out = np.zeros((4, S, D), np.float32)
    for core in range(8):
        b, h = core // 2, core % 2
        oT = np.asarray(res.results[core]["outT"], np.float32)
        out[b, perms[h][:OWN], :] = oT.T
    return out
```
